# Optimizing a Trainium2 kernel written in Bass

```python
import math
import jax, jax.numpy as jnp
from jax import lax
import numpy as np

D_MODEL = 1024
BATCH = 4
SEQ = 8192
DEPTH = 4

HEAD_DIM = 64
MIX_WIDTH = D_MODEL
SWA_HEADS = 4
SWA_KV_HEADS = 2
SWA_WINDOW = 128
SWA_BLOCK = 128
NSA_HEADS = 4
NSA_CMP_BLOCK = 32
NSA_CMP_STRIDE = 16
NSA_CMP_HIDDEN = 128
NSA_SEL_BLOCK = 64
NSA_TOP_N = 16
NSA_WINDOW = 512
NSA_Q_BLOCK = 128
NSA_KV_STREAMS = 6
NSA_BRANCHES = 3
SWA_WIDTH = SWA_HEADS * HEAD_DIM
SWA_KV_WIDTH = SWA_KV_HEADS * HEAD_DIM
NSA_WIDTH = NSA_HEADS * HEAD_DIM
SSM_WIDTH = MIX_WIDTH - SWA_WIDTH - NSA_WIDTH
SSM_GROUP_CH = 16
SSM_GROUPS = SSM_WIDTH // SSM_GROUP_CH
SSM_STATE = 64
DT_MIN = 1e-3
DT_MAX = 1e-1
NUM_BUCKETS = 32
BUCKET_EXACT = NUM_BUCKETS // 2
BUCKET_MAX_DIST = 1024
N_BIAS_HEADS = SWA_HEADS + NSA_HEADS
D_FF = 4 * D_MODEL
EPS = 1e-6
NEG_INF = -1e30
SEL_FORCE = 1e4
OFF_QA = 0
OFF_KA = OFF_QA + SWA_WIDTH
OFF_VA = OFF_KA + SWA_KV_WIDTH
OFF_U = OFF_VA + SWA_KV_WIDTH
OFF_QC = OFF_U + SSM_WIDTH
OFF_KVC = OFF_QC + NSA_WIDTH
OFF_GC = OFF_KVC + NSA_KV_STREAMS * HEAD_DIM
IN_WIDTH = OFF_GC + NSA_BRANCHES * NSA_HEADS

kernel_name = "hybrid_swa_s5_nsa_trunk"


def rms_norm(x, g):
    xf = x.astype(jnp.float32)
    y = xf * lax.rsqrt(jnp.mean(xf * xf, axis=-1, keepdims=True) + EPS)
    return (y * g.astype(jnp.float32)).astype(x.dtype)


def rel_bucket(dist):
    n = jnp.maximum(dist, 0)
    nf = jnp.maximum(n, 1).astype(jnp.float32)
    large = BUCKET_EXACT + (jnp.log(nf / BUCKET_EXACT) / math.log(BUCKET_MAX_DIST / BUCKET_EXACT)
                            * (NUM_BUCKETS - BUCKET_EXACT)).astype(jnp.int32)
    return jnp.where(n < BUCKET_EXACT, n, jnp.minimum(large, NUM_BUCKETS - 1))


def masked_softmax(logits, valid):
    p = jax.nn.softmax(jnp.where(valid, logits, NEG_INF), axis=-1)
    return jnp.where(valid, p, 0.0)


def swa_sink_mixer(q, k, v, sinks, bias_tbl):
    Bsz, L, H, d = q.shape
    KV = k.shape[2]
    G = H // KV
    nb = L // SWA_BLOCK
    pad = ((0, 0), (SWA_BLOCK, 0), (0, 0), (0, 0))
    kp = jnp.pad(k, pad).reshape(Bsz, nb + 1, SWA_BLOCK, KV, d)
    vp = jnp.pad(v, pad).reshape(Bsz, nb + 1, SWA_BLOCK, KV, d)
    kb = jnp.concatenate([kp[:, :-1], kp[:, 1:]], axis=2)
    vb = jnp.concatenate([vp[:, :-1], vp[:, 1:]], axis=2)
    qb = q.reshape(Bsz, nb, SWA_BLOCK, KV, G, d)
    lg = jnp.einsum('bnqkgd,bnskd->bnkgqs', qb, kb, preferred_element_type=jnp.float32) * (d ** -0.5)
    i = jnp.arange(SWA_BLOCK)[:, None]
    j = jnp.arange(2 * SWA_BLOCK)[None, :]
    dist = i - j + SWA_BLOCK
    bias = bias_tbl.astype(jnp.float32)[rel_bucket(dist)]
    bias = bias.transpose(2, 0, 1).reshape(KV, G, SWA_BLOCK, 2 * SWA_BLOCK)
    s_pos = jnp.arange(nb)[:, None] * SWA_BLOCK - SWA_BLOCK + j
    valid = ((dist >= 0) & (dist < SWA_WINDOW))[None] & (s_pos >= 0)[:, None, :]
    valid = valid[None, :, None, None]
    lg = jnp.where(valid, lg + bias, NEG_INF)
    sink = sinks.astype(jnp.float32).reshape(KV, G, 1, 1)
    m = jnp.maximum(jnp.max(lg, axis=-1, keepdims=True), sink)
    p = jnp.exp(lg - m)
    denom = jnp.sum(p, axis=-1, keepdims=True) + jnp.exp(sink - m)
    o = jnp.einsum('bnkgqs,bnskd->bnqkgd', p / denom, vb.astype(jnp.float32))
    return o.reshape(Bsz, L, H * d).astype(q.dtype)


def _scan_combine(left, right):
    a_l, b_l = left
    a_r, b_r = right
    return a_r * a_l, a_r * b_l + b_r


def s5_mixer(u, a_re, a_im, log_dt, b_re, b_im, c_re, c_im, d_skip, glu_w, glu_b):
    Bsz, L, W = u.shape
    f32 = jnp.float32
    uf = u.astype(f32).reshape(Bsz, L, SSM_GROUPS, SSM_GROUP_CH)
    A = lax.complex(a_re.astype(f32), a_im.astype(f32))
    dt = jnp.exp(log_dt.astype(f32))[:, None]
    A_bar = jnp.exp(A * dt)
    Bm = lax.complex(b_re.astype(f32), b_im.astype(f32))
    B_bar = ((A_bar - 1.0) / A)[..., None] * Bm
    Bu = jnp.einsum('blgp,gnp->blgn', uf.astype(jnp.complex64), B_bar)
    a_seq = jnp.broadcast_to(A_bar, (1, L) + A_bar.shape)
    _, states = lax.associative_scan(_scan_combine, (a_seq, Bu), axis=1)
    Cm = lax.complex(c_re.astype(f32), c_im.astype(f32))
    y = jnp.einsum('blgn,gpn->blgp', states, Cm).real
    y = y + d_skip.astype(f32).reshape(SSM_GROUPS, SSM_GROUP_CH) * uf
    z = jax.nn.gelu(y.reshape(Bsz, L, W))
    ab = z @ glu_w.astype(f32) + glu_b.astype(f32)
    out = ab[..., :W] * jax.nn.sigmoid(ab[..., W:])
    return out.astype(u.dtype)


def nsa_compress(tok, pos, w1, w2):
    Bsz, L, d = tok.shape
    ch = tok.reshape(Bsz, L // NSA_CMP_STRIDE, NSA_CMP_STRIDE, d)
    blocks = jnp.concatenate([ch[:, :-1], ch[:, 1:]], axis=2)
    M = blocks.shape[1]
    blocks = (blocks + pos).reshape(Bsz, M, NSA_CMP_BLOCK * d)
    return jax.nn.gelu(blocks @ w1) @ w2


def nsa_mixer(q, kv, gates, cmp_pos, cmp_w1, cmp_w2, g_kc, g_ks, g_kw, bias_tbl):
    Bsz, L, HC, d = q.shape
    f32 = jnp.float32
    k_cmp = rms_norm(nsa_compress(kv[:, :, 0], cmp_pos[0], cmp_w1[0], cmp_w2[0]), g_kc)
    v_cmp = nsa_compress(kv[:, :, 1], cmp_pos[1], cmp_w1[1], cmp_w2[1]).astype(f32)
    NS = L // NSA_SEL_BLOCK
    n_top = min(NSA_TOP_N, NS)
    k_sel = rms_norm(kv[:, :, 2], g_ks).reshape(Bsz, NS, NSA_SEL_BLOCK, d)
    v_sel = kv[:, :, 3].reshape(Bsz, NS, NSA_SEL_BLOCK, d)
    wpad = ((0, 0), (NSA_WINDOW, 0), (0, 0))
    k_win = jnp.pad(rms_norm(kv[:, :, 4], g_kw), wpad)
    v_win = jnp.pad(kv[:, :, 5], wpad)
    M = k_cmp.shape[1]
    cmp_start = jnp.arange(M) * NSA_CMP_STRIDE
    cmp_end = cmp_start + NSA_CMP_BLOCK - 1
    sel_start = jnp.arange(NS) * NSA_SEL_BLOCK
    overlap = ((cmp_start[:, None] < sel_start[None, :] + NSA_SEL_BLOCK)
               & (cmp_start[:, None] + NSA_CMP_BLOCK > sel_start[None, :])).astype(f32)
    bias_c = bias_tbl.astype(f32)
    bidx = jnp.arange(Bsz)[:, None, None]
    scale = d ** -0.5
    QB = NSA_Q_BLOCK

    def block(bi):
        t = bi * QB + jnp.arange(QB)
        qb = lax.dynamic_slice_in_dim(q, bi * QB, QB, axis=1)
        gb = lax.dynamic_slice_in_dim(gates, bi * QB, QB, axis=1).astype(f32)
        dist_c = t[:, None] - cmp_end[None, :]
        lg_c = jnp.einsum('bqhd,bmd->bhqm', qb, k_cmp, preferred_element_type=f32) * scale
        lg_c = lg_c + bias_c[rel_bucket(dist_c)].transpose(2, 0, 1)
        p_cmp = masked_softmax(lg_c, dist_c >= 0)
        o_cmp = jnp.einsum('bhqm,bmd->bqhd', p_cmp, v_cmp)
        imp = jnp.einsum('bhqm,mj->bqj', p_cmp, overlap)
        cur = t // NSA_SEL_BLOCK
        blk = jnp.arange(NS)[None, :]
        forced = (blk == 0) | (blk == cur[:, None]) | (blk == cur[:, None] - 1)
        future = sel_start[None, :] > t[:, None]
        score = jnp.where(forced, SEL_FORCE, jnp.where(future, -SEL_FORCE, imp))
        _, idx = lax.top_k(score, n_top)
        ks = k_sel[bidx, idx]
        vs = v_sel[bidx, idx].astype(f32)
        s_pos = idx[..., None] * NSA_SEL_BLOCK + jnp.arange(NSA_SEL_BLOCK)
        dist_s = t[None, :, None, None] - s_pos
        lg_s = jnp.einsum('bqhd,bqnsd->bqhns', qb, ks, preferred_element_type=f32) * scale
        lg_s = lg_s + jnp.moveaxis(bias_c[rel_bucket(dist_s)], -1, 2)
        shp = lg_s.shape
        valid_s = (dist_s >= 0).reshape(Bsz, QB, 1, -1)
        p_s = masked_softmax(lg_s.reshape(shp[:3] + (-1,)), valid_s).reshape(shp)
        o_sel = jnp.einsum('bqhns,bqnsd->bqhd', p_s, vs)
        kw = lax.dynamic_slice_in_dim(k_win, bi * QB, QB + NSA_WINDOW, axis=1)
        vw = lax.dynamic_slice_in_dim(v_win, bi * QB, QB + NSA_WINDOW, axis=1).astype(f32)
        w_pos = bi * QB - NSA_WINDOW + jnp.arange(QB + NSA_WINDOW)
        dist_w = t[:, None] - w_pos[None, :]
        valid_w = (dist_w >= 0) & (dist_w < NSA_WINDOW) & (w_pos[None, :] >= 0)
        lg_w = jnp.einsum('bqhd,bsd->bhqs', qb, kw, preferred_element_type=f32) * scale
        lg_w = lg_w + bias_c[rel_bucket(dist_w)].transpose(2, 0, 1)
        p_w = masked_softmax(lg_w, valid_w)
        o_win = jnp.einsum('bhqs,bsd->bqhd', p_w, vw)
        o = gb[..., 0:1] * o_cmp + gb[..., 1:2] * o_sel + gb[..., 2:3] * o_win
        return o.reshape(Bsz, QB, HC * d).astype(q.dtype)

    out = lax.map(block, jnp.arange(L // QB))
    return out.transpose(1, 0, 2, 3).reshape(Bsz, L, HC * d)


def setup_inputs(seed: int = 0) -> dict:
    key = jax.random.key(seed)
    ks = jax.random.split(key, 32)
    f32 = jnp.float32

    def nrm(k, shape, scale):
        return jax.random.normal(k, shape, f32) * scale

    G, N, P = SSM_GROUPS, SSM_STATE, SSM_GROUP_CH
    return {
        'x': nrm(ks[0], (BATCH, SEQ, D_MODEL), 1.0),
        'norm1_g': 1.0 + nrm(ks[1], (DEPTH, D_MODEL), 0.02),
        'w_in': nrm(ks[2], (DEPTH, D_MODEL, IN_WIDTH), D_MODEL ** -0.5),
        'qk_g': 1.0 + nrm(ks[3], (DEPTH, 6, HEAD_DIM), 0.02),
        'sinks': nrm(ks[4], (DEPTH, SWA_HEADS), 0.5),
        'rel_bias': nrm(ks[5], (NUM_BUCKETS, N_BIAS_HEADS), 0.2),
        'ssm_a_re': -0.5 * jnp.exp(nrm(ks[6], (DEPTH, G, N), 0.05)),
        'ssm_a_im': math.pi * jnp.arange(N, dtype=f32)[None, None, :] + nrm(ks[7], (DEPTH, G, N), 0.01),
        'ssm_log_dt': jax.random.uniform(ks[8], (DEPTH, G), f32, math.log(DT_MIN), math.log(DT_MAX)),
        'ssm_b_re': nrm(ks[9], (DEPTH, G, N, P), (2 * P) ** -0.5),
        'ssm_b_im': nrm(ks[10], (DEPTH, G, N, P), (2 * P) ** -0.5),
        'ssm_c_re': nrm(ks[11], (DEPTH, G, P, N), N ** -0.5),
        'ssm_c_im': nrm(ks[12], (DEPTH, G, P, N), N ** -0.5),
        'ssm_d': nrm(ks[13], (DEPTH, SSM_WIDTH), 0.5),
        'glu_w': nrm(ks[14], (DEPTH, SSM_WIDTH, 2 * SSM_WIDTH), SSM_WIDTH ** -0.5),
        'glu_b': nrm(ks[15], (DEPTH, 2 * SSM_WIDTH), 0.02),
        'cmp_pos': nrm(ks[16], (DEPTH, 2, NSA_CMP_BLOCK, HEAD_DIM), 0.1),
        'cmp_w1': nrm(ks[17], (DEPTH, 2, NSA_CMP_BLOCK * HEAD_DIM, NSA_CMP_HIDDEN), (NSA_CMP_BLOCK * HEAD_DIM) ** -0.5),
        'cmp_w2': nrm(ks[18], (DEPTH, 2, NSA_CMP_HIDDEN, HEAD_DIM), NSA_CMP_HIDDEN ** -0.5),
        'out_norm_g': 1.0 + nrm(ks[19], (DEPTH, MIX_WIDTH), 0.02),
        'w_out': nrm(ks[20], (DEPTH, MIX_WIDTH, D_MODEL), 0.5 * MIX_WIDTH ** -0.5),
        'norm2_g': 1.0 + nrm(ks[21], (DEPTH, D_MODEL), 0.02),
        'w_up': nrm(ks[22], (DEPTH, D_MODEL, D_FF), D_MODEL ** -0.5),
        'w_down': nrm(ks[23], (DEPTH, D_FF, D_MODEL), 0.5 * D_FF ** -0.5),
    }


def reference(x, norm1_g, w_in, qk_g, sinks, rel_bias, ssm_a_re, ssm_a_im, ssm_log_dt,
              ssm_b_re, ssm_b_im, ssm_c_re, ssm_c_im, ssm_d, glu_w, glu_b,
              cmp_pos, cmp_w1, cmp_w2, out_norm_g, w_out, norm2_g, w_up, w_down):
    Bsz, L, _ = x.shape
    for l in range(DEPTH):
        h = rms_norm(x, norm1_g[l])
        proj = jnp.einsum('bld,de->ble', h, w_in[l])
        qa = rms_norm(proj[..., OFF_QA:OFF_KA].reshape(Bsz, L, SWA_HEADS, HEAD_DIM), qk_g[l, 0])
        ka = rms_norm(proj[..., OFF_KA:OFF_VA].reshape(Bsz, L, SWA_KV_HEADS, HEAD_DIM), qk_g[l, 1])
        va = proj[..., OFF_VA:OFF_U].reshape(Bsz, L, SWA_KV_HEADS, HEAD_DIM)
        o_a = swa_sink_mixer(qa, ka, va, sinks[l], rel_bias[:, :SWA_HEADS])
        o_b = s5_mixer(proj[..., OFF_U:OFF_QC], ssm_a_re[l], ssm_a_im[l], ssm_log_dt[l],
                       ssm_b_re[l], ssm_b_im[l], ssm_c_re[l], ssm_c_im[l], ssm_d[l], glu_w[l], glu_b[l])
        qc = rms_norm(proj[..., OFF_QC:OFF_KVC].reshape(Bsz, L, NSA_HEADS, HEAD_DIM), qk_g[l, 2])
        kvc = proj[..., OFF_KVC:OFF_GC].reshape(Bsz, L, NSA_KV_STREAMS, HEAD_DIM)
        gc = jax.nn.sigmoid(proj[..., OFF_GC:IN_WIDTH].reshape(Bsz, L, NSA_HEADS, NSA_BRANCHES))
        o_c = nsa_mixer(qc, kvc, gc, cmp_pos[l], cmp_w1[l], cmp_w2[l],
                        qk_g[l, 3], qk_g[l, 4], qk_g[l, 5], rel_bias[:, SWA_HEADS:])
        g_out = out_norm_g[l]
        mixed = jnp.concatenate([
            rms_norm(o_a, g_out[:SWA_WIDTH]),
            rms_norm(o_b, g_out[SWA_WIDTH:SWA_WIDTH + SSM_WIDTH]),
            rms_norm(o_c, g_out[SWA_WIDTH + SSM_WIDTH:]),
        ], axis=-1)
        x = x + jnp.einsum('ble,ed->bld', mixed, w_out[l])
        h2 = rms_norm(x, norm2_g[l])
        hid = jax.nn.relu(jnp.einsum('bld,df->blf', h2, w_up[l]))
        x = x + jnp.einsum('blf,fd->bld', hid * hid, w_down[l])
    return x
```

```python
import contextlib
import math
import numpy as np
import concourse.bass as bass
import concourse.mybir as mybir
from concourse.bass_utils import run_bass_kernel_spmd

F32 = mybir.dt.float32
BF16 = mybir.dt.bfloat16
I32 = mybir.dt.int32
AF = mybir.ActivationFunctionType
ALU = mybir.AluOpType

SAFE_SAME_ENGINE = True
PRUNE = False
FORCE_EXT = ("kr",)

D = 1024
HD = 64
DEPTH = 4
T1 = 32
NG = 32
NEG = -30000.0
EPS = 1e-6
IN_W = 1676
TWO_PI = 2.0 * math.pi


class Buf:
    __slots__ = ("name", "w", "r", "dsem", "dcount")

    def __init__(self, name=""):
        self.name = name
        self.w = {}
        self.r = {}
        self.dsem = None
        self.dcount = 0


class EngW:
    def __init__(self, ctx, name, eng):
        self.ctx = ctx
        self.name = name
        self.eng = eng
        self.sem = ctx.es.enter_context(ctx.nc.semaphore(name + "_prog"))
        self.count = 0
        self.waited = {}

    def wait_tokens(self, toks):
        for key, (sem, val, ename) in toks.items():
            if ename == self.name:
                if self.name == "tensor" or not SAFE_SAME_ENGINE:
                    continue
            if self.waited.get(key, 0) >= val:
                continue
            self.eng.wait_ge(sem, val)
            self.waited[key] = val
            snap = self.ctx.snaps.get((key, val)) if PRUNE else None
            if snap is not None:
                w = self.waited
                for k2, v2 in snap.items():
                    if w.get(k2, 0) < v2:
                        w[k2] = v2


class Ctx:
    def __init__(self, nc):
        self.nc = nc
        self.es = contextlib.ExitStack()
        self.E = {}
        for name in ["tensor", "vector", "scalar", "gpsimd", "sync"]:
            self.E[name] = EngW(self, name, getattr(nc, name))
        self.nbuf = 0
        self.ninst = 0
        self.dma_bufs = []
        self.snaps = {}
        self.sem_pool = []
        self.all_slots = []
        self.rr = 0

    def buf(self, name=""):
        self.nbuf += 1
        return Buf((name or "b") + f"_{self.nbuf}")

    def _deps(self, reads, writes):
        toks = {}
        for b in list(reads) + list(writes):
            for k, v in b.w.items():
                if k not in toks or toks[k][1] < v[1]:
                    toks[k] = v
        for b in writes:
            for k, v in b.r.items():
                if k not in toks or toks[k][1] < v[1]:
                    toks[k] = v
        return toks

    def _commit(self, reads, writes, key, tok):
        for b in reads:
            if key not in b.r or b.r[key][1] < tok[1]:
                b.r[key] = tok
        for b in writes:
            b.w = {key: tok}
            b.r = {}

    def op(self, engname, method, *args, reads=(), writes=(), **kw):
        e = self.E[engname]
        e.wait_tokens(self._deps(reads, writes))
        inst = getattr(e.eng, method)(*args, **kw)
        e.count += 1
        inst.then_inc(e.sem, 1)
        self.ninst += 1
        tok = (e.sem, e.count, e.name)
        self.snaps[("E" + e.name, e.count)] = dict(e.waited)
        self._commit(reads, writes, "E" + e.name, tok)
        return tok

    def dma(self, out, in_, reads=(), writes=(), sbuf=None, queue="sync", **kw):
        e = self.E[queue]
        e.wait_tokens(self._deps(reads, writes))
        if sbuf.dsem is None:
            if self.sem_pool:
                sbuf.dsem = self.sem_pool.pop()
            else:
                h = self.es.enter_context(self.nc.semaphore(f"dsem{len(self.all_slots)}"))
                sbuf.dsem = [len(self.all_slots), h, 0]
                self.all_slots.append(sbuf.dsem)
            self.dma_bufs.append(sbuf)
        slot = sbuf.dsem
        inst = e.eng.dma_start(out=out, in_=in_, **kw)
        slot[2] += 16
        inst.then_inc(slot[1], 16)
        self.ninst += 1
        tok = (slot[1], slot[2], None)
        self.snaps[(f"S{slot[0]}", slot[2])] = dict(e.waited)
        self._commit(reads, writes, f"S{slot[0]}", tok)
        return tok

    def barrier(self):
        toks = {}
        for n, e in self.E.items():
            if e.count > 0:
                toks["E" + n] = (e.sem, e.count, "__none__")
        for slot in self.all_slots:
            if slot[2] > 0:
                toks[f"S{slot[0]}"] = (slot[1], slot[2], None)
        for n, e in self.E.items():
            t2 = {k: v for k, v in toks.items() if k != "E" + n}
            e.wait_tokens(t2)
        for b in self.dma_bufs:
            self.sem_pool.append(b.dsem)
            b.dsem = None
        self.dma_bufs = []

    def pick(self, *names):
        self.rr += 1
        return names[self.rr % len(names)]


def rel_bucket_np(dist):
    n = np.maximum(dist, 0)
    nf = np.maximum(n, 1).astype(np.float32)
    large = 16 + (np.log(nf / np.float32(16)) / np.float32(math.log(1024 / 16)) * np.float32(16)).astype(np.int32)
    return np.where(n < 16, n, np.minimum(large, 31)).astype(np.int64)


def host_tables(rel_bias, L):
    rb = np.asarray(rel_bias, dtype=np.float32)
    j = np.arange(128)[:, None]
    i = np.arange(128)[None, :]
    out = {}

    def gather(dist, valid, heads):
        bk = rel_bucket_np(dist)
        t = np.empty((128, len(heads), 128), np.float32)
        for hi, h in enumerate(heads):
            t[:, hi, :] = np.where(valid, rb[bk, h], np.float32(NEG))
        return t
    ba = np.empty((128, 2, 4, 128), np.float32)
    for dl in range(2):
        dist = i - j + 128 * dl
        valid = (dist >= 0) & (dist < 128)
        ba[:, dl] = gather(dist, valid, [0, 1, 2, 3])
    out["biasA"] = ba
    bn = np.empty((128, 8, 4, 128), np.float32)
    for dl in range(8):
        dist = i - j + 128 * dl
        bn[:, dl] = gather(dist, dist >= 0, [4, 5, 6, 7])
    out["biasN"] = bn
    dist = i - j + 128 * 4
    out["biasW4"] = gather(dist, (dist >= 0) & (dist < 512), [4, 5, 6, 7])
    bc = np.empty((128, 24, 4, 128), np.float32)
    for dl in range(24):
        dist = 128 * dl - 31 + i - 16 * j
        bc[:, dl] = gather(dist, dist >= 0, [4, 5, 6, 7])
    out["biasC"] = bc
    out["b31"] = np.broadcast_to(rb[31, 4:8][None, :], (128, 4)).copy()
    return out


def host_consts(L):
    NS = L // 64
    c = {}
    c["ident"] = np.eye(128, dtype=np.float32)
    bd = np.zeros((128, 128), np.float32)
    bd[:64, :64] = 1.0
    bd[64:, 64:] = 1.0
    c["bd64"] = bd
    w = np.zeros((128, L), np.float32)
    t = np.arange(L)
    w[t // 64, t] = 1.0
    c["wexp"] = w
    M = L // 16 - 1
    nch = (M + 127) // 128
    ov = np.zeros((128, 4, 128), np.float32)
    for cch in range(nch):
        for ml in range(128):
            m = cch * 128 + ml
            if m >= M:
                continue
            cs = m * 16
            for jb in range(NS):
                ss = jb * 64
                if cs < ss + 64 and cs + 32 > ss:
                    ov[ml, cch, jb] = 1.0
    c["overlap"] = ov
    m1 = np.ones((128, 256), np.float32)
    m2 = np.zeros((128, 256), np.float32)
    for i in range(128):
        cur_rel = 1 if i >= 64 else 0
        for x in range(256):
            rel = x - 126
            if rel == cur_rel or rel == cur_rel - 1:
                m1[i, x] = 0.0
                m2[i, x] = 1e4
            elif rel > cur_rel:
                m1[i, x] = 0.0
                m2[i, x] = -1e4
    c["m1base"] = m1
    c["m2base"] = m2
    kk = np.broadcast_to((T1 - np.arange(T1 + 1, dtype=np.float32))[None, :], (128, T1 + 1)).copy()
    c["kk"] = kk
    return c


def build(L, nlayers=DEPTH, dbg=False, phases=None, dbg_in=(), layer0=0):
    NT = L // 512
    NB = L // 128
    NS = L // 64
    MC = L // 16 - 1
    NCH = (MC + 127) // 128
    NTOP = min(16, NS)
    NC = L // T1
    SEG = min(L, 4096)
    NSEG = L // SEG
    NCS = SEG // T1
    nc = bass.Bass("TRN2", target_bir_lowering=False)
    C = Ctx(nc)
    skind = "ExternalOutput" if dbg else "Internal"

    def din(name, shape, dt=F32):
        return nc.dram_tensor(name, list(shape), dt, kind="ExternalInput").ap()

    def dscr(name, shape, dt):
        kd = "ExternalInput" if name in dbg_in else ("ExternalOutput" if name in FORCE_EXT else skind)
        return nc.dram_tensor(name, list(shape), dt, kind=kd).ap()

    x_in = din("x", [L, D])
    norm1_g = din("norm1_g", [DEPTH, D])
    w_in = din("w_in", [DEPTH, D, IN_W])
    qk_g = din("qk_g", [DEPTH, 6, HD])
    sinks = din("sinks", [DEPTH, 4])
    a_re_d = din("ssm_a_re", [DEPTH, NG, 64])
    a_im_d = din("ssm_a_im", [DEPTH, NG, 64])
    log_dt_d = din("ssm_log_dt", [DEPTH, NG])
    b_re_d = din("ssm_b_re", [DEPTH, NG, 64, 16])
    b_im_d = din("ssm_b_im", [DEPTH, NG, 64, 16])
    c_re_d = din("ssm_c_re", [DEPTH, NG, 16, 64])
    c_im_d = din("ssm_c_im", [DEPTH, NG, 16, 64])
    ssm_d_d = din("ssm_d", [DEPTH, 512])
    glu_w_d = din("glu_w", [DEPTH, 512, 1024])
    glu_b_d = din("glu_b", [DEPTH, 1024])
    cmp_pos_d = din("cmp_pos", [DEPTH, 2, 32, 64])
    cmp_w1_d = din("cmp_w1", [DEPTH, 2, 2048, 128])
    cmp_w2_d = din("cmp_w2", [DEPTH, 2, 128, 64])
    out_norm_g = din("out_norm_g", [DEPTH, D])
    w_out_d = din("w_out", [DEPTH, D, D])
    norm2_g = din("norm2_g", [DEPTH, D])
    w_up_d = din("w_up", [DEPTH, D, 4 * D])
    w_down_d = din("w_down", [DEPTH, 4 * D, D])
    c_ident = din("c_ident", [128, 128])
    c_bd64 = din("c_bd64", [128, 128])
    c_wexp = din("c_wexp", [128, L])
    c_overlap = din("c_overlap", [128, 4, 128])
    c_m1 = din("c_m1base", [128, 256])
    c_m2 = din("c_m2base", [128, 256])
    c_kk = din("c_kk", [128, T1 + 1])
    t_biasA = din("t_biasA", [128, 2, 4, 128])
    t_biasN = din("t_biasN", [128, 8, 4, 128])
    t_biasW4 = din("t_biasW4", [128, 4, 128])
    t_biasC = din("t_biasC", [128, 24, 4, 128])
    t_b31 = din("t_b31", [128, 4])

    y_out = nc.dram_tensor("y", [L, D], F32, kind="ExternalOutput").ap()

    xT = dscr("xT", [D, L], F32)
    fm = dscr("fm", [8, 128, L], BF16)
    utm = dscr("utm", [L, 512], BF16)
    vtm = dscr("vtm", [L, 260], BF16)
    gtm = dscr("gtm", [L, 12], F32)
    mixT = dscr("mixT", [D, L], BF16)
    kr = dscr("kr", [NG, 16, 63 * 16], BF16)
    xT_b = [C.buf(f"xT{i}") for i in range(NT)]
    fm_b = [C.buf(f"fm{i}") for i in range(NT)]
    utm_b = [C.buf(f"utm{i}") for i in range(NT)]
    vtm_b = [C.buf(f"vtm{i}") for i in range(NT)]
    gtm_b = [C.buf(f"gtm{i}") for i in range(NT)]
    mixa_b = [C.buf(f"mixa{i}") for i in range(NT)]
    mixb_b = [C.buf(f"mixb{i}") for i in range(NT)]
    kr_b = [C.buf(f"kr{g}") for g in range(NG)]
    y_b = C.buf("y")
    inb = C.buf("inputs")

    def want(p):
        return phases is None or p in phases

    with C.es:
        es = C.es
        PS = [es.enter_context(nc.psum_tensor(f"ps{i}", [128, 512], F32)) for i in range(8)]
        PSB = [C.buf(f"ps{i}") for i in range(8)]

        uniq = [0]

        def sb(stack, name, shape, dt):
            uniq[0] += 1
            return stack.enter_context(nc.sbuf_tensor(f"{name}_{uniq[0]}", list(shape), dt))

        ident = sb(es, "ident", [128, 128], F32)
        identb = sb(es, "identb", [128, 128], BF16)
        onesb = sb(es, "onesb", [128, 128], BF16)
        bd64b = sb(es, "bd64b", [128, 128], BF16)
        gstage = sb(es, "gstage", [128, 128], F32)
        b_ident, b_identb, b_onesb, b_bd64b, b_gstage = (C.buf(n) for n in ["ident", "identb", "onesb", "bd64b", "gstage"])
        C.dma(ident[:], c_ident[:, :], reads=[inb], writes=[b_ident], sbuf=b_ident)
        C.op("vector", "tensor_copy", out=identb[:], in_=ident[:], reads=[b_ident], writes=[b_identb])
        C.op("vector", "memset", onesb[:], 1.0, writes=[b_onesb])
        C.dma(gstage[:], c_bd64[:, :], reads=[inb], writes=[b_gstage], sbuf=b_gstage)
        C.op("vector", "tensor_copy", out=bd64b[:], in_=gstage[:], reads=[b_gstage], writes=[b_bd64b])

        def rstd_from(stack_unused, out_ap, in_ap, scale, tmp_ap, eng_reads, bout, btmp):
            C.op("scalar", "activation", out=tmp_ap, in_=in_ap, func=AF.Ln, bias=EPS, scale=scale,
                 reads=eng_reads, writes=[btmp])
            C.op("scalar", "activation", out=out_ap, in_=tmp_ap, func=AF.Exp, scale=-0.5,
                 reads=[btmp], writes=[bout])

        def load_cols(stack, name, dram_vec, n):
            t = sb(stack, name, [128, n], F32)
            b = C.buf(name)
            C.dma(t[:], dram_vec.rearrange("(k p) -> p k", p=128), reads=[inb], writes=[b], sbuf=b,
                  allow_slow_non_contiguous=True)
            return t, b

        if want("p0"):
            with contextlib.ExitStack() as ph:
                xin = [sb(ph, f"p0_xin{i}", [128, D], F32) for i in range(2)]
                xts = [sb(ph, f"p0_xts{i}", [128, 8, 128], F32) for i in range(2)]
                b_xin = [C.buf(f"p0xin{i}") for i in range(2)]
                b_xts = [C.buf(f"p0xts{i}") for i in range(2)]
                for tb in range(NB):
                    s = tb % 2
                    C.dma(xin[s][:], x_in[tb * 128:(tb + 1) * 128, :], reads=[inb], writes=[b_xin[s]], sbuf=b_xin[s])
                    for half in range(2):
                        pb = half
                        for k in range(4):
                            C.op("tensor", "transpose", PS[pb][:, k * 128:(k + 1) * 128],
                                 xin[s][:, (half * 4 + k) * 128:(half * 4 + k + 1) * 128], ident[:, :],
                                 reads=[b_xin[s], b_ident], writes=[PSB[pb]])
                        C.op("vector", "tensor_copy", out=xts[s][:, half * 4:(half + 1) * 4, :],
                             in_=PS[pb][:, :].rearrange("p (k t) -> p k t", k=4),
                             reads=[PSB[pb]], writes=[b_xts[s]])
                    C.dma(xT.rearrange("(k p) t -> p k t", p=128)[:, :, tb * 128:(tb + 1) * 128], xts[s][:],
                          reads=[b_xts[s]], writes=[xT_b[tb // 4]], sbuf=b_xts[s])
                C.barrier()

        zk = sb(es, "zk", [16, 1008], BF16)
        b_zk = C.buf("zk")
        C.op("vector", "memset", zk[:], 0.0, writes=[b_zk])
        for g in range(NG):
            C.dma(kr[g], zk[:], reads=[b_zk], writes=[kr_b[g]], sbuf=b_zk)

        for l in range(layer0, layer0 + nlayers):
            last = (l == layer0 + nlayers - 1)
            if want("p1"):
                with contextlib.ExitStack() as ph:
                    stage = sb(ph, "p1_stage", [128, 8, IN_W], F32)
                    b_stage = C.buf("p1stage")
                    wb = sb(ph, "p1_wb", [128, 8, 1804], BF16)
                    b_wb = C.buf("p1wb")
                    C.dma(stage[:], w_in[l].rearrange("(k p) c -> p k c", p=128), reads=[inb], writes=[b_stage], sbuf=b_stage)
                    g1, b_g1 = load_cols(ph, "p1_g1", norm1_g[l], 8)
                    groups = [(0, 0, 64), (64, 128, 64), (128, 64, 64), (192, 192, 64), (256, 256, 128),
                              (384, 1024, 128), (512, 1152, 128), (640, 1280, 128),
                              (768, 1408, 64), (832, 1408, 64), (896, 1536, 64), (960, 1536, 64),
                              (1024, 512, 512),
                              (1536, 384, 128), (1664, 1472, 64), (1728, 1600, 64), (1792, 1664, 12)]
                    for (dd, ss, n) in groups:
                        C.op(C.pick("vector", "gpsimd"), "tensor_tensor", out=wb[:, :, dd:dd + n], in0=stage[:, :, ss:ss + n],
                             in1=g1[:, :].unsqueeze(2).broadcast_to([128, 8, n]), op=ALU.mult,
                             reads=[b_stage, b_g1], writes=[b_wb])
                    gq = sb(ph, "p1_gq", [128, 6], F32)
                    b_gq = C.buf("p1gq")
                    for half in range(2):
                        C.dma(gq[half * 64:(half + 1) * 64, :], qk_g[l].rearrange("s d -> d s"), reads=[inb], writes=[b_gq],
                              sbuf=b_gq, allow_slow_non_contiguous=True)
                    gcol = sb(ph, "p1_gcol", [128, 8], F32)
                    b_gcol = C.buf("p1gcol")
                    C.op("vector", "memset", gcol[:], 1.0, writes=[b_gcol])
                    for (cc, sidx, scl) in [(0, 0, 0.125), (1, 0, 0.125), (2, 1, 1.0), (3, 2, 0.125), (4, 2, 0.125), (6, 4, 1.0), (7, 5, 1.0)]:
                        C.op("vector", "tensor_scalar", out=gcol[:, cc:cc + 1], in0=gq[:, sidx:sidx + 1], scalar1=scl, scalar2=None,
                             op0=ALU.mult, reads=[b_gq], writes=[b_gcol])
                    xt = [sb(ph, f"p1_xt{i}", [128, 8, 512], F32) for i in range(2)]
                    b_xt = [C.buf(f"p1xt{i}") for i in range(2)]
                    xb = [sb(ph, f"p1_xb{i}", [128, 8, 512], BF16) for i in range(2)]
                    b_xb = [C.buf(f"p1xb{i}") for i in range(2)]
                    sq = [sb(ph, f"p1_sq{i}", [128, 8, 512], BF16) for i in range(2)]
                    b_sq = [C.buf(f"p1sq{i}") for i in range(2)]
                    lnv = sb(ph, "p1_lnv", [128, 512], F32)
                    b_lnv = C.buf("p1lnv")
                    rstdb = [sb(ph, f"p1_rstdb{i}", [128, 512], F32) for i in range(2)]
                    b_rstdb = [C.buf(f"p1rstdb{i}") for i in range(2)]
                    rtm = [sb(ph, f"p1_rtm{i}", [128, 4], F32) for i in range(2)]
                    b_rtm = [C.buf(f"p1rtm{i}") for i in range(2)]
                    rtmp = sb(ph, "p1_rtmp", [128, 4], F32)
                    b_rtmp = C.buf("p1rtmp")
                    ysb = [sb(ph, f"p1_ysb{i}", [128, 512], F32) for i in range(2)]
                    b_ysb = [C.buf(f"p1ysb{i}") for i in range(2)]
                    sq2 = [sb(ph, f"p1_sq2{i}", [128, 512], BF16) for i in range(2)]
                    b_sq2 = [C.buf(f"p1sq2{i}") for i in range(2)]
                    lnv2 = [sb(ph, f"p1_lnv2{i}", [128, 512], F32) for i in range(2)]
                    b_lnv2 = [C.buf(f"p1lnv2{i}") for i in range(2)]
                    r2 = [sb(ph, f"p1_r2{i}", [128, 512], F32) for i in range(2)]
                    b_r2 = [C.buf(f"p1r2{i}") for i in range(2)]
                    fst = [sb(ph, f"p1_fst{i}", [128, 8, 512], BF16) for i in range(2)]
                    b_fst = [C.buf(f"p1fst{i}") for i in range(2)]
                    ust = [sb(ph, f"p1_ust{i}", [128, 4, 512], BF16) for i in range(2)]
                    b_ust = [C.buf(f"p1ust{i}") for i in range(2)]
                    vst = [sb(ph, f"p1_vst{i}", [128, 4, 4, 65], BF16) for i in range(2)]
                    b_vst = [C.buf(f"p1vst{i}") for i in range(2)]
                    gst = [sb(ph, f"p1_gst{i}", [128, 4, 12], F32) for i in range(2)]
                    b_gst = [C.buf(f"p1gst{i}") for i in range(2)]
                    for i in range(2):
                        C.op("vector", "memset", vst[i][:], 1.0, writes=[b_vst[i]])

                    def p1_load(i):
                        s = i % 2
                        C.dma(xt[s][:], xT.rearrange("(k p) t -> p k t", p=128)[:, :, i * 512:(i + 1) * 512],
                              reads=[xT_b[i]], writes=[b_xt[s]], sbuf=b_xt[s])
                    p1_load(0)
                    pcount = 0
                    for i in range(NT):
                        s = i % 2
                        if i + 1 < NT:
                            p1_load(i + 1)
                        C.op("scalar", "activation", out=sq[s][:], in_=xt[s][:], func=AF.Square, reads=[b_xt[s]], writes=[b_sq[s]])
                        C.op("gpsimd", "tensor_copy", out=xb[s][:], in_=xt[s][:], reads=[b_xt[s]], writes=[b_xb[s]])
                        for k in range(8):
                            C.op("tensor", "matmul", PS[0][:, :], lhsT=onesb[:, :], rhs=sq[s][:, k, :], start=(k == 0), stop=(k == 7),
                                 reads=[b_onesb, b_sq[s]], writes=[PSB[0]])
                        rstd_from(None, rstdb[s][:], PS[0][:, :], 1.0 / D, lnv[:], [PSB[0]], b_rstdb[s], b_lnv)
                        for st in range(4):
                            for k in range(8):
                                C.op("tensor", "matmul", PS[1][:, st:st + 1], lhsT=sq[s][:, k, st * 128:(st + 1) * 128], rhs=onesb[:, 0:1],
                                     start=(k == 0), stop=(k == 7), reads=[b_onesb, b_sq[s]], writes=[PSB[1]])
                        rstd_from(None, rtm[s][:], PS[1][:, 0:4], 1.0 / D, rtmp[:], [PSB[1]], b_rtm[s], b_rtmp)
                        for cidx in range(8):
                            pb = 2 + (pcount % 2)
                            pcount += 1
                            for k in range(8):
                                C.op("tensor", "matmul", PS[pb][:, :], lhsT=wb[:, k, cidx * 128:(cidx + 1) * 128], rhs=xb[s][:, k, :],
                                     start=(k == 0), stop=(k == 7), reads=[b_wb, b_xb[s]], writes=[PSB[pb]])
                            if cidx == 5:
                                C.op("vector", "tensor_tensor", out=fst[s][:, cidx, :], in0=PS[pb][:, :], in1=rstdb[s][:], op=ALU.mult,
                                     reads=[PSB[pb], b_rstdb[s]], writes=[b_fst[s]])
                                continue
                            q = cidx % 2
                            C.op("vector", "tensor_tensor", out=ysb[q][:], in0=PS[pb][:, :], in1=rstdb[s][:], op=ALU.mult,
                                 reads=[PSB[pb], b_rstdb[s]], writes=[b_ysb[q]])
                            C.op("scalar", "activation", out=sq2[q][:], in_=ysb[q][:], func=AF.Square, reads=[b_ysb[q]], writes=[b_sq2[q]])
                            pb2 = 4 + q
                            C.op("tensor", "matmul", PS[pb2][:, :], lhsT=bd64b[:, :], rhs=sq2[q][:], start=True, stop=True,
                                 reads=[b_bd64b, b_sq2[q]], writes=[PSB[pb2]])
                            rstd_from(None, r2[q][:], PS[pb2][:, :], 1.0 / HD, lnv2[q][:], [PSB[pb2]], b_r2[q], b_lnv2[q])
                            C.op("vector", "scalar_tensor_tensor", out=fst[s][:, cidx, :], in0=ysb[q][:], scalar=gcol[:, cidx:cidx + 1],
                                 in1=r2[q][:], op0=ALU.mult, op1=ALU.mult, reads=[b_ysb[q], b_gcol, b_r2[q]], writes=[b_fst[s]])
                        C.dma(fm.rearrange("c p t -> p c t")[:, :, i * 512:(i + 1) * 512], fst[s][:], reads=[b_fst[s]], writes=[fm_b[i]],
                              sbuf=b_fst[s])
                        for st in range(4):
                            for k in range(8):
                                C.op("tensor", "matmul", PS[6][:, :], lhsT=xb[s][:, k, st * 128:(st + 1) * 128], rhs=wb[:, k, 1024:1536],
                                     start=(k == 0), stop=(k == 7), reads=[b_wb, b_xb[s]], writes=[PSB[6]])
                            for k in range(8):
                                C.op("tensor", "matmul", PS[7][:, 0:268], lhsT=xb[s][:, k, st * 128:(st + 1) * 128], rhs=wb[:, k, 1536:1804],
                                     start=(k == 0), stop=(k == 7), reads=[b_wb, b_xb[s]], writes=[PSB[7]])
                            C.op("vector", "tensor_scalar", out=ust[s][:, st, :], in0=PS[6][:, :], scalar1=rtm[s][:, st:st + 1], scalar2=None,
                                 op0=ALU.mult, reads=[PSB[6], b_rtm[s]], writes=[b_ust[s]])
                            C.op("vector", "tensor_scalar", out=vst[s][:, st, :, 0:64], in0=PS[7][:, 0:256].rearrange("p (a b) -> p a b", a=4),
                                 scalar1=rtm[s][:, st:st + 1], scalar2=None, op0=ALU.mult, reads=[PSB[7], b_rtm[s]], writes=[b_vst[s]])
                            C.op("scalar", "activation", out=gst[s][:, st, :], in_=PS[7][:, 256:268], func=AF.Sigmoid, scale=rtm[s][:, st:st + 1],
                                 reads=[PSB[7], b_rtm[s]], writes=[b_gst[s]])
                        C.dma(utm.rearrange("(n s p) c -> n p s c", s=4, p=128)[i], ust[s][:], reads=[b_ust[s]], writes=[utm_b[i]], sbuf=b_ust[s])
                        C.dma(vtm.rearrange("(n s p) c -> n p s c", s=4, p=128)[i], vst[s][:].rearrange("p s a b -> p s (a b)"),
                              reads=[b_vst[s]], writes=[vtm_b[i]], sbuf=b_vst[s])
                        C.dma(gtm.rearrange("(n s p) c -> n p s c", s=4, p=128)[i], gst[s][:], reads=[b_gst[s]], writes=[gtm_b[i]], sbuf=b_gst[s])
                    C.barrier()

            if want("p2"):
                with contextlib.ExitStack() as ph:
                    def T(name, shape, dt=F32, stack=None):
                        return sb(stack or ph, "p2_" + name, shape, dt), C.buf("p2" + name)
                    PI = math.pi
                    K33 = T1 + 1
                    AAr, b_AAr = T("AAr", [128, NG, K33])
                    AAi, b_AAi = T("AAi", [128, NG, K33])
                    BBb1, b_BBb1 = T("BBb1", [128, NG, 16])
                    BBb2, b_BBb2 = T("BBb2", [128, NG, 16])
                    BBb1b, b_BBb1b = T("BBb1b", [128, NG, 16], BF16)
                    CC1, b_CC1 = T("CC1", [128, NG, 16])
                    CC2, b_CC2 = T("CC2", [128, NG, 16])
                    PP1, b_PP1 = T("PP1", [128, 64])
                    PP2, b_PP2 = T("PP2", [128, 64])
                    dsk, b_dsk = T("dsk", [16, NG])
                    cur, b_cur = T("cur", [128, 96])
                    gw, b_gw = T("gw", [128, 4, 1024], BF16)
                    gbc, b_gbc = load_cols(ph, "p2_gbc", glu_b_d[l], 8)
                    with contextlib.ExitStack() as pp:
                        ain, b_ain = T("ain", [32, 2, 128], stack=pp)
                        for q_, src in enumerate([a_re_d, a_im_d]):
                            for hh in range(2):
                                C.dma(ain[:, q_, hh * 64:(hh + 1) * 64], src[l], reads=[inb], writes=[b_ain], sbuf=b_ain)
                        for q_ in range(2):
                            C.op("tensor", "transpose", PS[0][:, q_ * 32:(q_ + 1) * 32], ain[:, q_, :], ident[0:32, 0:32], reads=[b_ain, b_ident], writes=[PSB[0]])
                        aT, b_aT = T("aT", [128, 2, NG], stack=pp)
                        C.op("vector", "tensor_copy", out=aT[:].rearrange("p a g -> p (a g)"), in_=PS[0][:, 0:64], reads=[PSB[0]], writes=[b_aT])
                        dtb, b_dtb = T("dtb", [128, NG], stack=pp)
                        C.dma(dtb[:], log_dt_d[l].partition_broadcast(128), reads=[inb], writes=[b_dtb], sbuf=b_dtb)
                        C.op("scalar", "activation", out=dtb[:], in_=dtb[:], func=AF.Exp, reads=[b_dtb], writes=[b_dtb])
                        rho, b_rho = T("rho", [128, NG], stack=pp)
                        th, b_th = T("th", [128, NG], stack=pp)
                        C.op("vector", "tensor_tensor", out=rho[:], in0=aT[:, 0, :], in1=dtb[:], op=ALU.mult, reads=[b_aT, b_dtb], writes=[b_rho])
                        C.op("vector", "tensor_tensor", out=th[:], in0=aT[:, 1, :], in1=dtb[:], op=ALU.mult, reads=[b_aT, b_dtb], writes=[b_th])
                        kkt, b_kkt = T("kk", [128, K33], stack=pp)
                        C.dma(kkt[:], c_kk[:, :], reads=[inb], writes=[b_kkt], sbuf=b_kkt)
                        NE = NG * K33
                        ang, b_ang = T("ang", [128, NG, K33], stack=pp)
                        mag, b_mag = T("mag", [128, NG, K33], stack=pp)
                        w1_, b_w1_ = T("w1_", [128, NE], stack=pp)
                        w2_, b_w2_ = T("w2_", [128, NE], stack=pp)
                        wi_, b_wi_ = T("wi_", [128, NE], I32, stack=pp)
                        sn, b_sn = T("sn", [128, NG, K33], stack=pp)
                        cs, b_cs = T("cs", [128, NG, K33], stack=pp)
                        kb = kkt[:, :].unsqueeze(1).broadcast_to([128, NG, K33])
                        C.op("vector", "tensor_tensor", out=ang[:], in0=th[:, :].unsqueeze(2).broadcast_to([128, NG, K33]), in1=kb, op=ALU.mult,
                             reads=[b_th, b_kkt], writes=[b_ang])
                        C.op("vector", "tensor_tensor", out=mag[:], in0=rho[:, :].unsqueeze(2).broadcast_to([128, NG, K33]), in1=kb, op=ALU.mult,
                             reads=[b_rho, b_kkt], writes=[b_mag])
                        C.op("scalar", "activation", out=mag[:], in_=mag[:], func=AF.Exp, reads=[b_mag], writes=[b_mag])
                        angf = ang[:].rearrange("p g k -> p (g k)")
                        C.op("vector", "tensor_scalar", out=w1_[:], in0=angf, scalar1=1.0 / TWO_PI, scalar2=None, op0=ALU.mult, reads=[b_ang], writes=[b_w1_])
                        C.op("vector", "tensor_copy", out=wi_[:], in_=w1_[:], reads=[b_w1_], writes=[b_wi_])
                        C.op("vector", "tensor_copy", out=w1_[:], in_=wi_[:], reads=[b_wi_], writes=[b_w1_])
                        C.op("vector", "scalar_tensor_tensor", out=w1_[:], in0=w1_[:], scalar=-TWO_PI, in1=angf, op0=ALU.mult, op1=ALU.add,
                             reads=[b_w1_, b_ang], writes=[b_w1_])

                        def fold(t_, bt_):
                            C.op("vector", "tensor_scalar", out=w2_[:], in0=t_, scalar1=PI, scalar2=None, op0=ALU.is_gt, reads=[bt_], writes=[b_w2_])
                            C.op("vector", "scalar_tensor_tensor", out=t_, in0=w2_[:], scalar=-TWO_PI, in1=t_, op0=ALU.mult, op1=ALU.add, reads=[b_w2_, bt_], writes=[bt_])
                            C.op("vector", "tensor_scalar", out=w2_[:], in0=t_, scalar1=-PI, scalar2=None, op0=ALU.is_lt, reads=[bt_], writes=[b_w2_])
                            C.op("vector", "scalar_tensor_tensor", out=t_, in0=w2_[:], scalar=TWO_PI, in1=t_, op0=ALU.mult, op1=ALU.add, reads=[b_w2_, bt_], writes=[bt_])
                        fold(w1_[:], b_w1_)
                        C.op("scalar", "activation", out=sn[:].rearrange("p g k -> p (g k)"), in_=w1_[:], func=AF.Sin, reads=[b_w1_], writes=[b_sn])
                        C.op("vector", "tensor_scalar", out=w1_[:], in0=w1_[:], scalar1=PI / 2, scalar2=None, op0=ALU.add, reads=[b_w1_], writes=[b_w1_])
                        fold(w1_[:], b_w1_)
                        C.op("scalar", "activation", out=cs[:].rearrange("p g k -> p (g k)"), in_=w1_[:], func=AF.Sin, reads=[b_w1_], writes=[b_cs])
                        C.op("vector", "tensor_tensor", out=AAr[:], in0=mag[:], in1=cs[:], op=ALU.mult, reads=[b_mag, b_cs], writes=[b_AAr])
                        C.op("vector", "tensor_tensor", out=AAi[:], in0=mag[:], in1=sn[:], op=ALU.mult, reads=[b_mag, b_sn], writes=[b_AAi])
                        nre, b_nre = T("nre", [128, NG], stack=pp)
                        den_, b_den_ = T("den", [128, NG], stack=pp)
                        tq, b_tq = T("tq", [128, NG], stack=pp)
                        cr, b_cr = T("cr", [128, NG], stack=pp)
                        ci, b_ci = T("ci", [128, NG], stack=pp)
                        C.op("vector", "tensor_scalar", out=nre[:], in0=AAr[:, :, T1 - 1], scalar1=-1.0, scalar2=None, op0=ALU.add, reads=[b_AAr], writes=[b_nre])
                        C.op("vector", "tensor_tensor", out=den_[:], in0=aT[:, 0, :], in1=aT[:, 0, :], op=ALU.mult, reads=[b_aT], writes=[b_den_])
                        C.op("vector", "tensor_tensor", out=tq[:], in0=aT[:, 1, :], in1=aT[:, 1, :], op=ALU.mult, reads=[b_aT], writes=[b_tq])
                        C.op("vector", "tensor_tensor", out=den_[:], in0=den_[:], in1=tq[:], op=ALU.add, reads=[b_den_, b_tq], writes=[b_den_])
                        C.op("vector", "reciprocal", out=den_[:], in_=den_[:], reads=[b_den_], writes=[b_den_])
                        C.op("vector", "tensor_tensor", out=cr[:], in0=nre[:], in1=aT[:, 0, :], op=ALU.mult, reads=[b_nre, b_aT], writes=[b_cr])
                        C.op("vector", "tensor_tensor", out=tq[:], in0=AAi[:, :, T1 - 1], in1=aT[:, 1, :], op=ALU.mult, reads=[b_AAi, b_aT], writes=[b_tq])
                        C.op("vector", "tensor_tensor", out=cr[:], in0=cr[:], in1=tq[:], op=ALU.add, reads=[b_cr, b_tq], writes=[b_cr])
                        C.op("vector", "tensor_tensor", out=cr[:], in0=cr[:], in1=den_[:], op=ALU.mult, reads=[b_cr, b_den_], writes=[b_cr])
                        C.op("vector", "tensor_tensor", out=ci[:], in0=AAi[:, :, T1 - 1], in1=aT[:, 0, :], op=ALU.mult, reads=[b_AAi, b_aT], writes=[b_ci])
                        C.op("vector", "tensor_tensor", out=tq[:], in0=nre[:], in1=aT[:, 1, :], op=ALU.mult, reads=[b_nre, b_aT], writes=[b_tq])
                        C.op("vector", "tensor_tensor", out=ci[:], in0=ci[:], in1=tq[:], op=ALU.subtract, reads=[b_ci, b_tq], writes=[b_ci])
                        C.op("vector", "tensor_tensor", out=ci[:], in0=ci[:], in1=den_[:], op=ALU.mult, reads=[b_ci, b_den_], writes=[b_ci])
                        BB1, b_BB1 = T("BB1", [128, NG, 16], stack=pp)
                        BB2, b_BB2 = T("BB2", [128, NG, 16], stack=pp)
                        C.dma(BB1[0:64], b_re_d[l].rearrange("g n p -> n g p"), reads=[inb], writes=[b_BB1], sbuf=b_BB1)
                        C.dma(BB1[64:128], b_im_d[l].rearrange("g n p -> n g p"), reads=[inb], writes=[b_BB1], sbuf=b_BB1)
                        C.dma(BB2[0:64], b_im_d[l].rearrange("g n p -> n g p"), reads=[inb], writes=[b_BB2], sbuf=b_BB2)
                        C.dma(BB2[64:128], b_re_d[l].rearrange("g n p -> n g p"), reads=[inb], writes=[b_BB2], sbuf=b_BB2)
                        C.op("vector", "tensor_scalar", out=BB2[0:64], in0=BB2[0:64], scalar1=-1.0, scalar2=None, op0=ALU.mult, reads=[b_BB2], writes=[b_BB2])
                        tb1, b_tb1 = T("tb1", [128, NG, 16], stack=pp)
                        crb = cr[:, :].unsqueeze(2).broadcast_to([128, NG, 16])
                        cib = ci[:, :].unsqueeze(2).broadcast_to([128, NG, 16])
                        C.op("vector", "tensor_tensor", out=BBb1[:], in0=BB1[:], in1=crb, op=ALU.mult, reads=[b_BB1, b_cr], writes=[b_BBb1])
                        C.op("vector", "tensor_tensor", out=tb1[:], in0=BB2[:], in1=cib, op=ALU.mult, reads=[b_BB2, b_ci], writes=[b_tb1])
                        C.op("vector", "tensor_tensor", out=BBb1[:], in0=BBb1[:], in1=tb1[:], op=ALU.add, reads=[b_BBb1, b_tb1], writes=[b_BBb1])
                        C.op("vector", "tensor_tensor", out=BBb2[:], in0=BB2[:], in1=crb, op=ALU.mult, reads=[b_BB2, b_cr], writes=[b_BBb2])
                        C.op("vector", "tensor_tensor", out=tb1[:], in0=BB1[:], in1=cib, op=ALU.mult, reads=[b_BB1, b_ci], writes=[b_tb1])
                        C.op("vector", "tensor_tensor", out=BBb2[:], in0=BBb2[:], in1=tb1[:], op=ALU.subtract, reads=[b_BBb2, b_tb1], writes=[b_BBb2])
                        C.op("vector", "tensor_copy", out=BBb1b[:], in_=BBb1[:], reads=[b_BBb1], writes=[b_BBb1b])
                        cin, b_cin = T("cin", [128, 4, 128], stack=pp)
                        for v_, (s0, s1) in enumerate([(c_re_d, c_im_d), (c_im_d, c_re_d)]):
                            C.dma(cin[:, :, 0:64], s0[l].rearrange("g p n -> (g p) n").rearrange("(q r) n -> r q n", r=128), reads=[inb], writes=[b_cin], sbuf=b_cin)
                            C.dma(cin[:, :, 64:128], s1[l].rearrange("g p n -> (g p) n").rearrange("(q r) n -> r q n", r=128), reads=[inb], writes=[b_cin], sbuf=b_cin)
                            for q_ in range(4):
                                C.op("tensor", "transpose", PS[1][:, q_ * 128:(q_ + 1) * 128], cin[:, q_, :], ident[:, :], reads=[b_cin, b_ident], writes=[PSB[1]])
                            if v_ == 0:
                                C.op("vector", "tensor_copy", out=CC1[0:64].rearrange("p g c -> p (g c)"), in_=PS[1][0:64, :], reads=[PSB[1]], writes=[b_CC1])
                                C.op("vector", "tensor_scalar", out=CC1[64:128].rearrange("p g c -> p (g c)"), in0=PS[1][64:128, :], scalar1=-1.0, scalar2=None,
                                     op0=ALU.mult, reads=[PSB[1]], writes=[b_CC1])
                            else:
                                C.op("vector", "tensor_scalar", out=CC2[:].rearrange("p g c -> p (g c)"), in0=PS[1][:, :], scalar1=-1.0, scalar2=None,
                                     op0=ALU.mult, reads=[PSB[1]], writes=[b_CC2])
                        C.op("vector", "tensor_copy", out=PP1[:, 0:32], in_=AAr[:, :, 0], reads=[b_AAr], writes=[b_PP1])
                        C.op("vector", "tensor_copy", out=PP1[:, 32:64], in_=AAr[:, :, 0], reads=[b_AAr], writes=[b_PP1])
                        C.op("vector", "tensor_scalar", out=PP2[0:64, 0:32], in0=AAi[0:64, :, 0], scalar1=-1.0, scalar2=None, op0=ALU.mult, reads=[b_AAi], writes=[b_PP2])
                        C.op("vector", "tensor_copy", out=PP2[64:128, 0:32], in_=AAi[64:128, :, 0], reads=[b_AAi], writes=[b_PP2])
                        C.op("vector", "tensor_scalar", out=PP2[:, 32:64], in0=PP2[:, 0:32], scalar1=-1.0, scalar2=None, op0=ALU.mult, reads=[b_PP2], writes=[b_PP2])
                        C.dma(dsk[:], ssm_d_d[l].rearrange("(g p) -> p g", p=16), reads=[inb], writes=[b_dsk], sbuf=b_dsk, allow_slow_non_contiguous=True)
                        gws, b_gws = T("gws", [128, 1024], stack=pp)
                        for k in range(4):
                            C.dma(gws[:], glu_w_d[l][k * 128:(k + 1) * 128, :], reads=[inb], writes=[b_gws], sbuf=b_gws)
                            C.op("vector", "tensor_copy", out=gw[:, k, :], in_=gws[:], reads=[b_gws], writes=[b_gw])
                        C.barrier()
                    C.op("vector", "memset", cur[:], 0.0, writes=[b_cur])
                    PSb6 = PS[6][:, :].bitcast(BF16)
                    PSb7 = PS[7][:, :].bitcast(BF16)
                    for seg in range(NSEG):
                        with contextlib.ExitStack() as sg:
                            U, b_U = T("U", [128, NG, 4, NCS], BF16, stack=sg)
                            Sb, b_Sb = T("Sb", [128, NG, NCS], BF16, stack=sg)
                            with contextlib.ExitStack() as s1:
                                usb, b_usb = T("usb", [128, T1, 512], BF16, stack=s1)
                                C.dma(usb[0:NCS].rearrange("c j h -> c (j h)"), utm.rearrange("(c j) h -> c (j h)", j=T1)[seg * NCS:(seg + 1) * NCS, :],
                                      reads=utm_b, writes=[b_usb], sbuf=b_usb)
                                ug = [T(f"ug{i}", [128, T1, 16], BF16, stack=s1) for i in range(2)]
                                for g in range(NG):
                                    pv_, pbb = (PSb6, PSB[6]) if g % 2 == 0 else (PSb7, PSB[7])
                                    ugt, b_ugt = ug[g % 2]
                                    C.op("gpsimd" if g % 2 == 0 else "vector", "tensor_copy", out=ugt[0:NCS], in_=usb[0:NCS, :, 16 * g:16 * g + 16],
                                         reads=[b_usb], writes=[b_ugt])
                                    for s_ in range(4):
                                        C.op("tensor", "transpose", pv_[:, s_ * 128:s_ * 128 + NCS], ugt[0:NCS, 8 * s_:8 * s_ + 8, :],
                                             identb[0:NCS, 0:NCS], reads=[b_ugt, b_identb], writes=[pbb])
                                    C.op("vector" if g % 2 == 0 else "scalar", "tensor_copy" if g % 2 == 0 else "copy", out=U[:, g, :, :],
                                         in_=pv_[:, 0:512].rearrange("p (s c) -> p s c", s=4)[:, :, 0:NCS], reads=[pbb], writes=[b_U])
                                C.barrier()
                            with contextlib.ExitStack() as s2:
                                XX, b_XX = T("XX", [128, NCS, 64], stack=s2)
                                MinT = [T(f"MinT{i}", [128, T1, 16], stack=s2) for i in range(2)]
                                Mt2 = [T(f"Mt2{i}", [128, T1, 16], stack=s2) for i in range(2)]
                                MinG = [T(f"MinG{i}", [128, 4, 256], BF16, stack=s2) for i in range(2)]
                                for g in range(NG):
                                    q_ = g % 2
                                    mt, b_mt = MinT[q_]
                                    m2, b_m2 = Mt2[q_]
                                    mg, b_mg = MinG[q_]
                                    C.op("vector", "tensor_tensor", out=mt[:], in0=AAr[:, g, 1:K33].unsqueeze(2).broadcast_to([128, T1, 16]),
                                         in1=BBb1[:, g, :].unsqueeze(1).broadcast_to([128, T1, 16]), op=ALU.mult, reads=[b_AAr, b_BBb1], writes=[b_mt])
                                    C.op("gpsimd", "tensor_tensor", out=m2[:], in0=AAi[:, g, 1:K33].unsqueeze(2).broadcast_to([128, T1, 16]),
                                         in1=BBb2[:, g, :].unsqueeze(1).broadcast_to([128, T1, 16]), op=ALU.mult, reads=[b_AAi, b_BBb2], writes=[b_m2])
                                    C.op("vector", "tensor_tensor", out=mt[:], in0=mt[:], in1=m2[:], op=ALU.add, reads=[b_mt, b_m2], writes=[b_mt])
                                    pb = 2 + q_
                                    for s_ in range(4):
                                        C.op("tensor", "transpose", PS[pb][:, s_ * 128:(s_ + 1) * 128], mt[:, 8 * s_:8 * s_ + 8, :], ident[:, :],
                                             reads=[b_mt, b_ident], writes=[PSB[pb]])
                                    pview = PS[pb][:, :].rearrange("p (s n) -> p s n", s=4)
                                    C.op("vector", "tensor_copy", out=mg[:, :, 0:128], in_=pview, reads=[PSB[pb]], writes=[b_mg])
                                    C.op("scalar", "copy", out=mg[:, :, 128:192], in_=pview[:, :, 64:128], reads=[PSB[pb]], writes=[b_mg])
                                    C.op("scalar", "copy", out=mg[:, :, 192:256], in_=pview[:, :, 0:64], reads=[PSB[pb]], writes=[b_mg])
                                    pb2 = 4 + q_
                                    for hf in range(2):
                                        for s_ in range(4):
                                            C.op("tensor", "matmul", PS[pb2][:, hf * NCS:(hf + 1) * NCS], lhsT=mg[:, s_, hf * 128:(hf + 1) * 128], rhs=U[:, g, s_, :],
                                                 start=(hf == 0 and s_ == 0), stop=(hf == 1 and s_ == 3), reads=[b_mg, b_U], writes=[PSB[pb2]])
                                    C.op("vector", "tensor_copy", out=XX[:, :, g], in_=PS[pb2][:, 0:NCS], reads=[PSB[pb2]], writes=[b_XX])
                                    C.op("scalar", "copy", out=XX[:, :, 32 + g], in_=PS[pb2][:, NCS:2 * NCS], reads=[PSB[pb2]], writes=[b_XX])
                                st1, b_st1 = T("st1", [128, 64], stack=s2)
                                st2, b_st2 = T("st2", [128, 64], stack=s2)
                                for c_ in range(NCS):
                                    C.op("vector", "tensor_copy", out=Sb[:, :, c_], in_=cur[:, 0:32], reads=[b_cur], writes=[b_Sb])
                                    C.op("vector", "tensor_tensor", out=st1[:], in0=PP1[:], in1=cur[:, 0:64], op=ALU.mult, reads=[b_PP1, b_cur], writes=[b_st1])
                                    C.op("vector", "tensor_tensor", out=st2[:], in0=PP2[:], in1=cur[:, 32:96], op=ALU.mult, reads=[b_PP2, b_cur], writes=[b_st2])
                                    C.op("vector", "tensor_tensor", out=st1[:], in0=st1[:], in1=st2[:], op=ALU.add, reads=[b_st1, b_st2], writes=[b_st1])
                                    C.op("vector", "tensor_tensor", out=cur[:, 0:64], in0=XX[:, c_, :], in1=st1[:], op=ALU.add, reads=[b_XX, b_st1], writes=[b_cur])
                                    C.op("vector", "tensor_copy", out=cur[:, 64:96], in_=cur[:, 0:32], reads=[b_cur], writes=[b_cur])
                                C.barrier()
                            with contextlib.ExitStack() as s3:
                                z_tm, b_z_tm = T("z_tm", [128, T1, 512], BF16, stack=s3)
                                zT, b_zT = T("zT", [128, 4, SEG], BF16, stack=s3)
                                Mo1 = [T(f"Mo1{i}", [128, K33, 16], stack=s3) for i in range(2)]
                                Mo2 = [T(f"Mo2{i}", [128, K33, 16], stack=s3) for i in range(2)]
                                MoG = [T(f"MoG{i}", [128, K33 * 16], BF16, stack=s3) for i in range(2)]
                                kt = [T(f"kt{i}", [16, 512], BF16, stack=s3) for i in range(2)]
                                Tg = [T(f"Tg{i}", [128, 4, 512], BF16, stack=s3) for i in range(2)]
                                ysb_ = [T(f"ysb{i}", [128, 512], stack=s3) for i in range(2)]
                                ge1 = [T(f"ge1{i}", [128, 512], stack=s3) for i in range(2)]
                                ge2 = [T(f"ge2{i}", [128, 512], stack=s3) for i in range(2)]
                                for g in range(NG):
                                    q_ = g % 2
                                    mo1, b_mo1 = Mo1[q_]
                                    mo2, b_mo2 = Mo2[q_]
                                    mog, b_mog = MoG[q_]
                                    ktt, b_ktt = kt[q_]
                                    tg, b_tg = Tg[q_]
                                    C.op("vector", "tensor_tensor", out=mo1[:], in0=AAr[:, g, :].unsqueeze(2).broadcast_to([128, K33, 16]),
                                         in1=CC1[:, g, :].unsqueeze(1).broadcast_to([128, K33, 16]), op=ALU.mult, reads=[b_AAr, b_CC1], writes=[b_mo1])
                                    C.op("gpsimd", "tensor_tensor", out=mo2[:], in0=AAi[:, g, :].unsqueeze(2).broadcast_to([128, K33, 16]),
                                         in1=CC2[:, g, :].unsqueeze(1).broadcast_to([128, K33, 16]), op=ALU.mult, reads=[b_AAi, b_CC2], writes=[b_mo2])
                                    C.op("vector", "tensor_tensor", out=mog[:], in0=mo1[:].rearrange("p k c -> p (k c)"), in1=mo2[:].rearrange("p k c -> p (k c)"),
                                         op=ALU.add, reads=[b_mo1, b_mo2], writes=[b_mog])
                                    pb = 2 + q_
                                    C.op("tensor", "matmul", PS[pb][0:16, 0:512], lhsT=BBb1b[:, g, :], rhs=mog[:, 16:K33 * 16], start=True, stop=True,
                                         reads=[b_BBb1b, b_mog], writes=[PSB[pb]])
                                    C.op("vector", "tensor_copy", out=ktt[:, 0:496], in_=PS[pb][0:16, 0:496], reads=[PSB[pb]], writes=[b_ktt])
                                    C.op("vector", "scalar_tensor_tensor", out=ktt[:, 496:512], in0=ident[0:16, 0:16], scalar=dsk[:, g:g + 1], in1=PS[pb][0:16, 496:512],
                                         op0=ALU.mult, op1=ALU.add, reads=[b_ident, b_dsk, PSB[pb]], writes=[b_ktt])
                                    C.dma(kr[g][:, 0:512], ktt[:], reads=[b_ktt], writes=[kr_b[g]], sbuf=b_ktt)
                                    for s_ in range(4):
                                        src = bass.AP(kr.tensor, g * 16 * 1008 + 8 * s_ * 16, [[16, 8], [1008, 16], [1, 512]])
                                        C.dma(tg[:, s_, :], src, reads=[kr_b[g]], writes=[b_tg], sbuf=b_tg)
                                    pb2 = 4 + q_
                                    for s_ in range(4):
                                        C.op("tensor", "matmul", PS[pb2][0:NCS, :], lhsT=U[:, g, s_, :], rhs=tg[:, s_, :], start=(s_ == 0), stop=False,
                                             reads=[b_U, b_tg], writes=[PSB[pb2]])
                                    C.op("tensor", "matmul", PS[pb2][0:NCS, :], lhsT=Sb[:, g, :], rhs=mog[:, 0:512], start=False, stop=True,
                                         reads=[b_Sb, b_mog], writes=[PSB[pb2]])
                                    ys, b_ys = ysb_[q_]
                                    g1_, b_g1_ = ge1[q_]
                                    g2_, b_g2_ = ge2[q_]
                                    C.op("scalar", "copy", out=ys[0:NCS], in_=PS[pb2][0:NCS, :], reads=[PSB[pb2]], writes=[b_ys])
                                    C.op("scalar", "activation", out=g1_[0:NCS], in_=PS[pb2][0:NCS, :], func=AF.Square, reads=[PSB[pb2]], writes=[b_g1_])
                                    C.op("vector", "tensor_scalar", out=g1_[0:NCS], in0=g1_[0:NCS], scalar1=0.044715, scalar2=1.0, op0=ALU.mult, op1=ALU.add,
                                         reads=[b_g1_], writes=[b_g1_])
                                    C.op("gpsimd", "tensor_tensor", out=g1_[0:NCS], in0=g1_[0:NCS], in1=ys[0:NCS], op=ALU.mult, reads=[b_g1_, b_ys], writes=[b_g1_])
                                    C.op("scalar", "activation", out=g2_[0:NCS], in_=g1_[0:NCS], func=AF.Sigmoid, scale=1.5957691216, reads=[b_g1_], writes=[b_g2_])
                                    C.op("vector", "tensor_tensor", out=z_tm[0:NCS, :, 16 * g:16 * g + 16], in0=g2_[0:NCS].rearrange("c (t p) -> c t p", p=16),
                                         in1=ys[0:NCS].rearrange("c (t p) -> c t p", p=16), op=ALU.mult, reads=[b_g2_, b_ys], writes=[b_z_tm])
                                for tp in range(T1):
                                    pv_, pbb = (PSb6, PSB[6]) if tp % 2 == 0 else (PSb7, PSB[7])
                                    for k in range(4):
                                        C.op("tensor", "transpose", pv_[:, k * 128:k * 128 + NCS], z_tm[0:NCS, tp, k * 128:(k + 1) * 128], identb[0:NCS, 0:NCS],
                                             reads=[b_z_tm, b_identb], writes=[pbb])
                                    tau = T1 - 1 - tp
                                    C.op("vector" if tp % 2 == 0 else "scalar", "tensor_copy" if tp % 2 == 0 else "copy",
                                         out=zT[:, :, tau:tau + T1 * (NCS - 1) + 1:T1], in_=pv_[:, 0:512].rearrange("p (k c) -> p k c", k=4)[:, :, 0:NCS],
                                         reads=[pbb], writes=[b_zT])
                                sgt = [T(f"sgt{i}", [128, 512], stack=s3) for i in range(2)]
                                obt, b_obt = T("obt", [128, 4, 512], stack=s3)
                                sqb, b_sqb = T("sqb", [128, 4, 512], BF16, stack=s3)
                                lnb, b_lnb = T("lnb", [128, 512], stack=s3)
                                rsb, b_rsb = T("rsb", [128, 512], stack=s3)
                                mxb = [T(f"mxb{i}", [128, 4, 512], BF16, stack=s3) for i in range(2)]
                                for tt in range(SEG // 512):
                                    tsl = slice(tt * 512, (tt + 1) * 512)
                                    for oa in range(4):
                                        for half, pb in ((1, 0), (0, 1)):
                                            oc = oa + 4 * half
                                            for k in range(4):
                                                C.op("tensor", "matmul", PS[pb][:, :], lhsT=gw[:, k, oc * 128:(oc + 1) * 128], rhs=zT[:, k, tsl], start=(k == 0), stop=(k == 3),
                                                     reads=[b_gw, b_zT], writes=[PSB[pb]])
                                        sg_, b_sg_ = sgt[oa % 2]
                                        C.op("scalar", "activation", out=sg_[:], in_=PS[0][:, :], func=AF.Sigmoid, bias=gbc[:, oa + 4:oa + 5], reads=[PSB[0], b_gbc], writes=[b_sg_])
                                        C.op("vector", "scalar_tensor_tensor", out=obt[:, oa, :], in0=PS[1][:, :], scalar=gbc[:, oa:oa + 1], in1=sg_[:], op0=ALU.add, op1=ALU.mult,
                                             reads=[PSB[1], b_gbc, b_sg_], writes=[b_obt])
                                    C.op("scalar", "activation", out=sqb[:], in_=obt[:], func=AF.Square, reads=[b_obt], writes=[b_sqb])
                                    for k in range(4):
                                        C.op("tensor", "matmul", PS[2][:, :], lhsT=onesb[:, :], rhs=sqb[:, k, :], start=(k == 0), stop=(k == 3), reads=[b_onesb, b_sqb], writes=[PSB[2]])
                                    rstd_from(None, rsb[:], PS[2][:, :], 1.0 / 512, lnb[:], [PSB[2]], b_rsb, b_lnb)
                                    mb_, b_mb_ = mxb[tt % 2]
                                    C.op("vector", "tensor_tensor", out=mb_[:], in0=obt[:], in1=rsb[:, :].unsqueeze(1).broadcast_to([128, 4, 512]), op=ALU.mult,
                                         reads=[b_obt, b_rsb], writes=[b_mb_])
                                    gt = (seg * SEG) // 512 + tt
                                    C.dma(mixT.rearrange("(k p) t -> p k t", p=128)[:, 2:6, gt * 512:(gt + 1) * 512], mb_[:], reads=[b_mb_], writes=[mixb_b[gt]], sbuf=b_mb_)
                                C.barrier()
                    C.barrier()
            if want("p3"):
                with contextlib.ExitStack() as ph:
                    kcmpT = sb(ph, "p3_kcmpT", [128, 512], BF16)
                    vco = sb(ph, "p3_vco", [128, 4, 193], BF16)
                    b_kcmpT, b_vco = C.buf("kcmpT"), C.buf("vco")
                    gq3 = sb(ph, "p3_gq", [128, 6], F32)
                    b_gq3 = C.buf("p3gq")
                    for half in range(2):
                        C.dma(gq3[half * 64:(half + 1) * 64, :], qk_g[l].rearrange("s d -> d s"), reads=[inb], writes=[b_gq3],
                              sbuf=b_gq3, allow_slow_non_contiguous=True)
                    with contextlib.ExitStack() as pc1:
                        w1s = sb(pc1, "c_w1s", [128, 32, 128], F32)
                        w1 = sb(pc1, "c_w1", [128, 32, 128], BF16)
                        b_w1s, b_w1 = C.buf("cw1s"), C.buf("cw1")
                        for st in range(2):
                            C.dma(w1s[st * 64:(st + 1) * 64, :, :], cmp_w1_d[l, st].rearrange("(r d) h -> d r h", d=64), reads=[inb],
                                  writes=[b_w1s], sbuf=b_w1s)
                        C.op("vector", "tensor_copy", out=w1[:], in_=w1s[:], reads=[b_w1s], writes=[b_w1])
                        pin = sb(pc1, "c_pin", [32, 128], F32)
                        b_pin = C.buf("cpin")
                        for st in range(2):
                            C.dma(pin[:, st * 64:(st + 1) * 64], cmp_pos_d[l, st], reads=[inb], writes=[b_pin], sbuf=b_pin)
                        C.op("tensor", "transpose", PS[0][:, 0:32], pin[:, :], ident[0:32, 0:32], reads=[b_pin, b_ident], writes=[PSB[0]])
                        posT = sb(pc1, "c_posT", [128, 32], BF16)
                        b_posT = C.buf("cposT")
                        C.op("vector", "tensor_copy", out=posT[:], in_=PS[0][:, 0:32], reads=[PSB[0]], writes=[b_posT])
                        w2s = sb(pc1, "c_w2s", [128, 192], F32)
                        w2b = sb(pc1, "c_w2b", [128, 192], BF16)
                        b_w2s, b_w2b = C.buf("cw2s"), C.buf("cw2b")
                        C.dma(w2s[:, 0:64], cmp_w2_d[l, 0], reads=[inb], writes=[b_w2s], sbuf=b_w2s)
                        C.dma(w2s[:, 64:128], cmp_w2_d[l, 0], reads=[inb], writes=[b_w2s], sbuf=b_w2s)
                        C.dma(w2s[:, 128:192], cmp_w2_d[l, 1], reads=[inb], writes=[b_w2s], sbuf=b_w2s)
                        C.op("vector", "tensor_copy", out=w2b[:], in_=w2s[:], reads=[b_w2s], writes=[b_w2b])
                        kvc = sb(pc1, "c_kvc", [128, L], BF16)
                        b_kvc = C.buf("ckvc")
                        C.dma(kvc[:], fm[5], reads=fm_b, writes=[b_kvc], sbuf=b_kvc)
                        ovs = sb(pc1, "c_ovs", [128, 4, 128], F32)
                        b_ovs = C.buf("covs")
                        C.dma(ovs[:], c_overlap[:, :, :], reads=[inb], writes=[b_ovs], sbuf=b_ovs)
                        C.op("vector", "memset", vco[:], 0.0, writes=[b_vco])
                        C.op("vector", "tensor_copy", out=vco[:, :, 65:193], in_=ovs[:], reads=[b_ovs], writes=[b_vco])
                        C.op("vector", "memset", kcmpT[:], 0.0, writes=[b_kcmpT])
                        pbias = sb(pc1, "c_pbias", [128, 2], F32)
                        b_pbias = C.buf("cpbias")
                        hs = sb(pc1, "c_hs", [128, 512], F32)
                        t1c = sb(pc1, "c_t1", [128, 512], F32)
                        t2c = sb(pc1, "c_t2", [128, 512], F32)
                        hg = sb(pc1, "c_hg", [128, 512], BF16)
                        b_hs, b_t1c, b_t2c, b_hg = C.buf("chs"), C.buf("ct1"), C.buf("ct2"), C.buf("chg")
                        ykc = sb(pc1, "c_ykc", [128, 512], F32)
                        sqkc = sb(pc1, "c_sqkc", [128, 512], BF16)
                        b_ykc, b_sqkc = C.buf("cykc"), C.buf("csqkc")
                        for st in range(2):
                            base = st * 64
                            for r in range(32):
                                C.op("tensor", "matmul", PS[1][:, st:st + 1], lhsT=w1[base:base + 64, r, :], rhs=posT[base:base + 64, r:r + 1],
                                     start=(r == 0), stop=(r == 31), reads=[b_w1, b_posT], writes=[PSB[1]])
                            C.op("vector", "tensor_copy", out=pbias[:, st:st + 1], in_=PS[1][:, st:st + 1], reads=[PSB[1]], writes=[b_pbias])
                            for r in range(32):
                                C.op("tensor", "matmul", PS[2][:, 0:MC], lhsT=w1[base:base + 64, r, :],
                                     rhs=kvc[base:base + 64, r:r + 16 * (MC - 1) + 1:16],
                                     start=(r == 0), stop=(r == 31), reads=[b_w1, b_kvc], writes=[PSB[2]])
                            C.op("vector", "tensor_scalar", out=hs[:, 0:MC], in0=PS[2][:, 0:MC], scalar1=pbias[:, st:st + 1], scalar2=None, op0=ALU.add,
                                 reads=[PSB[2], b_pbias], writes=[b_hs])
                            C.op("scalar", "activation", out=t1c[:, 0:MC], in_=hs[:, 0:MC], func=AF.Square, reads=[b_hs], writes=[b_t1c])
                            C.op("vector", "tensor_scalar", out=t1c[:, 0:MC], in0=t1c[:, 0:MC], scalar1=0.044715, scalar2=1.0, op0=ALU.mult, op1=ALU.add,
                                 reads=[b_t1c], writes=[b_t1c])
                            C.op("vector", "tensor_tensor", out=t1c[:, 0:MC], in0=t1c[:, 0:MC], in1=hs[:, 0:MC], op=ALU.mult, reads=[b_t1c, b_hs], writes=[b_t1c])
                            C.op("scalar", "activation", out=t2c[:, 0:MC], in_=t1c[:, 0:MC], func=AF.Sigmoid, scale=1.5957691216, reads=[b_t1c], writes=[b_t2c])
                            C.op("vector", "memset", hg[:], 0.0, writes=[b_hg])
                            C.op("vector", "tensor_tensor", out=hg[:, 0:MC], in0=t2c[:, 0:MC], in1=hs[:, 0:MC], op=ALU.mult, reads=[b_t2c, b_hs], writes=[b_hg])
                            if st == 0:
                                C.op("tensor", "matmul", PS[3][:, 0:MC], lhsT=w2b[:, 0:128], rhs=hg[:, 0:MC], start=True, stop=True,
                                     reads=[b_w2b, b_hg], writes=[PSB[3]])
                                C.op("vector", "tensor_copy", out=ykc[:, 0:MC], in_=PS[3][:, 0:MC], reads=[PSB[3]], writes=[b_ykc])
                                C.op("scalar", "activation", out=sqkc[:, 0:MC], in_=ykc[:, 0:MC], func=AF.Square, reads=[b_ykc], writes=[b_sqkc])
                                C.op("tensor", "matmul", PS[4][:, 0:MC], lhsT=bd64b[:, :], rhs=sqkc[:, 0:MC], start=True, stop=True,
                                     reads=[b_bd64b, b_sqkc], writes=[PSB[4]])
                                rstd_from(None, t2c[:, 0:MC], PS[4][:, 0:MC], 1.0 / HD, t1c[:, 0:MC], [PSB[4]], b_t2c, b_t1c)
                                C.op("vector", "scalar_tensor_tensor", out=kcmpT[:, 0:MC], in0=ykc[:, 0:MC], scalar=gq3[:, 3:4], in1=t2c[:, 0:MC],
                                     op0=ALU.mult, op1=ALU.mult, reads=[b_ykc, b_gq3, b_t2c], writes=[b_kcmpT])
                            else:
                                for cch in range(NCH):
                                    mr = min(128, MC - cch * 128)
                                    C.op("tensor", "matmul", PS[3][0:mr, 0:64], lhsT=hg[:, cch * 128:cch * 128 + mr], rhs=w2b[:, 128:192], start=True, stop=True,
                                         reads=[b_w2b, b_hg], writes=[PSB[3]])
                                    C.op("vector", "tensor_copy", out=vco[0:mr, cch, 0:64], in_=PS[3][0:mr, 0:64], reads=[PSB[3]], writes=[b_vco])
                                    C.op("vector", "memset", vco[0:mr, cch, 64:65], 1.0, writes=[b_vco])
                        C.barrier()
                    kaT = sb(ph, "p3_kaT", [128, L], BF16)
                    ksT = sb(ph, "p3_ksT", [128, L], BF16)
                    kwT = sb(ph, "p3_kwT", [128, L], BF16)
                    b_kaT, b_ksT, b_kwT = C.buf("kaT"), C.buf("ksT"), C.buf("kwT")
                    C.dma(kaT[:], fm[2], reads=fm_b, writes=[b_kaT], sbuf=b_kaT)
                    C.dma(ksT[:], fm[6], reads=fm_b, writes=[b_ksT], sbuf=b_ksT)
                    C.dma(kwT[:], fm[7], reads=fm_b, writes=[b_kwT], sbuf=b_kwT)
                    vall = sb(ph, "p3_vall", [128, NB, 260], BF16)
                    gall = sb(ph, "p3_gall", [128, NB, 12], F32)
                    b_vall, b_gall = C.buf("vall"), C.buf("gall")
                    for q0 in range(0, NB, 16):
                        q1 = min(NB, q0 + 16)
                        C.dma(vall[:, q0:q1, :], vtm.rearrange("(b p) c -> p b c", p=128)[:, q0:q1, :], reads=vtm_b, writes=[b_vall], sbuf=b_vall)
                        C.dma(gall[:, q0:q1, :], gtm.rearrange("(b p) c -> p b c", p=128)[:, q0:q1, :], reads=gtm_b, writes=[b_gall], sbuf=b_gall)
                    wexpb = sb(ph, "p3_wexp", [128, L], BF16)
                    b_wexpb = C.buf("wexpb")
                    tst = [sb(ph, f"p3_tst{i}", [128, 1024], F32) for i in range(2)]
                    b_tst = [C.buf(f"p3tst{i}") for i in range(2)]
                    cnt = 0

                    def stage_cast(dst_ap, src_ap, n, bdst, post=None):
                        nonlocal cnt
                        s_ = cnt % 2
                        cnt += 1
                        C.dma(tst[s_][:, 0:n], src_ap, reads=[inb], writes=[b_tst[s_]], sbuf=b_tst[s_])
                        if post is None:
                            C.op("vector", "tensor_copy", out=dst_ap, in_=tst[s_][:, 0:n], reads=[b_tst[s_]], writes=[bdst])
                        else:
                            post(tst[s_], b_tst[s_])
                    for c0 in range(0, L, 1024):
                        stage_cast(wexpb[:, c0:c0 + 1024], c_wexp[:, c0:c0 + 1024], 1024, b_wexpb)
                    b31 = sb(ph, "p3_b31", [128, 4], F32)
                    b_b31 = C.buf("b31")
                    C.dma(b31[:], t_b31[:, :], reads=[inb], writes=[b_b31], sbuf=b_b31)
                    bA = sb(ph, "p3_bA", [128, 2, 512], BF16)
                    bNp = sb(ph, "p3_bNp", [128, 8, 512], BF16)
                    bNs = sb(ph, "p3_bNs", [128, 8, 512], BF16)
                    bW4 = sb(ph, "p3_bW4", [128, 512], BF16)
                    bCt = sb(ph, "p3_bC", [128, 24, 512], BF16)
                    b_bA, b_bNp, b_bNs, b_bW4, b_bCt = C.buf("bA"), C.buf("bNp"), C.buf("bNs"), C.buf("bW4"), C.buf("bCt")
                    for dl in range(2):
                        stage_cast(bA[:, dl, :], t_biasA[:, dl].rearrange("p h q -> p (h q)"), 512, b_bA)
                    for dl in range(8):
                        def post(tt, btt, dl=dl):
                            C.op("vector", "tensor_copy", out=bNp[:, dl, :], in_=tt[:, 0:512], reads=[btt], writes=[b_bNp])
                            C.op("vector", "tensor_tensor", out=bNs[:, dl, :].rearrange("p (h q) -> p h q", h=4),
                                 in0=tt[:, 0:512].rearrange("p (h q) -> p h q", h=4),
                                 in1=b31[:, :].unsqueeze(2).broadcast_to([128, 4, 128]), op=ALU.subtract,
                                 reads=[btt, b_b31], writes=[b_bNs])
                        stage_cast(None, t_biasN[:, dl].rearrange("p h q -> p (h q)"), 512, None, post=post)
                    stage_cast(bW4[:, :], t_biasW4.rearrange("p h q -> p (h q)"), 512, b_bW4)
                    for dl in range(24):
                        stage_cast(bCt[:, dl, :], t_biasC[:, dl].rearrange("p h q -> p (h q)"), 512, b_bCt)
                    m1b = sb(ph, "p3_m1b", [128, 256], F32)
                    m2b = sb(ph, "p3_m2b", [128, 256], F32)
                    b_m1b, b_m2b = C.buf("m1b"), C.buf("m2b")
                    C.dma(m1b[:], c_m1[:, :], reads=[inb], writes=[b_m1b], sbuf=b_m1b)
                    C.dma(m2b[:], c_m2[:, :], reads=[inb], writes=[b_m2b], sbuf=b_m2b)
                    es4 = sb(ph, "p3_es4", [128, 4], F32)
                    b_es4 = C.buf("es4")
                    C.dma(es4[:], sinks[l].partition_broadcast(128), reads=[inb], writes=[b_es4], sbuf=b_es4)
                    C.op("scalar", "activation", out=es4[:], in_=es4[:], func=AF.Exp, reads=[b_es4], writes=[b_es4])
                    qt = [sb(ph, f"p3_qt{i}", [128, 4, 8, 128], BF16) for i in range(2)]
                    b_qt = [C.buf(f"p3qt{i}") for i in range(2)]
                    for i in range(2):
                        C.op("vector", "memset", qt[i][:], 0.0, writes=[b_qt[i]])
                    pt = [sb(ph, f"p3_pt{i}", [128, 512], BF16) for i in range(3)]
                    b_pt = [C.buf(f"p3pt{i}") for i in range(3)]
                    NSB = 3
                    sctr = [0]
                    imp = sb(ph, "p3_imp", [128, 128], F32)
                    score = sb(ph, "p3_score", [128, 128], F32)
                    sc2 = sb(ph, "p3_sc2", [128, 128], F32)
                    m8 = sb(ph, "p3_m8", [128, 16], F32)
                    negm = sb(ph, "p3_negm", [128, 128], BF16)
                    negmT4 = sb(ph, "p3_negmT4", [128, 4, 128], BF16)
                    b_imp, b_score, b_sc2, b_m8, b_negm, b_negmT4 = (C.buf(n) for n in ["imp", "score", "sc2", "m8", "negm", "negmT4"])
                    dens = sb(ph, "p3_dens", [128, 16], F32)
                    rden = sb(ph, "p3_rden", [128, 16], F32)
                    coef = sb(ph, "p3_coef", [128, 12], F32)
                    b_dens, b_rden, b_coef = C.buf("dens"), C.buf("rden"), C.buf("coef")
                    o_a = sb(ph, "p3_oa", [128, 4, 64], F32)
                    o_c = sb(ph, "p3_oc", [128, 4, 64], F32)
                    otmp = sb(ph, "p3_otmp", [128, 4, 64], F32)
                    b_oa, b_oc, b_otmp = C.buf("oa"), C.buf("oc"), C.buf("otmp")
                    junk = sb(ph, "p3_junk", [128, 256], F32)
                    ssn = sb(ph, "p3_ssn", [128, 4], F32)
                    onb = sb(ph, "p3_onb", [128, 2, 256], BF16)
                    b_junk, b_ssn, b_onb = C.buf("junk"), C.buf("ssn"), C.buf("onb")
                    otx = [sb(ph, f"p3_otx{i}", [128, 512], F32) for i in range(3)]
                    b_otx = [C.buf(f"p3otx{i}") for i in range(3)]
                    mst = [sb(ph, f"p3_mst{i}", [128, 4, 512], BF16) for i in range(2)]
                    b_mst = [C.buf(f"p3mst{i}") for i in range(2)]
                    PSTb = PS[7][:, :].bitcast(BF16)

                    def score_tile(ncols, bias_rhs, bias_bufs, mask, qk, pv):
                        pb = sctr[0] % 2
                        j = sctr[0] % NSB
                        sctr[0] += 1
                        first = True
                        if bias_rhs is not None:
                            C.op("tensor", "matmul", PS[pb][:, 0:ncols], lhsT=identb[:, :], rhs=bias_rhs, start=True, stop=False,
                                 reads=[b_identb] + bias_bufs, writes=[PSB[pb]])
                            first = False
                        if mask is not None:
                            C.op("tensor", "matmul", PS[pb][:, 0:ncols], lhsT=mask[0], rhs=mask[1], start=first, stop=False,
                                 reads=[b_wexpb, b_negmT4], writes=[PSB[pb]])
                            first = False
                        assert not first
                        for qi, (lhsT, rhs, c0, n, rb) in enumerate(qk):
                            C.op("tensor", "matmul", PS[pb][:, c0:c0 + n], lhsT=lhsT, rhs=rhs, start=False, stop=(qi == len(qk) - 1),
                                 reads=rb, writes=[PSB[pb]])
                        C.op("scalar", "activation", out=pt[j][:, 0:ncols], in_=PS[pb][:, 0:ncols], func=AF.Exp, reads=[PSB[pb]], writes=[b_pt[j]])
                        for (out_ap, c0, rhs, st_, sp_, ob, rb) in pv:
                            if c0 is None:
                                vl, cc0, ncl = rhs
                                C.op("tensor", "matmul", out_ap, lhsT=vl, rhs=pt[j][:, cc0:cc0 + ncl], start=st_, stop=sp_,
                                     reads=[b_pt[j]] + rb, writes=[ob])
                            else:
                                C.op("tensor", "matmul", out_ap, lhsT=pt[j][:, c0:c0 + 128], rhs=rhs, start=st_, stop=sp_,
                                     reads=[b_pt[j]] + rb, writes=[ob])

                    def load_q(ti):
                        s_ = ti % 2
                        tsl = slice(ti * 512, (ti + 1) * 512)
                        for kvh in range(2):
                            for e in range(2):
                                C.dma(qt[s_][kvh * 64:(kvh + 1) * 64, :, 2 * kvh + e, :], fm[e, kvh * 64:(kvh + 1) * 64, tsl].rearrange("p (b q) -> p b q", b=4),
                                      reads=[fm_b[ti]], writes=[b_qt[s_]], sbuf=b_qt[s_])
                        for h in range(4):
                            r0 = (h % 2) * 64
                            C.dma(qt[s_][r0:r0 + 64, :, 4 + h, :], fm[3 + h // 2, r0:r0 + 64, tsl].rearrange("p (b q) -> p b q", b=4),
                                  reads=[fm_b[ti]], writes=[b_qt[s_]], sbuf=b_qt[s_])
                    load_q(0)
                    for bi in range(NB if want("p3loop") else 0):
                        ti = bi // 4
                        s_ = ti % 2
                        qo = (bi % 4) * 128
                        if bi % 4 == 0 and ti + 1 < NT:
                            load_q(ti + 1)
                        qT = qt[s_]
                        bq = b_qt[s_]

                        qb_ = bi % 4
                        qc_all = qT[:, qb_, 4:8, :].rearrange("p h q -> p (h q)")
                        qa_all = qT[:, qb_, 0:4, :].rearrange("p h q -> p (h q)")
                        nck = min(NCH, (8 * bi + 6) // 128 + 1)
                        for cch in range(nck if want("cmp") else 0):
                            dli = min(bi - 16 * cch, 23)
                            qk = [(kcmpT[:, cch * 128:(cch + 1) * 128], qc_all, 0, 512, [b_kcmpT, bq])]
                            pv = [(PS[2 + h // 2][:, (h % 2) * 193:(h % 2) * 193 + 193], h * 128, vco[:, cch, :], cch == 0 and h % 2 == 0, cch == nck - 1 and h % 2 == 1, PSB[2 + h // 2], [b_vco])
                                  for h in range(4)]
                            score_tile(512, bCt[:, dli, :], [b_bCt], None, qk, pv)
                        if want("topk"):
                            for bk in range(2):
                                C.op("vector", "tensor_scalar", out=dens[:, 2 * bk:2 * bk + 2], in0=PS[2 + bk][:, 64:64 + 194:193], scalar1=1e-30, scalar2=None,
                                     op0=ALU.max, reads=[PSB[2 + bk]], writes=[b_dens])
                            C.op("vector", "reciprocal", out=rden[:, 0:4], in_=dens[:, 0:4], reads=[b_dens], writes=[b_rden])
                            for h in range(4):
                                src = PS[2 + h // 2][:, (h % 2) * 193 + 65:(h % 2) * 193 + 193]
                                if h == 0:
                                    C.op("vector", "tensor_scalar", out=imp[:], in0=src, scalar1=rden[:, 0:1], scalar2=None, op0=ALU.mult,
                                         reads=[PSB[2], b_rden], writes=[b_imp])
                                else:
                                    C.op("vector", "scalar_tensor_tensor", out=imp[:], in0=src, scalar=rden[:, h:h + 1], in1=imp[:], op0=ALU.mult, op1=ALU.add,
                                         reads=[PSB[2 + h // 2], b_rden, b_imp], writes=[b_imp])
                            w0 = 126 - 2 * bi
                            C.op("vector", "tensor_tensor", out=score[:], in0=imp[:], in1=m1b[:, w0:w0 + 128], op=ALU.mult, reads=[b_imp, b_m1b], writes=[b_score])
                            C.op("vector", "tensor_tensor", out=score[:], in0=score[:], in1=m2b[:, w0:w0 + 128], op=ALU.add, reads=[b_score, b_m2b], writes=[b_score])
                            C.op("vector", "memset", score[:, 0:1], 1e4, writes=[b_score])
                            C.op("vector", "max", out=m8[:, 0:8], in_=score[:], reads=[b_score], writes=[b_m8])
                            C.op("vector", "match_replace", out=sc2[:], in_to_replace=m8[:, 0:8], in_values=score[:], imm_value=-3e4,
                                 reads=[b_score, b_m8], writes=[b_sc2])
                            C.op("vector", "max", out=m8[:, 8:16], in_=sc2[:], reads=[b_sc2], writes=[b_m8])
                            C.op("vector", "tensor_scalar", out=negm[:], in0=score[:], scalar1=m8[:, 15:16], scalar2=NEG, op0=ALU.is_lt, op1=ALU.mult,
                                 reads=[b_score, b_m8], writes=[b_negm])
                            C.op("tensor", "transpose", PSTb[:, 0:128], negm[:, :], identb[:, :], reads=[b_negm, b_identb], writes=[PSB[7]])
                            for h in range(4):
                                C.op("vector" if h % 2 == 0 else "gpsimd" if False else "vector", "tensor_scalar", out=negmT4[:, h, :], in0=PSTb[:, 0:128],
                                     scalar1=b31[:, h:h + 1], scalar2=None, op0=ALU.add, reads=[PSB[7], b_b31], writes=[b_negmT4])
                        for kc in range(bi + 1 if want("sel") else 0):
                            dl = bi - kc
                            near = dl < 8
                            qk = [(ksT[:, kc * 128:(kc + 1) * 128], qc_all, 0, 512, [b_ksT, bq])]
                            pv = [(PS[4][0:65, 0:512], None, (vall[:, kc, 130:195], 0, 512), kc == 0, kc == bi, PSB[4], [b_vall])]
                            score_tile(512, bNs[:, dl, :] if near else None, [b_bNs], (wexpb[:, kc * 128:(kc + 1) * 128], negmT4[:].rearrange("p h q -> p (h q)")), qk, pv)
                        k0 = max(0, bi - 4)
                        for kc in range(k0, bi + 1 if want("win") else 0):
                            dl = bi - kc
                            qk = [(kwT[:, kc * 128:(kc + 1) * 128], qc_all, 0, 512, [b_kwT, bq])]
                            pv = [(PS[5][0:65, 0:512], None, (vall[:, kc, 195:260], 0, 512), kc == k0, kc == bi, PSB[5], [b_vall])]
                            score_tile(512, bW4[:, :] if dl == 4 else bNp[:, dl, :], [b_bW4, b_bNp], None, qk, pv)
                        k0 = max(0, bi - 1)
                        for kc in range(k0, bi + 1 if want("swa") else 0):
                            dl = bi - kc
                            qk = [(kaT[:, kc * 128:(kc + 1) * 128], qa_all, 0, 512, [b_kaT, bq])]
                            pv = [(PS[6][0:65, kvh * 256:(kvh + 1) * 256], None, (vall[:, kc, kvh * 65:(kvh + 1) * 65], kvh * 256, 256),
                                   kc == k0 and kvh == 0, kc == bi and kvh == 1, PSB[6], [b_vall]) for kvh in range(2)]
                            score_tile(512, bA[:, dl, :], [b_bA], None, qk, pv)
                        if want("epi"):
                            for xi, bnk in enumerate((4, 5, 6)):
                                C.op("scalar" if xi != 1 else "vector", "copy" if xi != 1 else "tensor_copy", out=otx[xi][0:65, :], in_=PS[bnk][0:65, 0:512],
                                     reads=[PSB[bnk]], writes=[b_otx[xi]])
                                for h in range(4):
                                    C.op("tensor", "transpose", PS[bnk][:, h * 65:(h + 1) * 65], otx[xi][0:65, h * 128:(h + 1) * 128], ident[0:65, 0:65],
                                         reads=[b_otx[xi], b_ident], writes=[PSB[bnk]])
                            C.op("vector", "tensor_copy", out=dens[:, 4:8], in_=PS[4][:, 64:64 + 4 * 65:65], reads=[PSB[4]], writes=[b_dens])
                            C.op("vector", "tensor_copy", out=dens[:, 8:12], in_=PS[5][:, 64:64 + 4 * 65:65], reads=[PSB[5]], writes=[b_dens])
                            C.op("vector", "tensor_tensor", out=dens[:, 12:16], in0=PS[6][:, 64:64 + 4 * 65:65], in1=es4[:], op=ALU.add, reads=[PSB[6], b_es4], writes=[b_dens])
                            C.op("vector", "reciprocal", out=rden[:, 4:16], in_=dens[:, 4:16], reads=[b_dens], writes=[b_rden])
                            C.op("vector", "tensor_tensor", out=coef[:].rearrange("p (b h) -> p b h", b=3), in0=rden[:, 0:12].rearrange("p (b h) -> p b h", b=3),
                                 in1=gall[:, bi, :].rearrange("p (h b) -> p b h", b=3), op=ALU.mult, reads=[b_rden, b_gall], writes=[b_coef])
                            for bk in range(2):
                                C.op("vector", "tensor_tensor", out=o_c[:, 2 * bk:2 * bk + 2, :],
                                     in0=PS[2 + bk][:, 0:386].rearrange("p (h c) -> p h c", c=193)[:, :, 0:64],
                                     in1=coef[:, 2 * bk:2 * bk + 2].unsqueeze(2).broadcast_to([128, 2, 64]), op=ALU.mult,
                                     reads=[PSB[2 + bk], b_coef], writes=[b_oc])
                            for (bnk, c0) in [(4, 4), (5, 8)]:
                                C.op("vector", "tensor_tensor", out=otmp[:], in0=PS[bnk][:, 0:260].rearrange("p (h c) -> p h c", c=65)[:, :, 0:64],
                                     in1=coef[:, c0:c0 + 4].unsqueeze(2).broadcast_to([128, 4, 64]), op=ALU.mult, reads=[PSB[bnk], b_coef], writes=[b_otmp])
                                C.op("gpsimd", "tensor_tensor", out=o_c[:], in0=o_c[:], in1=otmp[:], op=ALU.add, reads=[b_oc, b_otmp], writes=[b_oc])
                            C.op("vector", "tensor_tensor", out=o_a[:], in0=PS[6][:, 0:260].rearrange("p (h c) -> p h c", c=65)[:, :, 0:64],
                                 in1=rden[:, 12:16].unsqueeze(2).broadcast_to([128, 4, 64]), op=ALU.mult, reads=[PSB[6], b_rden], writes=[b_oa])
                            for gi, (ot, bo) in enumerate([(o_a, b_oa), (o_c, b_oc)]):
                                C.op("scalar", "activation", out=junk[:], in_=ot[:].rearrange("p h d -> p (h d)"), func=AF.Square, accum_out=ssn[:, gi:gi + 1],
                                     reads=[bo], writes=[b_junk, b_ssn])
                            C.op("scalar", "activation", out=ssn[:, 2:4], in_=ssn[:, 0:2], func=AF.Ln, bias=EPS, scale=1.0 / 256, reads=[b_ssn], writes=[b_ssn])
                            C.op("scalar", "activation", out=ssn[:, 2:4], in_=ssn[:, 2:4], func=AF.Exp, scale=-0.5, reads=[b_ssn], writes=[b_ssn])
                            for gi, (ot, bo) in enumerate([(o_a, b_oa), (o_c, b_oc)]):
                                C.op("vector", "tensor_scalar", out=onb[:, gi, :], in0=ot[:].rearrange("p h d -> p (h d)"), scalar1=ssn[:, 2 + gi:3 + gi], scalar2=None,
                                     op0=ALU.mult, reads=[bo, b_ssn], writes=[b_onb])
                            for gi in range(2):
                                for cc in range(2):
                                    C.op("tensor", "transpose", PSTb[:, 128 + (gi * 2 + cc) * 128:128 + (gi * 2 + cc + 1) * 128], onb[:, gi, cc * 128:(cc + 1) * 128],
                                         identb[:, :], reads=[b_onb, b_identb], writes=[PSB[7]])
                            ms = mst[ti % 2]
                            bms = b_mst[ti % 2]
                            C.op("scalar", "activation", out=ms[:, :, qo:qo + 128], in_=PSTb[:, 128:640].rearrange("p (c q) -> p c q", c=4), func=AF.Copy, reads=[PSB[7]], writes=[bms])
                            if bi % 4 == 3:
                                tsl = slice(ti * 512, (ti + 1) * 512)
                                mv = mixT.rearrange("(k p) t -> p k t", p=128)
                                C.dma(mv[:, 0:2, tsl], ms[:, 0:2, :], reads=[bms], writes=[mixa_b[ti]], sbuf=bms)
                                C.dma(mv[:, 6:8, tsl], ms[:, 2:4, :], reads=[bms], writes=[mixa_b[ti]], sbuf=bms)
                    C.barrier()
            if want("p4"):
                with contextlib.ExitStack() as ph:
                    wo = sb(ph, "p4_wo", [128, 8, D], BF16)
                    b_wo = C.buf("p4wo")
                    stg = [sb(ph, f"p4_stg{i}", [128, 1, D], F32) for i in range(2)]
                    b_stg = [C.buf(f"p4stg{i}") for i in range(2)]
                    gout, b_gout = load_cols(ph, "p4_gout", out_norm_g[l], 8)
                    wov = w_out_d[l].rearrange("(k p) c -> p k c", p=128)
                    for kk2 in range(8):
                        s_ = kk2 % 2
                        C.dma(stg[s_][:], wov[:, kk2:kk2 + 1, :], reads=[inb], writes=[b_stg[s_]], sbuf=b_stg[s_])
                        C.op("vector" if s_ == 0 else "gpsimd", "tensor_tensor", out=wo[:, kk2:kk2 + 1, :], in0=stg[s_][:],
                             in1=gout[:, kk2:kk2 + 1].unsqueeze(2).broadcast_to([128, 1, D]), op=ALU.mult, reads=[b_stg[s_], b_gout], writes=[b_wo])
                    xa = [sb(ph, f"p4_xa{i}", [128, 8, 512], F32) for i in range(2)]
                    b_xa = [C.buf(f"p4xa{i}") for i in range(2)]
                    mx = [sb(ph, f"p4_mx{i}", [128, 8, 512], BF16) for i in range(2)]
                    b_mx = [C.buf(f"p4mx{i}") for i in range(2)]

                    def p4a_load(i):
                        s_ = i % 2
                        tsl = slice(i * 512, (i + 1) * 512)
                        C.dma(xa[s_][:], xT.rearrange("(k p) t -> p k t", p=128)[:, :, tsl], reads=[xT_b[i]], writes=[b_xa[s_]], sbuf=b_xa[s_])
                        C.dma(mx[s_][:], mixT.rearrange("(k p) t -> p k t", p=128)[:, :, tsl], reads=[mixa_b[i], mixb_b[i]], writes=[b_mx[s_]], sbuf=b_mx[s_])
                    p4a_load(0)
                    pc = 0
                    for i in range(NT):
                        s_ = i % 2
                        if i + 1 < NT:
                            p4a_load(i + 1)
                        for oc in range(8):
                            pb = pc % 4
                            pc += 1
                            for k in range(8):
                                C.op("tensor", "matmul", PS[pb][:, :], lhsT=wo[:, k, oc * 128:(oc + 1) * 128], rhs=mx[s_][:, k, :],
                                     start=(k == 0), stop=(k == 7), reads=[b_wo, b_mx[s_]], writes=[PSB[pb]])
                            C.op("vector", "tensor_tensor", out=xa[s_][:, oc, :], in0=PS[pb][:, :], in1=xa[s_][:, oc, :], op=ALU.add,
                                 reads=[PSB[pb], b_xa[s_]], writes=[b_xa[s_]])
                        C.dma(xT.rearrange("(k p) t -> p k t", p=128)[:, :, i * 512:(i + 1) * 512], xa[s_][:], reads=[b_xa[s_]], writes=[xT_b[i]], sbuf=b_xa[s_])
                    C.barrier()
            if want("p4"):
                with contextlib.ExitStack() as ph:
                    TT = 512
                    wu = sb(ph, "p4_wu", [128, 8, 4 * D], BF16)
                    wd = sb(ph, "p4_wd", [128, 32, D], BF16)
                    b_wu, b_wd = C.buf("p4wu"), C.buf("p4wd")
                    stg = [sb(ph, f"p4b_stg{i}", [128, 1, D], F32) for i in range(2)]
                    b_stg = [C.buf(f"p4bstg{i}") for i in range(2)]
                    g2, b_g2 = load_cols(ph, "p4_g2", norm2_g[l], 8)
                    pieces = []
                    wuv = w_up_d[l].rearrange("(k p) c -> p k c", p=128)
                    for kk2 in range(8):
                        for cq in range(4):
                            pieces.append((wuv[:, kk2:kk2 + 1, cq * D:(cq + 1) * D], wu[:, kk2:kk2 + 1, cq * D:(cq + 1) * D],
                                           g2[:, kk2:kk2 + 1], b_g2, b_wu))
                    wdv = w_down_d[l].rearrange("(k p) c -> p k c", p=128)
                    for kk2 in range(32):
                        pieces.append((wdv[:, kk2:kk2 + 1, :], wd[:, kk2:kk2 + 1, :], None, None, b_wd))
                    for pi, (src, dst, gsc, bg, bdst) in enumerate(pieces):
                        s_ = pi % 2
                        C.dma(stg[s_][:], src, reads=[inb], writes=[b_stg[s_]], sbuf=b_stg[s_])
                        eng = "vector" if pi % 2 == 0 else "gpsimd"
                        if gsc is None:
                            C.op(eng, "tensor_copy", out=dst, in_=stg[s_][:], reads=[b_stg[s_]], writes=[bdst])
                        else:
                            C.op(eng, "tensor_tensor", out=dst, in0=stg[s_][:], in1=gsc.unsqueeze(2).broadcast_to([128, 1, D]), op=ALU.mult,
                                 reads=[b_stg[s_], bg], writes=[bdst])
                    xt4 = sb(ph, "p4_xt", [128, 8, TT], F32)
                    b_xt4 = C.buf("p4xt")
                    h2 = sb(ph, "p4_h2", [128, 8, TT], BF16)
                    sq4 = sb(ph, "p4_sq", [128, 8, TT], BF16)
                    b_h2, b_sq4 = C.buf("p4h2"), C.buf("p4sq")
                    hid = sb(ph, "p4_hid", [128, 16, TT], BF16)
                    b_hid = C.buf("p4hid")
                    lnv4 = sb(ph, "p4_lnv", [128, TT], F32)
                    rs4 = sb(ph, "p4_rs", [128, TT], F32)
                    b_lnv4, b_rs4 = C.buf("p4lnv"), C.buf("p4rs")
                    tmp4 = [sb(ph, f"p4_tmp{i}", [128, TT], F32) for i in range(2)]
                    b_tmp4 = [C.buf(f"p4tmp{i}") for i in range(2)]
                    yst = sb(ph, "p4_yst0", [128, D], F32)
                    b_yst = C.buf("p4yst0")
                    pc = 0
                    for i in range(NT):
                        tsl = slice(i * TT, (i + 1) * TT)
                        C.dma(xt4[:], xT.rearrange("(k p) t -> p k t", p=128)[:, :, tsl], reads=[xT_b[i]], writes=[b_xt4], sbuf=b_xt4)
                        C.op("scalar", "activation", out=sq4[:], in_=xt4[:], func=AF.Square, reads=[b_xt4], writes=[b_sq4])
                        C.op("gpsimd", "tensor_copy", out=h2[:], in_=xt4[:], reads=[b_xt4], writes=[b_h2])
                        for k in range(8):
                            C.op("tensor", "matmul", PS[4][:, :], lhsT=onesb[:, :], rhs=sq4[:, k, :], start=(k == 0), stop=(k == 7),
                                 reads=[b_onesb, b_sq4], writes=[PSB[4]])
                        rstd_from(None, rs4[:], PS[4][:, :], 1.0 / D, lnv4[:], [PSB[4]], b_rs4, b_lnv4)
                        for hf in range(2):
                            for fcl in range(16):
                                fc = hf * 16 + fcl
                                pb = pc % 4
                                pc += 1
                                q = fc % 2
                                for k in range(8):
                                    C.op("tensor", "matmul", PS[pb][:, :], lhsT=wu[:, k, fc * 128:(fc + 1) * 128], rhs=h2[:, k, :],
                                         start=(k == 0), stop=(k == 7), reads=[b_wu, b_h2], writes=[PSB[pb]])
                                C.op("vector", "scalar_tensor_tensor", out=tmp4[q][:], in0=PS[pb][:, :], scalar=0.0, in1=rs4[:], op0=ALU.max, op1=ALU.mult,
                                     reads=[PSB[pb], b_rs4], writes=[b_tmp4[q]])
                                C.op("scalar", "activation", out=hid[:, fcl, :], in_=tmp4[q][:], func=AF.Square, reads=[b_tmp4[q]], writes=[b_hid])
                            for oc in range(8):
                                pb = pc % 4
                                pc += 1
                                for k in range(16):
                                    C.op("tensor", "matmul", PS[pb][:, :], lhsT=wd[:, hf * 16 + k, oc * 128:(oc + 1) * 128], rhs=hid[:, k, :],
                                         start=(k == 0), stop=(k == 15), reads=[b_wd, b_hid], writes=[PSB[pb]])
                                C.op("vector", "tensor_tensor", out=xt4[:, oc, :], in0=PS[pb][:, :], in1=xt4[:, oc, :], op=ALU.add,
                                     reads=[PSB[pb], b_xt4], writes=[b_xt4])
                        if not last:
                            C.dma(xT.rearrange("(k p) t -> p k t", p=128)[:, :, tsl], xt4[:], reads=[b_xt4], writes=[xT_b[i]], sbuf=b_xt4)
                        else:
                            for sbk in range(TT // 128):
                                for half in range(2):
                                    pb = 5 + half
                                    for k in range(4):
                                        C.op("tensor", "transpose", PS[pb][:, k * 128:(k + 1) * 128],
                                             xt4[:, half * 4 + k, sbk * 128:(sbk + 1) * 128], ident[:, :],
                                             reads=[b_xt4, b_ident], writes=[PSB[pb]])
                                    C.op("vector" if half == 0 else "scalar", "tensor_copy" if half == 0 else "copy",
                                         out=yst[:, half * 512:(half + 1) * 512], in_=PS[pb][:, :], reads=[PSB[pb]], writes=[b_yst])
                                t0 = i * TT + sbk * 128
                                C.dma(y_out[t0:t0 + 128, :], yst[:], reads=[b_yst], writes=[y_b], sbuf=b_yst)
                    C.barrier()
        C.barrier()
    return nc


_CACHE = {}
LAYERS_PER_LAUNCH = 2


def kernel(**inputs):
    x = np.ascontiguousarray(np.asarray(inputs["x"], dtype=np.float32))
    B, L, _ = x.shape
    consts = host_consts(L)
    tabs = host_tables(inputs["rel_bias"], L)
    shared = {}
    for k, v in inputs.items():
        if k in ("x", "rel_bias"):
            continue
        shared[k] = np.ascontiguousarray(np.asarray(v, dtype=np.float32))
    for k, v in consts.items():
        shared["c_" + k] = v
    for k, v in tabs.items():
        shared["t_" + k] = v
    cur = [x[b] for b in range(B)]
    for l0 in range(0, DEPTH, LAYERS_PER_LAUNCH):
        key = (L, l0, LAYERS_PER_LAUNCH)
        if key not in _CACHE:
            _CACHE[key] = build(L, nlayers=LAYERS_PER_LAUNCH, layer0=l0)
        nc = _CACHE[key]
        in_maps = []
        for b in range(B):
            m = dict(shared)
            m["x"] = np.ascontiguousarray(cur[b])
            in_maps.append(m)
        res = run_bass_kernel_spmd(nc, in_maps, core_ids=list(range(B)))
        cur = [np.asarray(r["y"], dtype=np.float32) for r in res.results]
    return np.stack(cur, axis=0)
```

```python
import contextlib
import math
import numpy as np
import concourse.bass as bass
import concourse.mybir as mybir
from concourse.bass_utils import run_bass_kernel_spmd

F32 = mybir.dt.float32
BF16 = mybir.dt.bfloat16
I32 = mybir.dt.int32
AF = mybir.ActivationFunctionType
ALU = mybir.AluOpType

SAFE_SAME_ENGINE = True
PRUNE = False
FORCE_EXT = ("kr",)

D = 1024
HD = 64
DEPTH = 4
T1 = 32
NG = 32
NEG = -30000.0
EPS = 1e-6
IN_W = 1676
TWO_PI = 2.0 * math.pi


class Buf:
    __slots__ = ("name", "w", "r", "dsem", "dcount")

    def __init__(self, name=""):
        self.name = name
        self.w = {}
        self.r = {}
        self.dsem = None
        self.dcount = 0


class EngW:
    def __init__(self, ctx, name, eng):
        self.ctx = ctx
        self.name = name
        self.eng = eng
        self.sem = ctx.es.enter_context(ctx.nc.semaphore(name + "_prog"))
        self.count = 0
        self.waited = {}

    def wait_tokens(self, toks):
        for key, (sem, val, ename) in toks.items():
            if ename == self.name:
                if self.name == "tensor" or not SAFE_SAME_ENGINE:
                    continue
            if self.waited.get(key, 0) >= val:
                continue
            self.eng.wait_ge(sem, val)
            self.waited[key] = val
            snap = self.ctx.snaps.get((key, val)) if PRUNE else None
            if snap is not None:
                w = self.waited
                for k2, v2 in snap.items():
                    if w.get(k2, 0) < v2:
                        w[k2] = v2


class Ctx:
    def __init__(self, nc):
        self.nc = nc
        self.es = contextlib.ExitStack()
        self.E = {}
        for name in ["tensor", "vector", "scalar", "gpsimd", "sync"]:
            self.E[name] = EngW(self, name, getattr(nc, name))
        self.nbuf = 0
        self.ninst = 0
        self.dma_bufs = []
        self.snaps = {}
        self.sem_pool = []
        self.all_slots = []
        self.rr = 0

    def buf(self, name=""):
        self.nbuf += 1
        return Buf((name or "b") + f"_{self.nbuf}")

    def _deps(self, reads, writes):
        toks = {}
        for b in list(reads) + list(writes):
            for k, v in b.w.items():
                if k not in toks or toks[k][1] < v[1]:
                    toks[k] = v
        for b in writes:
            for k, v in b.r.items():
                if k not in toks or toks[k][1] < v[1]:
                    toks[k] = v
        return toks

    def _commit(self, reads, writes, key, tok):
        for b in reads:
            if key not in b.r or b.r[key][1] < tok[1]:
                b.r[key] = tok
        for b in writes:
            b.w = {key: tok}
            b.r = {}

    def op(self, engname, method, *args, reads=(), writes=(), **kw):
        e = self.E[engname]
        e.wait_tokens(self._deps(reads, writes))
        inst = getattr(e.eng, method)(*args, **kw)
        e.count += 1
        inst.then_inc(e.sem, 1)
        self.ninst += 1
        tok = (e.sem, e.count, e.name)
        self.snaps[("E" + e.name, e.count)] = dict(e.waited)
        self._commit(reads, writes, "E" + e.name, tok)
        return tok

    def dma(self, out, in_, reads=(), writes=(), sbuf=None, queue="sync", **kw):
        e = self.E[queue]
        e.wait_tokens(self._deps(reads, writes))
        if sbuf.dsem is None:
            if self.sem_pool:
                sbuf.dsem = self.sem_pool.pop()
            else:
                h = self.es.enter_context(self.nc.semaphore(f"dsem{len(self.all_slots)}"))
                sbuf.dsem = [len(self.all_slots), h, 0]
                self.all_slots.append(sbuf.dsem)
            self.dma_bufs.append(sbuf)
        slot = sbuf.dsem
        inst = e.eng.dma_start(out=out, in_=in_, **kw)
        slot[2] += 16
        inst.then_inc(slot[1], 16)
        self.ninst += 1
        tok = (slot[1], slot[2], None)
        self.snaps[(f"S{slot[0]}", slot[2])] = dict(e.waited)
        self._commit(reads, writes, f"S{slot[0]}", tok)
        return tok

    def barrier(self):
        toks = {}
        for n, e in self.E.items():
            if e.count > 0:
                toks["E" + n] = (e.sem, e.count, "__none__")
        for slot in self.all_slots:
            if slot[2] > 0:
                toks[f"S{slot[0]}"] = (slot[1], slot[2], None)
        for n, e in self.E.items():
            t2 = {k: v for k, v in toks.items() if k != "E" + n}
            e.wait_tokens(t2)
        for b in self.dma_bufs:
            self.sem_pool.append(b.dsem)
            b.dsem = None
        self.dma_bufs = []

    def pick(self, *names):
        self.rr += 1
        return names[self.rr % len(names)]


def rel_bucket_np(dist):
    n = np.maximum(dist, 0)
    nf = np.maximum(n, 1).astype(np.float32)
    large = 16 + (np.log(nf / np.float32(16)) / np.float32(math.log(1024 / 16)) * np.float32(16)).astype(np.int32)
    return np.where(n < 16, n, np.minimum(large, 31)).astype(np.int64)


def host_tables(rel_bias, L):
    rb = np.asarray(rel_bias, dtype=np.float32)
    j = np.arange(128)[:, None]
    i = np.arange(128)[None, :]
    out = {}

    def gather(dist, valid, heads):
        bk = rel_bucket_np(dist)
        t = np.empty((128, len(heads), 128), np.float32)
        for hi, h in enumerate(heads):
            t[:, hi, :] = np.where(valid, rb[bk, h], np.float32(NEG))
        return t
    ba = np.empty((128, 2, 4, 128), np.float32)
    for dl in range(2):
        dist = i - j + 128 * dl
        valid = (dist >= 0) & (dist < 128)
        ba[:, dl] = gather(dist, valid, [0, 1, 2, 3])
    out["biasA"] = ba
    bn = np.empty((128, 8, 4, 128), np.float32)
    for dl in range(8):
        dist = i - j + 128 * dl
        bn[:, dl] = gather(dist, dist >= 0, [4, 5, 6, 7])
    out["biasN"] = bn
    dist = i - j + 128 * 4
    out["biasW4"] = gather(dist, (dist >= 0) & (dist < 512), [4, 5, 6, 7])
    bc = np.empty((128, 24, 4, 128), np.float32)
    for dl in range(24):
        dist = 128 * dl - 31 + i - 16 * j
        bc[:, dl] = gather(dist, dist >= 0, [4, 5, 6, 7])
    out["biasC"] = bc
    out["b31"] = np.broadcast_to(rb[31, 4:8][None, :], (128, 4)).copy()
    return out


def host_consts(L):
    NS = L // 64
    c = {}
    c["ident"] = np.eye(128, dtype=np.float32)
    bd = np.zeros((128, 128), np.float32)
    bd[:64, :64] = 1.0
    bd[64:, 64:] = 1.0
    c["bd64"] = bd
    w = np.zeros((128, L), np.float32)
    t = np.arange(L)
    w[t // 64, t] = 1.0
    c["wexp"] = w
    M = L // 16 - 1
    nch = (M + 127) // 128
    ov = np.zeros((128, 4, 128), np.float32)
    for cch in range(nch):
        for ml in range(128):
            m = cch * 128 + ml
            if m >= M:
                continue
            cs = m * 16
            for jb in range(NS):
                ss = jb * 64
                if cs < ss + 64 and cs + 32 > ss:
                    ov[ml, cch, jb] = 1.0
    c["overlap"] = ov
    m1 = np.ones((128, 256), np.float32)
    m2 = np.zeros((128, 256), np.float32)
    for i in range(128):
        cur_rel = 1 if i >= 64 else 0
        for x in range(256):
            rel = x - 126
            if rel == cur_rel or rel == cur_rel - 1:
                m1[i, x] = 0.0
                m2[i, x] = 1e4
            elif rel > cur_rel:
                m1[i, x] = 0.0
                m2[i, x] = -1e4
    c["m1base"] = m1
    c["m2base"] = m2
    kk = np.broadcast_to((T1 - np.arange(T1 + 1, dtype=np.float32))[None, :], (128, T1 + 1)).copy()
    c["kk"] = kk
    return c


def build(L, nlayers=DEPTH, dbg=False, phases=None, dbg_in=(), layer0=0):
    NT = L // 512
    NB = L // 128
    NS = L // 64
    MC = L // 16 - 1
    NCH = (MC + 127) // 128
    NTOP = min(16, NS)
    NC = L // T1
    SEG = min(L, 4096)
    NSEG = L // SEG
    NCS = SEG // T1
    nc = bass.Bass("TRN2", target_bir_lowering=False)
    C = Ctx(nc)
    skind = "ExternalOutput" if dbg else "Internal"

    def din(name, shape, dt=F32):
        return nc.dram_tensor(name, list(shape), dt, kind="ExternalInput").ap()

    def dscr(name, shape, dt):
        kd = "ExternalInput" if name in dbg_in else ("ExternalOutput" if name in FORCE_EXT else skind)
        return nc.dram_tensor(name, list(shape), dt, kind=kd).ap()

    x_in = din("x", [L, D])
    norm1_g = din("norm1_g", [DEPTH, D])
    w_in = din("w_in", [DEPTH, D, IN_W])
    qk_g = din("qk_g", [DEPTH, 6, HD])
    sinks = din("sinks", [DEPTH, 4])
    a_re_d = din("ssm_a_re", [DEPTH, NG, 64])
    a_im_d = din("ssm_a_im", [DEPTH, NG, 64])
    log_dt_d = din("ssm_log_dt", [DEPTH, NG])
    b_re_d = din("ssm_b_re", [DEPTH, NG, 64, 16])
    b_im_d = din("ssm_b_im", [DEPTH, NG, 64, 16])
    c_re_d = din("ssm_c_re", [DEPTH, NG, 16, 64])
    c_im_d = din("ssm_c_im", [DEPTH, NG, 16, 64])
    ssm_d_d = din("ssm_d", [DEPTH, 512])
    glu_w_d = din("glu_w", [DEPTH, 512, 1024])
    glu_b_d = din("glu_b", [DEPTH, 1024])
    cmp_pos_d = din("cmp_pos", [DEPTH, 2, 32, 64])
    cmp_w1_d = din("cmp_w1", [DEPTH, 2, 2048, 128])
    cmp_w2_d = din("cmp_w2", [DEPTH, 2, 128, 64])
    out_norm_g = din("out_norm_g", [DEPTH, D])
    w_out_d = din("w_out", [DEPTH, D, D])
    norm2_g = din("norm2_g", [DEPTH, D])
    w_up_d = din("w_up", [DEPTH, D, 4 * D])
    w_down_d = din("w_down", [DEPTH, 4 * D, D])
    c_ident = din("c_ident", [128, 128])
    c_bd64 = din("c_bd64", [128, 128])
    c_wexp = din("c_wexp", [128, L])
    c_overlap = din("c_overlap", [128, 4, 128])
    c_m1 = din("c_m1base", [128, 256])
    c_m2 = din("c_m2base", [128, 256])
    c_kk = din("c_kk", [128, T1 + 1])
    t_biasA = din("t_biasA", [128, 2, 4, 128])
    t_biasN = din("t_biasN", [128, 8, 4, 128])
    t_biasW4 = din("t_biasW4", [128, 4, 128])
    t_biasC = din("t_biasC", [128, 24, 4, 128])
    t_b31 = din("t_b31", [128, 4])

    y_out = nc.dram_tensor("y", [L, D], F32, kind="ExternalOutput").ap()

    xT = dscr("xT", [D, L], F32)
    fm = dscr("fm", [8, 128, L], BF16)
    utm = dscr("utm", [L, 512], BF16)
    vtm = dscr("vtm", [L, 260], BF16)
    gtm = dscr("gtm", [L, 12], F32)
    mixT = dscr("mixT", [D, L], BF16)
    kr = dscr("kr", [NG, 16, 63 * 16], BF16)
    xT_b = [C.buf(f"xT{i}") for i in range(NT)]
    fm_b = [C.buf(f"fm{i}") for i in range(NT)]
    utm_b = [C.buf(f"utm{i}") for i in range(NT)]
    vtm_b = [C.buf(f"vtm{i}") for i in range(NT)]
    gtm_b = [C.buf(f"gtm{i}") for i in range(NT)]
    mixa_b = [C.buf(f"mixa{i}") for i in range(NT)]
    mixb_b = [C.buf(f"mixb{i}") for i in range(NT)]
    kr_b = [C.buf(f"kr{g}") for g in range(NG)]
    y_b = C.buf("y")
    inb = C.buf("inputs")

    def want(p):
        return phases is None or p in phases

    with C.es:
        es = C.es
        PS = [es.enter_context(nc.psum_tensor(f"ps{i}", [128, 512], F32)) for i in range(8)]
        PSB = [C.buf(f"ps{i}") for i in range(8)]

        uniq = [0]

        def sb(stack, name, shape, dt):
            uniq[0] += 1
            return stack.enter_context(nc.sbuf_tensor(f"{name}_{uniq[0]}", list(shape), dt))

        ident = sb(es, "ident", [128, 128], F32)
        identb = sb(es, "identb", [128, 128], BF16)
        onesb = sb(es, "onesb", [128, 128], BF16)
        bd64b = sb(es, "bd64b", [128, 128], BF16)
        gstage = sb(es, "gstage", [128, 128], F32)
        b_ident, b_identb, b_onesb, b_bd64b, b_gstage = (C.buf(n) for n in ["ident", "identb", "onesb", "bd64b", "gstage"])
        C.dma(ident[:], c_ident[:, :], reads=[inb], writes=[b_ident], sbuf=b_ident)
        C.op("vector", "tensor_copy", out=identb[:], in_=ident[:], reads=[b_ident], writes=[b_identb])
        C.op("vector", "memset", onesb[:], 1.0, writes=[b_onesb])
        C.dma(gstage[:], c_bd64[:, :], reads=[inb], writes=[b_gstage], sbuf=b_gstage)
        C.op("vector", "tensor_copy", out=bd64b[:], in_=gstage[:], reads=[b_gstage], writes=[b_bd64b])

        def rstd_from(stack_unused, out_ap, in_ap, scale, tmp_ap, eng_reads, bout, btmp):
            C.op("scalar", "activation", out=tmp_ap, in_=in_ap, func=AF.Ln, bias=EPS, scale=scale,
                 reads=eng_reads, writes=[btmp])
            C.op("scalar", "activation", out=out_ap, in_=tmp_ap, func=AF.Exp, scale=-0.5,
                 reads=[btmp], writes=[bout])

        def load_cols(stack, name, dram_vec, n):
            t = sb(stack, name, [128, n], F32)
            b = C.buf(name)
            C.dma(t[:], dram_vec.rearrange("(k p) -> p k", p=128), reads=[inb], writes=[b], sbuf=b,
                  allow_slow_non_contiguous=True)
            return t, b

        if want("p0"):
            with contextlib.ExitStack() as ph:
                xin = [sb(ph, f"p0_xin{i}", [128, D], F32) for i in range(2)]
                xts = [sb(ph, f"p0_xts{i}", [128, 8, 128], F32) for i in range(2)]
                b_xin = [C.buf(f"p0xin{i}") for i in range(2)]
                b_xts = [C.buf(f"p0xts{i}") for i in range(2)]
                for tb in range(NB):
                    s = tb % 2
                    C.dma(xin[s][:], x_in[tb * 128:(tb + 1) * 128, :], reads=[inb], writes=[b_xin[s]], sbuf=b_xin[s])
                    for half in range(2):
                        pb = half
                        for k in range(4):
                            C.op("tensor", "transpose", PS[pb][:, k * 128:(k + 1) * 128],
                                 xin[s][:, (half * 4 + k) * 128:(half * 4 + k + 1) * 128], ident[:, :],
                                 reads=[b_xin[s], b_ident], writes=[PSB[pb]])
                        C.op("vector", "tensor_copy", out=xts[s][:, half * 4:(half + 1) * 4, :],
                             in_=PS[pb][:, :].rearrange("p (k t) -> p k t", k=4),
                             reads=[PSB[pb]], writes=[b_xts[s]])
                    C.dma(xT.rearrange("(k p) t -> p k t", p=128)[:, :, tb * 128:(tb + 1) * 128], xts[s][:],
                          reads=[b_xts[s]], writes=[xT_b[tb // 4]], sbuf=b_xts[s])
                C.barrier()

        zk = sb(es, "zk", [16, 1008], BF16)
        b_zk = C.buf("zk")
        C.op("vector", "memset", zk[:], 0.0, writes=[b_zk])
        for g in range(NG):
            C.dma(kr[g], zk[:], reads=[b_zk], writes=[kr_b[g]], sbuf=b_zk)

        for l in range(layer0, layer0 + nlayers):
            last = (l == layer0 + nlayers - 1)
            if want("p1"):
                with contextlib.ExitStack() as ph:
                    stage = sb(ph, "p1_stage", [128, 8, IN_W], F32)
                    b_stage = C.buf("p1stage")
                    wb = sb(ph, "p1_wb", [128, 8, 1804], BF16)
                    b_wb = C.buf("p1wb")
                    C.dma(stage[:], w_in[l].rearrange("(k p) c -> p k c", p=128), reads=[inb], writes=[b_stage], sbuf=b_stage)
                    g1, b_g1 = load_cols(ph, "p1_g1", norm1_g[l], 8)
                    groups = [(0, 0, 64), (64, 128, 64), (128, 64, 64), (192, 192, 64), (256, 256, 128),
                              (384, 1024, 128), (512, 1152, 128), (640, 1280, 128),
                              (768, 1408, 64), (832, 1408, 64), (896, 1536, 64), (960, 1536, 64),
                              (1024, 512, 512),
                              (1536, 384, 128), (1664, 1472, 64), (1728, 1600, 64), (1792, 1664, 12)]
                    for (dd, ss, n) in groups:
                        C.op(C.pick("vector", "gpsimd"), "tensor_tensor", out=wb[:, :, dd:dd + n], in0=stage[:, :, ss:ss + n],
                             in1=g1[:, :].unsqueeze(2).broadcast_to([128, 8, n]), op=ALU.mult,
                             reads=[b_stage, b_g1], writes=[b_wb])
                    gq = sb(ph, "p1_gq", [128, 6], F32)
                    b_gq = C.buf("p1gq")
                    for half in range(2):
                        C.dma(gq[half * 64:(half + 1) * 64, :], qk_g[l].rearrange("s d -> d s"), reads=[inb], writes=[b_gq],
                              sbuf=b_gq, allow_slow_non_contiguous=True)
                    gcol = sb(ph, "p1_gcol", [128, 8], F32)
                    b_gcol = C.buf("p1gcol")
                    C.op("vector", "memset", gcol[:], 1.0, writes=[b_gcol])
                    for (cc, sidx, scl) in [(0, 0, 0.125), (1, 0, 0.125), (2, 1, 1.0), (3, 2, 0.125), (4, 2, 0.125), (6, 4, 1.0), (7, 5, 1.0)]:
                        C.op("vector", "tensor_scalar", out=gcol[:, cc:cc + 1], in0=gq[:, sidx:sidx + 1], scalar1=scl, scalar2=None,
                             op0=ALU.mult, reads=[b_gq], writes=[b_gcol])
                    xt = [sb(ph, f"p1_xt{i}", [128, 8, 512], F32) for i in range(2)]
                    b_xt = [C.buf(f"p1xt{i}") for i in range(2)]
                    xb = [sb(ph, f"p1_xb{i}", [128, 8, 512], BF16) for i in range(2)]
                    b_xb = [C.buf(f"p1xb{i}") for i in range(2)]
                    sq = [sb(ph, f"p1_sq{i}", [128, 8, 512], BF16) for i in range(2)]
                    b_sq = [C.buf(f"p1sq{i}") for i in range(2)]
                    lnv = sb(ph, "p1_lnv", [128, 512], F32)
                    b_lnv = C.buf("p1lnv")
                    rstdb = [sb(ph, f"p1_rstdb{i}", [128, 512], F32) for i in range(2)]
                    b_rstdb = [C.buf(f"p1rstdb{i}") for i in range(2)]
                    rtm = [sb(ph, f"p1_rtm{i}", [128, 4], F32) for i in range(2)]
                    b_rtm = [C.buf(f"p1rtm{i}") for i in range(2)]
                    rtmp = sb(ph, "p1_rtmp", [128, 4], F32)
                    b_rtmp = C.buf("p1rtmp")
                    ysb = [sb(ph, f"p1_ysb{i}", [128, 512], F32) for i in range(2)]
                    b_ysb = [C.buf(f"p1ysb{i}") for i in range(2)]
                    sq2 = [sb(ph, f"p1_sq2{i}", [128, 512], BF16) for i in range(2)]
                    b_sq2 = [C.buf(f"p1sq2{i}") for i in range(2)]
                    lnv2 = [sb(ph, f"p1_lnv2{i}", [128, 512], F32) for i in range(2)]
                    b_lnv2 = [C.buf(f"p1lnv2{i}") for i in range(2)]
                    r2 = [sb(ph, f"p1_r2{i}", [128, 512], F32) for i in range(2)]
                    b_r2 = [C.buf(f"p1r2{i}") for i in range(2)]
                    fst = [sb(ph, f"p1_fst{i}", [128, 8, 512], BF16) for i in range(2)]
                    b_fst = [C.buf(f"p1fst{i}") for i in range(2)]
                    ust = [sb(ph, f"p1_ust{i}", [128, 4, 512], BF16) for i in range(2)]
                    b_ust = [C.buf(f"p1ust{i}") for i in range(2)]
                    vst = [sb(ph, f"p1_vst{i}", [128, 4, 4, 65], BF16) for i in range(2)]
                    b_vst = [C.buf(f"p1vst{i}") for i in range(2)]
                    gst = [sb(ph, f"p1_gst{i}", [128, 4, 12], F32) for i in range(2)]
                    b_gst = [C.buf(f"p1gst{i}") for i in range(2)]
                    for i in range(2):
                        C.op("vector", "memset", vst[i][:], 1.0, writes=[b_vst[i]])

                    def p1_load(i):
                        s = i % 2
                        C.dma(xt[s][:], xT.rearrange("(k p) t -> p k t", p=128)[:, :, i * 512:(i + 1) * 512],
                              reads=[xT_b[i]], writes=[b_xt[s]], sbuf=b_xt[s])
                    p1_load(0)
                    pcount = 0
                    for i in range(NT):
                        s = i % 2
                        if i + 1 < NT:
                            p1_load(i + 1)
                        C.op("scalar", "activation", out=sq[s][:], in_=xt[s][:], func=AF.Square, reads=[b_xt[s]], writes=[b_sq[s]])
                        C.op("gpsimd", "tensor_copy", out=xb[s][:], in_=xt[s][:], reads=[b_xt[s]], writes=[b_xb[s]])
                        for k in range(8):
                            C.op("tensor", "matmul", PS[0][:, :], lhsT=onesb[:, :], rhs=sq[s][:, k, :], start=(k == 0), stop=(k == 7),
                                 reads=[b_onesb, b_sq[s]], writes=[PSB[0]])
                        rstd_from(None, rstdb[s][:], PS[0][:, :], 1.0 / D, lnv[:], [PSB[0]], b_rstdb[s], b_lnv)
                        for st in range(4):
                            for k in range(8):
                                C.op("tensor", "matmul", PS[1][:, st:st + 1], lhsT=sq[s][:, k, st * 128:(st + 1) * 128], rhs=onesb[:, 0:1],
                                     start=(k == 0), stop=(k == 7), reads=[b_onesb, b_sq[s]], writes=[PSB[1]])
                        rstd_from(None, rtm[s][:], PS[1][:, 0:4], 1.0 / D, rtmp[:], [PSB[1]], b_rtm[s], b_rtmp)
                        for cidx in range(8):
                            pb = 2 + (pcount % 2)
                            pcount += 1
                            for k in range(8):
                                C.op("tensor", "matmul", PS[pb][:, :], lhsT=wb[:, k, cidx * 128:(cidx + 1) * 128], rhs=xb[s][:, k, :],
                                     start=(k == 0), stop=(k == 7), reads=[b_wb, b_xb[s]], writes=[PSB[pb]])
                            if cidx == 5:
                                C.op("vector", "tensor_tensor", out=fst[s][:, cidx, :], in0=PS[pb][:, :], in1=rstdb[s][:], op=ALU.mult,
                                     reads=[PSB[pb], b_rstdb[s]], writes=[b_fst[s]])
                                continue
                            q = cidx % 2
                            C.op("vector", "tensor_tensor", out=ysb[q][:], in0=PS[pb][:, :], in1=rstdb[s][:], op=ALU.mult,
                                 reads=[PSB[pb], b_rstdb[s]], writes=[b_ysb[q]])
                            C.op("scalar", "activation", out=sq2[q][:], in_=ysb[q][:], func=AF.Square, reads=[b_ysb[q]], writes=[b_sq2[q]])
                            pb2 = 4 + q
                            C.op("tensor", "matmul", PS[pb2][:, :], lhsT=bd64b[:, :], rhs=sq2[q][:], start=True, stop=True,
                                 reads=[b_bd64b, b_sq2[q]], writes=[PSB[pb2]])
                            rstd_from(None, r2[q][:], PS[pb2][:, :], 1.0 / HD, lnv2[q][:], [PSB[pb2]], b_r2[q], b_lnv2[q])
                            C.op("vector", "scalar_tensor_tensor", out=fst[s][:, cidx, :], in0=ysb[q][:], scalar=gcol[:, cidx:cidx + 1],
                                 in1=r2[q][:], op0=ALU.mult, op1=ALU.mult, reads=[b_ysb[q], b_gcol, b_r2[q]], writes=[b_fst[s]])
                        C.dma(fm.rearrange("c p t -> p c t")[:, :, i * 512:(i + 1) * 512], fst[s][:], reads=[b_fst[s]], writes=[fm_b[i]],
                              sbuf=b_fst[s])
                        for st in range(4):
                            for k in range(8):
                                C.op("tensor", "matmul", PS[6][:, :], lhsT=xb[s][:, k, st * 128:(st + 1) * 128], rhs=wb[:, k, 1024:1536],
                                     start=(k == 0), stop=(k == 7), reads=[b_wb, b_xb[s]], writes=[PSB[6]])
                            for k in range(8):
                                C.op("tensor", "matmul", PS[7][:, 0:268], lhsT=xb[s][:, k, st * 128:(st + 1) * 128], rhs=wb[:, k, 1536:1804],
                                     start=(k == 0), stop=(k == 7), reads=[b_wb, b_xb[s]], writes=[PSB[7]])
                            C.op("vector", "tensor_scalar", out=ust[s][:, st, :], in0=PS[6][:, :], scalar1=rtm[s][:, st:st + 1], scalar2=None,
                                 op0=ALU.mult, reads=[PSB[6], b_rtm[s]], writes=[b_ust[s]])
                            C.op("vector", "tensor_scalar", out=vst[s][:, st, :, 0:64], in0=PS[7][:, 0:256].rearrange("p (a b) -> p a b", a=4),
                                 scalar1=rtm[s][:, st:st + 1], scalar2=None, op0=ALU.mult, reads=[PSB[7], b_rtm[s]], writes=[b_vst[s]])
                            C.op("scalar", "activation", out=gst[s][:, st, :], in_=PS[7][:, 256:268], func=AF.Sigmoid, scale=rtm[s][:, st:st + 1],
                                 reads=[PSB[7], b_rtm[s]], writes=[b_gst[s]])
                        C.dma(utm.rearrange("(n s p) c -> n p s c", s=4, p=128)[i], ust[s][:], reads=[b_ust[s]], writes=[utm_b[i]], sbuf=b_ust[s])
                        C.dma(vtm.rearrange("(n s p) c -> n p s c", s=4, p=128)[i], vst[s][:].rearrange("p s a b -> p s (a b)"),
                              reads=[b_vst[s]], writes=[vtm_b[i]], sbuf=b_vst[s])
                        C.dma(gtm.rearrange("(n s p) c -> n p s c", s=4, p=128)[i], gst[s][:], reads=[b_gst[s]], writes=[gtm_b[i]], sbuf=b_gst[s])
                    C.barrier()

            if want("p2"):
                with contextlib.ExitStack() as ph:
                    def T(name, shape, dt=F32, stack=None):
                        return sb(stack or ph, "p2_" + name, shape, dt), C.buf("p2" + name)
                    PI = math.pi
                    K33 = T1 + 1
                    AAr, b_AAr = T("AAr", [128, NG, K33])
                    AAi, b_AAi = T("AAi", [128, NG, K33])
                    BBb1, b_BBb1 = T("BBb1", [128, NG, 16])
                    BBb2, b_BBb2 = T("BBb2", [128, NG, 16])
                    BBb1b, b_BBb1b = T("BBb1b", [128, NG, 16], BF16)
                    CC1, b_CC1 = T("CC1", [128, NG, 16])
                    CC2, b_CC2 = T("CC2", [128, NG, 16])
                    PP1, b_PP1 = T("PP1", [128, 64])
                    PP2, b_PP2 = T("PP2", [128, 64])
                    dsk, b_dsk = T("dsk", [16, NG])
                    cur, b_cur = T("cur", [128, 96])
                    gw, b_gw = T("gw", [128, 4, 1024], BF16)
                    gbc, b_gbc = load_cols(ph, "p2_gbc", glu_b_d[l], 8)
                    with contextlib.ExitStack() as pp:
                        ain, b_ain = T("ain", [32, 2, 128], stack=pp)
                        for q_, src in enumerate([a_re_d, a_im_d]):
                            for hh in range(2):
                                C.dma(ain[:, q_, hh * 64:(hh + 1) * 64], src[l], reads=[inb], writes=[b_ain], sbuf=b_ain)
                        for q_ in range(2):
                            C.op("tensor", "transpose", PS[0][:, q_ * 32:(q_ + 1) * 32], ain[:, q_, :], ident[0:32, 0:32], reads=[b_ain, b_ident], writes=[PSB[0]])
                        aT, b_aT = T("aT", [128, 2, NG], stack=pp)
                        C.op("vector", "tensor_copy", out=aT[:].rearrange("p a g -> p (a g)"), in_=PS[0][:, 0:64], reads=[PSB[0]], writes=[b_aT])
                        dtb, b_dtb = T("dtb", [128, NG], stack=pp)
                        C.dma(dtb[:], log_dt_d[l].partition_broadcast(128), reads=[inb], writes=[b_dtb], sbuf=b_dtb)
                        C.op("scalar", "activation", out=dtb[:], in_=dtb[:], func=AF.Exp, reads=[b_dtb], writes=[b_dtb])
                        rho, b_rho = T("rho", [128, NG], stack=pp)
                        th, b_th = T("th", [128, NG], stack=pp)
                        C.op("vector", "tensor_tensor", out=rho[:], in0=aT[:, 0, :], in1=dtb[:], op=ALU.mult, reads=[b_aT, b_dtb], writes=[b_rho])
                        C.op("vector", "tensor_tensor", out=th[:], in0=aT[:, 1, :], in1=dtb[:], op=ALU.mult, reads=[b_aT, b_dtb], writes=[b_th])
                        kkt, b_kkt = T("kk", [128, K33], stack=pp)
                        C.dma(kkt[:], c_kk[:, :], reads=[inb], writes=[b_kkt], sbuf=b_kkt)
                        NE = NG * K33
                        ang, b_ang = T("ang", [128, NG, K33], stack=pp)
                        mag, b_mag = T("mag", [128, NG, K33], stack=pp)
                        w1_, b_w1_ = T("w1_", [128, NE], stack=pp)
                        w2_, b_w2_ = T("w2_", [128, NE], stack=pp)
                        wi_, b_wi_ = T("wi_", [128, NE], I32, stack=pp)
                        sn, b_sn = T("sn", [128, NG, K33], stack=pp)
                        cs, b_cs = T("cs", [128, NG, K33], stack=pp)
                        kb = kkt[:, :].unsqueeze(1).broadcast_to([128, NG, K33])
                        C.op("vector", "tensor_tensor", out=ang[:], in0=th[:, :].unsqueeze(2).broadcast_to([128, NG, K33]), in1=kb, op=ALU.mult,
                             reads=[b_th, b_kkt], writes=[b_ang])
                        C.op("vector", "tensor_tensor", out=mag[:], in0=rho[:, :].unsqueeze(2).broadcast_to([128, NG, K33]), in1=kb, op=ALU.mult,
                             reads=[b_rho, b_kkt], writes=[b_mag])
                        C.op("scalar", "activation", out=mag[:], in_=mag[:], func=AF.Exp, reads=[b_mag], writes=[b_mag])
                        angf = ang[:].rearrange("p g k -> p (g k)")
                        C.op("vector", "tensor_scalar", out=w1_[:], in0=angf, scalar1=1.0 / TWO_PI, scalar2=None, op0=ALU.mult, reads=[b_ang], writes=[b_w1_])
                        C.op("vector", "tensor_copy", out=wi_[:], in_=w1_[:], reads=[b_w1_], writes=[b_wi_])
                        C.op("vector", "tensor_copy", out=w1_[:], in_=wi_[:], reads=[b_wi_], writes=[b_w1_])
                        C.op("vector", "scalar_tensor_tensor", out=w1_[:], in0=w1_[:], scalar=-TWO_PI, in1=angf, op0=ALU.mult, op1=ALU.add,
                             reads=[b_w1_, b_ang], writes=[b_w1_])

                        def fold(t_, bt_):
                            C.op("vector", "tensor_scalar", out=w2_[:], in0=t_, scalar1=PI, scalar2=None, op0=ALU.is_gt, reads=[bt_], writes=[b_w2_])
                            C.op("vector", "scalar_tensor_tensor", out=t_, in0=w2_[:], scalar=-TWO_PI, in1=t_, op0=ALU.mult, op1=ALU.add, reads=[b_w2_, bt_], writes=[bt_])
                            C.op("vector", "tensor_scalar", out=w2_[:], in0=t_, scalar1=-PI, scalar2=None, op0=ALU.is_lt, reads=[bt_], writes=[b_w2_])
                            C.op("vector", "scalar_tensor_tensor", out=t_, in0=w2_[:], scalar=TWO_PI, in1=t_, op0=ALU.mult, op1=ALU.add, reads=[b_w2_, bt_], writes=[bt_])
                        fold(w1_[:], b_w1_)
                        C.op("scalar", "activation", out=sn[:].rearrange("p g k -> p (g k)"), in_=w1_[:], func=AF.Sin, reads=[b_w1_], writes=[b_sn])
                        C.op("vector", "tensor_scalar", out=w1_[:], in0=w1_[:], scalar1=PI / 2, scalar2=None, op0=ALU.add, reads=[b_w1_], writes=[b_w1_])
                        fold(w1_[:], b_w1_)
                        C.op("scalar", "activation", out=cs[:].rearrange("p g k -> p (g k)"), in_=w1_[:], func=AF.Sin, reads=[b_w1_], writes=[b_cs])
                        C.op("vector", "tensor_tensor", out=AAr[:], in0=mag[:], in1=cs[:], op=ALU.mult, reads=[b_mag, b_cs], writes=[b_AAr])
                        C.op("vector", "tensor_tensor", out=AAi[:], in0=mag[:], in1=sn[:], op=ALU.mult, reads=[b_mag, b_sn], writes=[b_AAi])
                        nre, b_nre = T("nre", [128, NG], stack=pp)
                        den_, b_den_ = T("den", [128, NG], stack=pp)
                        tq, b_tq = T("tq", [128, NG], stack=pp)
                        cr, b_cr = T("cr", [128, NG], stack=pp)
                        ci, b_ci = T("ci", [128, NG], stack=pp)
                        C.op("vector", "tensor_scalar", out=nre[:], in0=AAr[:, :, T1 - 1], scalar1=-1.0, scalar2=None, op0=ALU.add, reads=[b_AAr], writes=[b_nre])
                        C.op("vector", "tensor_tensor", out=den_[:], in0=aT[:, 0, :], in1=aT[:, 0, :], op=ALU.mult, reads=[b_aT], writes=[b_den_])
                        C.op("vector", "tensor_tensor", out=tq[:], in0=aT[:, 1, :], in1=aT[:, 1, :], op=ALU.mult, reads=[b_aT], writes=[b_tq])
                        C.op("vector", "tensor_tensor", out=den_[:], in0=den_[:], in1=tq[:], op=ALU.add, reads=[b_den_, b_tq], writes=[b_den_])
                        C.op("vector", "reciprocal", out=den_[:], in_=den_[:], reads=[b_den_], writes=[b_den_])
                        C.op("vector", "tensor_tensor", out=cr[:], in0=nre[:], in1=aT[:, 0, :], op=ALU.mult, reads=[b_nre, b_aT], writes=[b_cr])
                        C.op("vector", "tensor_tensor", out=tq[:], in0=AAi[:, :, T1 - 1], in1=aT[:, 1, :], op=ALU.mult, reads=[b_AAi, b_aT], writes=[b_tq])
                        C.op("vector", "tensor_tensor", out=cr[:], in0=cr[:], in1=tq[:], op=ALU.add, reads=[b_cr, b_tq], writes=[b_cr])
                        C.op("vector", "tensor_tensor", out=cr[:], in0=cr[:], in1=den_[:], op=ALU.mult, reads=[b_cr, b_den_], writes=[b_cr])
                        C.op("vector", "tensor_tensor", out=ci[:], in0=AAi[:, :, T1 - 1], in1=aT[:, 0, :], op=ALU.mult, reads=[b_AAi, b_aT], writes=[b_ci])
                        C.op("vector", "tensor_tensor", out=tq[:], in0=nre[:], in1=aT[:, 1, :], op=ALU.mult, reads=[b_nre, b_aT], writes=[b_tq])
                        C.op("vector", "tensor_tensor", out=ci[:], in0=ci[:], in1=tq[:], op=ALU.subtract, reads=[b_ci, b_tq], writes=[b_ci])
                        C.op("vector", "tensor_tensor", out=ci[:], in0=ci[:], in1=den_[:], op=ALU.mult, reads=[b_ci, b_den_], writes=[b_ci])
                        BB1, b_BB1 = T("BB1", [128, NG, 16], stack=pp)
                        BB2, b_BB2 = T("BB2", [128, NG, 16], stack=pp)
                        C.dma(BB1[0:64], b_re_d[l].rearrange("g n p -> n g p"), reads=[inb], writes=[b_BB1], sbuf=b_BB1)
                        C.dma(BB1[64:128], b_im_d[l].rearrange("g n p -> n g p"), reads=[inb], writes=[b_BB1], sbuf=b_BB1)
                        C.dma(BB2[0:64], b_im_d[l].rearrange("g n p -> n g p"), reads=[inb], writes=[b_BB2], sbuf=b_BB2)
                        C.dma(BB2[64:128], b_re_d[l].rearrange("g n p -> n g p"), reads=[inb], writes=[b_BB2], sbuf=b_BB2)
                        C.op("vector", "tensor_scalar", out=BB2[0:64], in0=BB2[0:64], scalar1=-1.0, scalar2=None, op0=ALU.mult, reads=[b_BB2], writes=[b_BB2])
                        tb1, b_tb1 = T("tb1", [128, NG, 16], stack=pp)
                        crb = cr[:, :].unsqueeze(2).broadcast_to([128, NG, 16])
                        cib = ci[:, :].unsqueeze(2).broadcast_to([128, NG, 16])
                        C.op("vector", "tensor_tensor", out=BBb1[:], in0=BB1[:], in1=crb, op=ALU.mult, reads=[b_BB1, b_cr], writes=[b_BBb1])
                        C.op("vector", "tensor_tensor", out=tb1[:], in0=BB2[:], in1=cib, op=ALU.mult, reads=[b_BB2, b_ci], writes=[b_tb1])
                        C.op("vector", "tensor_tensor", out=BBb1[:], in0=BBb1[:], in1=tb1[:], op=ALU.add, reads=[b_BBb1, b_tb1], writes=[b_BBb1])
                        C.op("vector", "tensor_tensor", out=BBb2[:], in0=BB2[:], in1=crb, op=ALU.mult, reads=[b_BB2, b_cr], writes=[b_BBb2])
                        C.op("vector", "tensor_tensor", out=tb1[:], in0=BB1[:], in1=cib, op=ALU.mult, reads=[b_BB1, b_ci], writes=[b_tb1])
                        C.op("vector", "tensor_tensor", out=BBb2[:], in0=BBb2[:], in1=tb1[:], op=ALU.subtract, reads=[b_BBb2, b_tb1], writes=[b_BBb2])
                        C.op("vector", "tensor_copy", out=BBb1b[:], in_=BBb1[:], reads=[b_BBb1], writes=[b_BBb1b])
                        cin, b_cin = T("cin", [128, 4, 128], stack=pp)
                        for v_, (s0, s1) in enumerate([(c_re_d, c_im_d), (c_im_d, c_re_d)]):
                            C.dma(cin[:, :, 0:64], s0[l].rearrange("g p n -> (g p) n").rearrange("(q r) n -> r q n", r=128), reads=[inb], writes=[b_cin], sbuf=b_cin)
                            C.dma(cin[:, :, 64:128], s1[l].rearrange("g p n -> (g p) n").rearrange("(q r) n -> r q n", r=128), reads=[inb], writes=[b_cin], sbuf=b_cin)
                            for q_ in range(4):
                                C.op("tensor", "transpose", PS[1][:, q_ * 128:(q_ + 1) * 128], cin[:, q_, :], ident[:, :], reads=[b_cin, b_ident], writes=[PSB[1]])
                            if v_ == 0:
                                C.op("vector", "tensor_copy", out=CC1[0:64].rearrange("p g c -> p (g c)"), in_=PS[1][0:64, :], reads=[PSB[1]], writes=[b_CC1])
                                C.op("vector", "tensor_scalar", out=CC1[64:128].rearrange("p g c -> p (g c)"), in0=PS[1][64:128, :], scalar1=-1.0, scalar2=None,
                                     op0=ALU.mult, reads=[PSB[1]], writes=[b_CC1])
                            else:
                                C.op("vector", "tensor_scalar", out=CC2[:].rearrange("p g c -> p (g c)"), in0=PS[1][:, :], scalar1=-1.0, scalar2=None,
                                     op0=ALU.mult, reads=[PSB[1]], writes=[b_CC2])
                        C.op("vector", "tensor_copy", out=PP1[:, 0:32], in_=AAr[:, :, 0], reads=[b_AAr], writes=[b_PP1])
                        C.op("vector", "tensor_copy", out=PP1[:, 32:64], in_=AAr[:, :, 0], reads=[b_AAr], writes=[b_PP1])
                        C.op("vector", "tensor_scalar", out=PP2[0:64, 0:32], in0=AAi[0:64, :, 0], scalar1=-1.0, scalar2=None, op0=ALU.mult, reads=[b_AAi], writes=[b_PP2])
                        C.op("vector", "tensor_copy", out=PP2[64:128, 0:32], in_=AAi[64:128, :, 0], reads=[b_AAi], writes=[b_PP2])
                        C.op("vector", "tensor_scalar", out=PP2[:, 32:64], in0=PP2[:, 0:32], scalar1=-1.0, scalar2=None, op0=ALU.mult, reads=[b_PP2], writes=[b_PP2])
                        C.dma(dsk[:], ssm_d_d[l].rearrange("(g p) -> p g", p=16), reads=[inb], writes=[b_dsk], sbuf=b_dsk, allow_slow_non_contiguous=True)
                        gws, b_gws = T("gws", [128, 1024], stack=pp)
                        for k in range(4):
                            C.dma(gws[:], glu_w_d[l][k * 128:(k + 1) * 128, :], reads=[inb], writes=[b_gws], sbuf=b_gws)
                            C.op("vector", "tensor_copy", out=gw[:, k, :], in_=gws[:], reads=[b_gws], writes=[b_gw])
                        C.barrier()
                    C.op("vector", "memset", cur[:], 0.0, writes=[b_cur])
                    PSb6 = PS[6][:, :].bitcast(BF16)
                    PSb7 = PS[7][:, :].bitcast(BF16)
                    for seg in range(NSEG):
                        with contextlib.ExitStack() as sg:
                            U, b_U = T("U", [128, NG, 4, NCS], BF16, stack=sg)
                            Sb, b_Sb = T("Sb", [128, NG, NCS], BF16, stack=sg)
                            with contextlib.ExitStack() as s1:
                                usb, b_usb = T("usb", [128, T1, 512], BF16, stack=s1)
                                C.dma(usb[0:NCS].rearrange("c j h -> c (j h)"), utm.rearrange("(c j) h -> c (j h)", j=T1)[seg * NCS:(seg + 1) * NCS, :],
                                      reads=utm_b, writes=[b_usb], sbuf=b_usb)
                                ug = [T(f"ug{i}", [128, T1, 16], BF16, stack=s1) for i in range(2)]
                                for g in range(NG):
                                    pv_, pbb = (PSb6, PSB[6]) if g % 2 == 0 else (PSb7, PSB[7])
                                    ugt, b_ugt = ug[g % 2]
                                    C.op("gpsimd" if g % 2 == 0 else "vector", "tensor_copy", out=ugt[0:NCS], in_=usb[0:NCS, :, 16 * g:16 * g + 16],
                                         reads=[b_usb], writes=[b_ugt])
                                    for s_ in range(4):
                                        C.op("tensor", "transpose", pv_[:, s_ * 128:s_ * 128 + NCS], ugt[0:NCS, 8 * s_:8 * s_ + 8, :],
                                             identb[0:NCS, 0:NCS], reads=[b_ugt, b_identb], writes=[pbb])
                                    C.op("vector" if g % 2 == 0 else "scalar", "tensor_copy" if g % 2 == 0 else "copy", out=U[:, g, :, :],
                                         in_=pv_[:, 0:512].rearrange("p (s c) -> p s c", s=4)[:, :, 0:NCS], reads=[pbb], writes=[b_U])
                                C.barrier()
                            with contextlib.ExitStack() as s2:
                                XX, b_XX = T("XX", [128, NCS, 64], stack=s2)
                                MinT = [T(f"MinT{i}", [128, T1, 16], stack=s2) for i in range(2)]
                                Mt2 = [T(f"Mt2{i}", [128, T1, 16], stack=s2) for i in range(2)]
                                MinG = [T(f"MinG{i}", [128, 4, 256], BF16, stack=s2) for i in range(2)]
                                for g in range(NG):
                                    q_ = g % 2
                                    mt, b_mt = MinT[q_]
                                    m2, b_m2 = Mt2[q_]
                                    mg, b_mg = MinG[q_]
                                    C.op("vector", "tensor_tensor", out=mt[:], in0=AAr[:, g, 1:K33].unsqueeze(2).broadcast_to([128, T1, 16]),
                                         in1=BBb1[:, g, :].unsqueeze(1).broadcast_to([128, T1, 16]), op=ALU.mult, reads=[b_AAr, b_BBb1], writes=[b_mt])
                                    C.op("gpsimd", "tensor_tensor", out=m2[:], in0=AAi[:, g, 1:K33].unsqueeze(2).broadcast_to([128, T1, 16]),
                                         in1=BBb2[:, g, :].unsqueeze(1).broadcast_to([128, T1, 16]), op=ALU.mult, reads=[b_AAi, b_BBb2], writes=[b_m2])
                                    C.op("vector", "tensor_tensor", out=mt[:], in0=mt[:], in1=m2[:], op=ALU.add, reads=[b_mt, b_m2], writes=[b_mt])
                                    pb = 2 + q_
                                    for s_ in range(4):
                                        C.op("tensor", "transpose", PS[pb][:, s_ * 128:(s_ + 1) * 128], mt[:, 8 * s_:8 * s_ + 8, :], ident[:, :],
                                             reads=[b_mt, b_ident], writes=[PSB[pb]])
                                    pview = PS[pb][:, :].rearrange("p (s n) -> p s n", s=4)
                                    C.op("vector", "tensor_copy", out=mg[:, :, 0:128], in_=pview, reads=[PSB[pb]], writes=[b_mg])
                                    C.op("scalar", "copy", out=mg[:, :, 128:192], in_=pview[:, :, 64:128], reads=[PSB[pb]], writes=[b_mg])
                                    C.op("scalar", "copy", out=mg[:, :, 192:256], in_=pview[:, :, 0:64], reads=[PSB[pb]], writes=[b_mg])
                                    pb2 = 4 + q_
                                    for hf in range(2):
                                        for s_ in range(4):
                                            C.op("tensor", "matmul", PS[pb2][:, hf * NCS:(hf + 1) * NCS], lhsT=mg[:, s_, hf * 128:(hf + 1) * 128], rhs=U[:, g, s_, :],
                                                 start=(hf == 0 and s_ == 0), stop=(hf == 1 and s_ == 3), reads=[b_mg, b_U], writes=[PSB[pb2]])
                                    C.op("vector", "tensor_copy", out=XX[:, :, g], in_=PS[pb2][:, 0:NCS], reads=[PSB[pb2]], writes=[b_XX])
                                    C.op("scalar", "copy", out=XX[:, :, 32 + g], in_=PS[pb2][:, NCS:2 * NCS], reads=[PSB[pb2]], writes=[b_XX])
                                st1, b_st1 = T("st1", [128, 64], stack=s2)
                                st2, b_st2 = T("st2", [128, 64], stack=s2)
                                for c_ in range(NCS):
                                    C.op("vector", "tensor_copy", out=Sb[:, :, c_], in_=cur[:, 0:32], reads=[b_cur], writes=[b_Sb])
                                    C.op("vector", "tensor_tensor", out=st1[:], in0=PP1[:], in1=cur[:, 0:64], op=ALU.mult, reads=[b_PP1, b_cur], writes=[b_st1])
                                    C.op("vector", "tensor_tensor", out=st2[:], in0=PP2[:], in1=cur[:, 32:96], op=ALU.mult, reads=[b_PP2, b_cur], writes=[b_st2])
                                    C.op("vector", "tensor_tensor", out=st1[:], in0=st1[:], in1=st2[:], op=ALU.add, reads=[b_st1, b_st2], writes=[b_st1])
                                    C.op("vector", "tensor_tensor", out=cur[:, 0:64], in0=XX[:, c_, :], in1=st1[:], op=ALU.add, reads=[b_XX, b_st1], writes=[b_cur])
                                    C.op("vector", "tensor_copy", out=cur[:, 64:96], in_=cur[:, 0:32], reads=[b_cur], writes=[b_cur])
                                C.barrier()
                            with contextlib.ExitStack() as s3:
                                z_tm, b_z_tm = T("z_tm", [128, T1, 512], BF16, stack=s3)
                                zT, b_zT = T("zT", [128, 4, SEG], BF16, stack=s3)
                                Mo1 = [T(f"Mo1{i}", [128, K33, 16], stack=s3) for i in range(2)]
                                Mo2 = [T(f"Mo2{i}", [128, K33, 16], stack=s3) for i in range(2)]
                                MoG = [T(f"MoG{i}", [128, K33 * 16], BF16, stack=s3) for i in range(2)]
                                kt = [T(f"kt{i}", [16, 512], BF16, stack=s3) for i in range(2)]
                                Tg = [T(f"Tg{i}", [128, 4, 512], BF16, stack=s3) for i in range(2)]
                                ysb_ = [T(f"ysb{i}", [128, 512], stack=s3) for i in range(2)]
                                ge1 = [T(f"ge1{i}", [128, 512], stack=s3) for i in range(2)]
                                ge2 = [T(f"ge2{i}", [128, 512], stack=s3) for i in range(2)]
                                for g in range(NG):
                                    q_ = g % 2
                                    mo1, b_mo1 = Mo1[q_]
                                    mo2, b_mo2 = Mo2[q_]
                                    mog, b_mog = MoG[q_]
                                    ktt, b_ktt = kt[q_]
                                    tg, b_tg = Tg[q_]
                                    C.op("vector", "tensor_tensor", out=mo1[:], in0=AAr[:, g, :].unsqueeze(2).broadcast_to([128, K33, 16]),
                                         in1=CC1[:, g, :].unsqueeze(1).broadcast_to([128, K33, 16]), op=ALU.mult, reads=[b_AAr, b_CC1], writes=[b_mo1])
                                    C.op("gpsimd", "tensor_tensor", out=mo2[:], in0=AAi[:, g, :].unsqueeze(2).broadcast_to([128, K33, 16]),
                                         in1=CC2[:, g, :].unsqueeze(1).broadcast_to([128, K33, 16]), op=ALU.mult, reads=[b_AAi, b_CC2], writes=[b_mo2])
                                    C.op("vector", "tensor_tensor", out=mog[:], in0=mo1[:].rearrange("p k c -> p (k c)"), in1=mo2[:].rearrange("p k c -> p (k c)"),
                                         op=ALU.add, reads=[b_mo1, b_mo2], writes=[b_mog])
                                    pb = 2 + q_
                                    C.op("tensor", "matmul", PS[pb][0:16, 0:512], lhsT=BBb1b[:, g, :], rhs=mog[:, 16:K33 * 16], start=True, stop=True,
                                         reads=[b_BBb1b, b_mog], writes=[PSB[pb]])
                                    C.op("vector", "tensor_copy", out=ktt[:, 0:496], in_=PS[pb][0:16, 0:496], reads=[PSB[pb]], writes=[b_ktt])
                                    C.op("vector", "scalar_tensor_tensor", out=ktt[:, 496:512], in0=ident[0:16, 0:16], scalar=dsk[:, g:g + 1], in1=PS[pb][0:16, 496:512],
                                         op0=ALU.mult, op1=ALU.add, reads=[b_ident, b_dsk, PSB[pb]], writes=[b_ktt])
                                    C.dma(kr[g][:, 0:512], ktt[:], reads=[b_ktt], writes=[kr_b[g]], sbuf=b_ktt)
                                    for s_ in range(4):
                                        src = bass.AP(kr.tensor, g * 16 * 1008 + 8 * s_ * 16, [[16, 8], [1008, 16], [1, 512]])
                                        C.dma(tg[:, s_, :], src, reads=[kr_b[g]], writes=[b_tg], sbuf=b_tg)
                                    pb2 = 4 + q_
                                    for s_ in range(4):
                                        C.op("tensor", "matmul", PS[pb2][0:NCS, :], lhsT=U[:, g, s_, :], rhs=tg[:, s_, :], start=(s_ == 0), stop=False,
                                             reads=[b_U, b_tg], writes=[PSB[pb2]])
                                    C.op("tensor", "matmul", PS[pb2][0:NCS, :], lhsT=Sb[:, g, :], rhs=mog[:, 0:512], start=False, stop=True,
                                         reads=[b_Sb, b_mog], writes=[PSB[pb2]])
                                    ys, b_ys = ysb_[q_]
                                    g1_, b_g1_ = ge1[q_]
                                    g2_, b_g2_ = ge2[q_]
                                    C.op("scalar", "copy", out=ys[0:NCS], in_=PS[pb2][0:NCS, :], reads=[PSB[pb2]], writes=[b_ys])
                                    C.op("scalar", "activation", out=g1_[0:NCS], in_=PS[pb2][0:NCS, :], func=AF.Square, reads=[PSB[pb2]], writes=[b_g1_])
                                    C.op("vector", "tensor_scalar", out=g1_[0:NCS], in0=g1_[0:NCS], scalar1=0.044715, scalar2=1.0, op0=ALU.mult, op1=ALU.add,
                                         reads=[b_g1_], writes=[b_g1_])
                                    C.op("gpsimd", "tensor_tensor", out=g1_[0:NCS], in0=g1_[0:NCS], in1=ys[0:NCS], op=ALU.mult, reads=[b_g1_, b_ys], writes=[b_g1_])
                                    C.op("scalar", "activation", out=g2_[0:NCS], in_=g1_[0:NCS], func=AF.Sigmoid, scale=1.5957691216, reads=[b_g1_], writes=[b_g2_])
                                    C.op("vector", "tensor_tensor", out=z_tm[0:NCS, :, 16 * g:16 * g + 16], in0=g2_[0:NCS].rearrange("c (t p) -> c t p", p=16),
                                         in1=ys[0:NCS].rearrange("c (t p) -> c t p", p=16), op=ALU.mult, reads=[b_g2_, b_ys], writes=[b_z_tm])
                                for tp in range(T1):
                                    pv_, pbb = (PSb6, PSB[6]) if tp % 2 == 0 else (PSb7, PSB[7])
                                    for k in range(4):
                                        C.op("tensor", "transpose", pv_[:, k * 128:k * 128 + NCS], z_tm[0:NCS, tp, k * 128:(k + 1) * 128], identb[0:NCS, 0:NCS],
                                             reads=[b_z_tm, b_identb], writes=[pbb])
                                    tau = T1 - 1 - tp
                                    C.op("vector" if tp % 2 == 0 else "scalar", "tensor_copy" if tp % 2 == 0 else "copy",
                                         out=zT[:, :, tau:tau + T1 * (NCS - 1) + 1:T1], in_=pv_[:, 0:512].rearrange("p (k c) -> p k c", k=4)[:, :, 0:NCS],
                                         reads=[pbb], writes=[b_zT])
                                sgt = [T(f"sgt{i}", [128, 512], stack=s3) for i in range(2)]
                                obt, b_obt = T("obt", [128, 4, 512], stack=s3)
                                sqb, b_sqb = T("sqb", [128, 4, 512], BF16, stack=s3)
                                lnb, b_lnb = T("lnb", [128, 512], stack=s3)
                                rsb, b_rsb = T("rsb", [128, 512], stack=s3)
                                mxb = [T(f"mxb{i}", [128, 4, 512], BF16, stack=s3) for i in range(2)]
                                for tt in range(SEG // 512):
                                    tsl = slice(tt * 512, (tt + 1) * 512)
                                    for oa in range(4):
                                        for half, pb in ((1, 0), (0, 1)):
                                            oc = oa + 4 * half
                                            for k in range(4):
                                                C.op("tensor", "matmul", PS[pb][:, :], lhsT=gw[:, k, oc * 128:(oc + 1) * 128], rhs=zT[:, k, tsl], start=(k == 0), stop=(k == 3),
                                                     reads=[b_gw, b_zT], writes=[PSB[pb]])
                                        sg_, b_sg_ = sgt[oa % 2]
                                        C.op("scalar", "activation", out=sg_[:], in_=PS[0][:, :], func=AF.Sigmoid, bias=gbc[:, oa + 4:oa + 5], reads=[PSB[0], b_gbc], writes=[b_sg_])
                                        C.op("vector", "scalar_tensor_tensor", out=obt[:, oa, :], in0=PS[1][:, :], scalar=gbc[:, oa:oa + 1], in1=sg_[:], op0=ALU.add, op1=ALU.mult,
                                             reads=[PSB[1], b_gbc, b_sg_], writes=[b_obt])
                                    C.op("scalar", "activation", out=sqb[:], in_=obt[:], func=AF.Square, reads=[b_obt], writes=[b_sqb])
                                    for k in range(4):
                                        C.op("tensor", "matmul", PS[2][:, :], lhsT=onesb[:, :], rhs=sqb[:, k, :], start=(k == 0), stop=(k == 3), reads=[b_onesb, b_sqb], writes=[PSB[2]])
                                    rstd_from(None, rsb[:], PS[2][:, :], 1.0 / 512, lnb[:], [PSB[2]], b_rsb, b_lnb)
                                    mb_, b_mb_ = mxb[tt % 2]
                                    C.op("vector", "tensor_tensor", out=mb_[:], in0=obt[:], in1=rsb[:, :].unsqueeze(1).broadcast_to([128, 4, 512]), op=ALU.mult,
                                         reads=[b_obt, b_rsb], writes=[b_mb_])
                                    gt = (seg * SEG) // 512 + tt
                                    C.dma(mixT.rearrange("(k p) t -> p k t", p=128)[:, 2:6, gt * 512:(gt + 1) * 512], mb_[:], reads=[b_mb_], writes=[mixb_b[gt]], sbuf=b_mb_)
                                C.barrier()
                    C.barrier()
            if want("p3"):
                with contextlib.ExitStack() as ph:
                    kcmpT = sb(ph, "p3_kcmpT", [128, 512], BF16)
                    vco = sb(ph, "p3_vco", [128, 4, 193], BF16)
                    b_kcmpT, b_vco = C.buf("kcmpT"), C.buf("vco")
                    gq3 = sb(ph, "p3_gq", [128, 6], F32)
                    b_gq3 = C.buf("p3gq")
                    for half in range(2):
                        C.dma(gq3[half * 64:(half + 1) * 64, :], qk_g[l].rearrange("s d -> d s"), reads=[inb], writes=[b_gq3],
                              sbuf=b_gq3, allow_slow_non_contiguous=True)
                    with contextlib.ExitStack() as pc1:
                        w1s = sb(pc1, "c_w1s", [128, 32, 128], F32)
                        w1 = sb(pc1, "c_w1", [128, 32, 128], BF16)
                        b_w1s, b_w1 = C.buf("cw1s"), C.buf("cw1")
                        for st in range(2):
                            C.dma(w1s[st * 64:(st + 1) * 64, :, :], cmp_w1_d[l, st].rearrange("(r d) h -> d r h", d=64), reads=[inb],
                                  writes=[b_w1s], sbuf=b_w1s)
                        C.op("vector", "tensor_copy", out=w1[:], in_=w1s[:], reads=[b_w1s], writes=[b_w1])
                        pin = sb(pc1, "c_pin", [32, 128], F32)
                        b_pin = C.buf("cpin")
                        for st in range(2):
                            C.dma(pin[:, st * 64:(st + 1) * 64], cmp_pos_d[l, st], reads=[inb], writes=[b_pin], sbuf=b_pin)
                        C.op("tensor", "transpose", PS[0][:, 0:32], pin[:, :], ident[0:32, 0:32], reads=[b_pin, b_ident], writes=[PSB[0]])
                        posT = sb(pc1, "c_posT", [128, 32], BF16)
                        b_posT = C.buf("cposT")
                        C.op("vector", "tensor_copy", out=posT[:], in_=PS[0][:, 0:32], reads=[PSB[0]], writes=[b_posT])
                        w2s = sb(pc1, "c_w2s", [128, 192], F32)
                        w2b = sb(pc1, "c_w2b", [128, 192], BF16)
                        b_w2s, b_w2b = C.buf("cw2s"), C.buf("cw2b")
                        C.dma(w2s[:, 0:64], cmp_w2_d[l, 0], reads=[inb], writes=[b_w2s], sbuf=b_w2s)
                        C.dma(w2s[:, 64:128], cmp_w2_d[l, 0], reads=[inb], writes=[b_w2s], sbuf=b_w2s)
                        C.dma(w2s[:, 128:192], cmp_w2_d[l, 1], reads=[inb], writes=[b_w2s], sbuf=b_w2s)
                        C.op("vector", "tensor_copy", out=w2b[:], in_=w2s[:], reads=[b_w2s], writes=[b_w2b])
                        kvc = sb(pc1, "c_kvc", [128, L], BF16)
                        b_kvc = C.buf("ckvc")
                        C.dma(kvc[:], fm[5], reads=fm_b, writes=[b_kvc], sbuf=b_kvc)
                        ovs = sb(pc1, "c_ovs", [128, 4, 128], F32)
                        b_ovs = C.buf("covs")
                        C.dma(ovs[:], c_overlap[:, :, :], reads=[inb], writes=[b_ovs], sbuf=b_ovs)
                        C.op("vector", "memset", vco[:], 0.0, writes=[b_vco])
                        C.op("vector", "tensor_copy", out=vco[:, :, 65:193], in_=ovs[:], reads=[b_ovs], writes=[b_vco])
                        C.op("vector", "memset", kcmpT[:], 0.0, writes=[b_kcmpT])
                        pbias = sb(pc1, "c_pbias", [128, 2], F32)
                        b_pbias = C.buf("cpbias")
                        hs = sb(pc1, "c_hs", [128, 512], F32)
                        t1c = sb(pc1, "c_t1", [128, 512], F32)
                        t2c = sb(pc1, "c_t2", [128, 512], F32)
                        hg = sb(pc1, "c_hg", [128, 512], BF16)
                        b_hs, b_t1c, b_t2c, b_hg = C.buf("chs"), C.buf("ct1"), C.buf("ct2"), C.buf("chg")
                        ykc = sb(pc1, "c_ykc", [128, 512], F32)
                        sqkc = sb(pc1, "c_sqkc", [128, 512], BF16)
                        b_ykc, b_sqkc = C.buf("cykc"), C.buf("csqkc")
                        for st in range(2):
                            base = st * 64
                            for r in range(32):
                                C.op("tensor", "matmul", PS[1][:, st:st + 1], lhsT=w1[base:base + 64, r, :], rhs=posT[base:base + 64, r:r + 1],
                                     start=(r == 0), stop=(r == 31), reads=[b_w1, b_posT], writes=[PSB[1]])
                            C.op("vector", "tensor_copy", out=pbias[:, st:st + 1], in_=PS[1][:, st:st + 1], reads=[PSB[1]], writes=[b_pbias])
                            for r in range(32):
                                C.op("tensor", "matmul", PS[2][:, 0:MC], lhsT=w1[base:base + 64, r, :],
                                     rhs=kvc[base:base + 64, r:r + 16 * (MC - 1) + 1:16],
                                     start=(r == 0), stop=(r == 31), reads=[b_w1, b_kvc], writes=[PSB[2]])
                            C.op("vector", "tensor_scalar", out=hs[:, 0:MC], in0=PS[2][:, 0:MC], scalar1=pbias[:, st:st + 1], scalar2=None, op0=ALU.add,
                                 reads=[PSB[2], b_pbias], writes=[b_hs])
                            C.op("scalar", "activation", out=t1c[:, 0:MC], in_=hs[:, 0:MC], func=AF.Square, reads=[b_hs], writes=[b_t1c])
                            C.op("vector", "tensor_scalar", out=t1c[:, 0:MC], in0=t1c[:, 0:MC], scalar1=0.044715, scalar2=1.0, op0=ALU.mult, op1=ALU.add,
                                 reads=[b_t1c], writes=[b_t1c])
                            C.op("vector", "tensor_tensor", out=t1c[:, 0:MC], in0=t1c[:, 0:MC], in1=hs[:, 0:MC], op=ALU.mult, reads=[b_t1c, b_hs], writes=[b_t1c])
                            C.op("scalar", "activation", out=t2c[:, 0:MC], in_=t1c[:, 0:MC], func=AF.Sigmoid, scale=1.5957691216, reads=[b_t1c], writes=[b_t2c])
                            C.op("vector", "memset", hg[:], 0.0, writes=[b_hg])
                            C.op("vector", "tensor_tensor", out=hg[:, 0:MC], in0=t2c[:, 0:MC], in1=hs[:, 0:MC], op=ALU.mult, reads=[b_t2c, b_hs], writes=[b_hg])
                            if st == 0:
                                C.op("tensor", "matmul", PS[3][:, 0:MC], lhsT=w2b[:, 0:128], rhs=hg[:, 0:MC], start=True, stop=True,
                                     reads=[b_w2b, b_hg], writes=[PSB[3]])
                                C.op("vector", "tensor_copy", out=ykc[:, 0:MC], in_=PS[3][:, 0:MC], reads=[PSB[3]], writes=[b_ykc])
                                C.op("scalar", "activation", out=sqkc[:, 0:MC], in_=ykc[:, 0:MC], func=AF.Square, reads=[b_ykc], writes=[b_sqkc])
                                C.op("tensor", "matmul", PS[4][:, 0:MC], lhsT=bd64b[:, :], rhs=sqkc[:, 0:MC], start=True, stop=True,
                                     reads=[b_bd64b, b_sqkc], writes=[PSB[4]])
                                rstd_from(None, t2c[:, 0:MC], PS[4][:, 0:MC], 1.0 / HD, t1c[:, 0:MC], [PSB[4]], b_t2c, b_t1c)
                                C.op("vector", "scalar_tensor_tensor", out=kcmpT[:, 0:MC], in0=ykc[:, 0:MC], scalar=gq3[:, 3:4], in1=t2c[:, 0:MC],
                                     op0=ALU.mult, op1=ALU.mult, reads=[b_ykc, b_gq3, b_t2c], writes=[b_kcmpT])
                            else:
                                for cch in range(NCH):
                                    mr = min(128, MC - cch * 128)
                                    C.op("tensor", "matmul", PS[3][0:mr, 0:64], lhsT=hg[:, cch * 128:cch * 128 + mr], rhs=w2b[:, 128:192], start=True, stop=True,
                                         reads=[b_w2b, b_hg], writes=[PSB[3]])
                                    C.op("vector", "tensor_copy", out=vco[0:mr, cch, 0:64], in_=PS[3][0:mr, 0:64], reads=[PSB[3]], writes=[b_vco])
                                    C.op("vector", "memset", vco[0:mr, cch, 64:65], 1.0, writes=[b_vco])
                        C.barrier()
                    kaT = sb(ph, "p3_kaT", [128, L], BF16)
                    ksT = sb(ph, "p3_ksT", [128, L], BF16)
                    kwT = sb(ph, "p3_kwT", [128, L], BF16)
                    b_kaT, b_ksT, b_kwT = C.buf("kaT"), C.buf("ksT"), C.buf("kwT")
                    C.dma(kaT[:], fm[2], reads=fm_b, writes=[b_kaT], sbuf=b_kaT)
                    C.dma(ksT[:], fm[6], reads=fm_b, writes=[b_ksT], sbuf=b_ksT)
                    C.dma(kwT[:], fm[7], reads=fm_b, writes=[b_kwT], sbuf=b_kwT)
                    vall = sb(ph, "p3_vall", [128, NB, 260], BF16)
                    gall = sb(ph, "p3_gall", [128, NB, 12], F32)
                    b_vall, b_gall = C.buf("vall"), C.buf("gall")
                    for q0 in range(0, NB, 16):
                        q1 = min(NB, q0 + 16)
                        C.dma(vall[:, q0:q1, :], vtm.rearrange("(b p) c -> p b c", p=128)[:, q0:q1, :], reads=vtm_b, writes=[b_vall], sbuf=b_vall)
                        C.dma(gall[:, q0:q1, :], gtm.rearrange("(b p) c -> p b c", p=128)[:, q0:q1, :], reads=gtm_b, writes=[b_gall], sbuf=b_gall)
                    wexpb = sb(ph, "p3_wexp", [128, L], BF16)
                    b_wexpb = C.buf("wexpb")
                    tst = [sb(ph, f"p3_tst{i}", [128, 1024], F32) for i in range(2)]
                    b_tst = [C.buf(f"p3tst{i}") for i in range(2)]
                    cnt = 0

                    def stage_cast(dst_ap, src_ap, n, bdst, post=None):
                        nonlocal cnt
                        s_ = cnt % 2
                        cnt += 1
                        C.dma(tst[s_][:, 0:n], src_ap, reads=[inb], writes=[b_tst[s_]], sbuf=b_tst[s_])
                        if post is None:
                            C.op("vector", "tensor_copy", out=dst_ap, in_=tst[s_][:, 0:n], reads=[b_tst[s_]], writes=[bdst])
                        else:
                            post(tst[s_], b_tst[s_])
                    for c0 in range(0, L, 1024):
                        stage_cast(wexpb[:, c0:c0 + 1024], c_wexp[:, c0:c0 + 1024], 1024, b_wexpb)
                    b31 = sb(ph, "p3_b31", [128, 4], F32)
                    b_b31 = C.buf("b31")
                    C.dma(b31[:], t_b31[:, :], reads=[inb], writes=[b_b31], sbuf=b_b31)
                    bA = sb(ph, "p3_bA", [128, 2, 512], BF16)
                    bNp = sb(ph, "p3_bNp", [128, 8, 512], BF16)
                    bNs = sb(ph, "p3_bNs", [128, 8, 512], BF16)
                    bW4 = sb(ph, "p3_bW4", [128, 512], BF16)
                    bCt = sb(ph, "p3_bC", [128, 24, 512], BF16)
                    b_bA, b_bNp, b_bNs, b_bW4, b_bCt = C.buf("bA"), C.buf("bNp"), C.buf("bNs"), C.buf("bW4"), C.buf("bCt")
                    for dl in range(2):
                        stage_cast(bA[:, dl, :], t_biasA[:, dl].rearrange("p h q -> p (h q)"), 512, b_bA)
                    for dl in range(8):
                        def post(tt, btt, dl=dl):
                            C.op("vector", "tensor_copy", out=bNp[:, dl, :], in_=tt[:, 0:512], reads=[btt], writes=[b_bNp])
                            C.op("vector", "tensor_tensor", out=bNs[:, dl, :].rearrange("p (h q) -> p h q", h=4),
                                 in0=tt[:, 0:512].rearrange("p (h q) -> p h q", h=4),
                                 in1=b31[:, :].unsqueeze(2).broadcast_to([128, 4, 128]), op=ALU.subtract,
                                 reads=[btt, b_b31], writes=[b_bNs])
                        stage_cast(None, t_biasN[:, dl].rearrange("p h q -> p (h q)"), 512, None, post=post)
                    stage_cast(bW4[:, :], t_biasW4.rearrange("p h q -> p (h q)"), 512, b_bW4)
                    for dl in range(24):
                        stage_cast(bCt[:, dl, :], t_biasC[:, dl].rearrange("p h q -> p (h q)"), 512, b_bCt)
                    m1b = sb(ph, "p3_m1b", [128, 256], F32)
                    m2b = sb(ph, "p3_m2b", [128, 256], F32)
                    b_m1b, b_m2b = C.buf("m1b"), C.buf("m2b")
                    C.dma(m1b[:], c_m1[:, :], reads=[inb], writes=[b_m1b], sbuf=b_m1b)
                    C.dma(m2b[:], c_m2[:, :], reads=[inb], writes=[b_m2b], sbuf=b_m2b)
                    es4 = sb(ph, "p3_es4", [128, 4], F32)
                    b_es4 = C.buf("es4")
                    C.dma(es4[:], sinks[l].partition_broadcast(128), reads=[inb], writes=[b_es4], sbuf=b_es4)
                    C.op("scalar", "activation", out=es4[:], in_=es4[:], func=AF.Exp, reads=[b_es4], writes=[b_es4])
                    qt = [sb(ph, f"p3_qt{i}", [128, 4, 8, 128], BF16) for i in range(2)]
                    b_qt = [C.buf(f"p3qt{i}") for i in range(2)]
                    for i in range(2):
                        C.op("vector", "memset", qt[i][:], 0.0, writes=[b_qt[i]])
                    pt = [sb(ph, f"p3_pt{i}", [128, 512], BF16) for i in range(3)]
                    b_pt = [C.buf(f"p3pt{i}") for i in range(3)]
                    NSB = 3
                    sctr = [0]
                    imp = sb(ph, "p3_imp", [128, 128], F32)
                    score = sb(ph, "p3_score", [128, 128], F32)
                    sc2 = sb(ph, "p3_sc2", [128, 128], F32)
                    m8 = sb(ph, "p3_m8", [128, 16], F32)
                    negm = sb(ph, "p3_negm", [128, 128], BF16)
                    negmT4 = sb(ph, "p3_negmT4", [128, 4, 128], BF16)
                    b_imp, b_score, b_sc2, b_m8, b_negm, b_negmT4 = (C.buf(n) for n in ["imp", "score", "sc2", "m8", "negm", "negmT4"])
                    dens = sb(ph, "p3_dens", [128, 16], F32)
                    rden = sb(ph, "p3_rden", [128, 16], F32)
                    coef = sb(ph, "p3_coef", [128, 12], F32)
                    b_dens, b_rden, b_coef = C.buf("dens"), C.buf("rden"), C.buf("coef")
                    o_a = sb(ph, "p3_oa", [128, 4, 64], F32)
                    o_c = sb(ph, "p3_oc", [128, 4, 64], F32)
                    otmp = sb(ph, "p3_otmp", [128, 4, 64], F32)
                    b_oa, b_oc, b_otmp = C.buf("oa"), C.buf("oc"), C.buf("otmp")
                    junk = sb(ph, "p3_junk", [128, 256], F32)
                    ssn = sb(ph, "p3_ssn", [128, 4], F32)
                    onb = sb(ph, "p3_onb", [128, 2, 256], BF16)
                    b_junk, b_ssn, b_onb = C.buf("junk"), C.buf("ssn"), C.buf("onb")
                    otx = [sb(ph, f"p3_otx{i}", [128, 512], F32) for i in range(3)]
                    b_otx = [C.buf(f"p3otx{i}") for i in range(3)]
                    mst = [sb(ph, f"p3_mst{i}", [128, 4, 512], BF16) for i in range(2)]
                    b_mst = [C.buf(f"p3mst{i}") for i in range(2)]
                    PSTb = PS[7][:, :].bitcast(BF16)

                    pending = []

                    def score_tile(ncols, bias_rhs, bias_bufs, mask, qk, pv):
                        pb = sctr[0] % 2
                        j = sctr[0] % NSB
                        sctr[0] += 1
                        first = True
                        if bias_rhs is not None:
                            C.op("tensor", "matmul", PS[pb][:, 0:ncols], lhsT=identb[:, :], rhs=bias_rhs, start=True, stop=False,
                                 reads=[b_identb] + bias_bufs, writes=[PSB[pb]])
                            first = False
                        if mask is not None:
                            C.op("tensor", "matmul", PS[pb][:, 0:ncols], lhsT=mask[0], rhs=mask[1], start=first, stop=False,
                                 reads=[b_wexpb, b_negmT4], writes=[PSB[pb]])
                            first = False
                        assert not first
                        for qi, (lhsT, rhs, c0, n, rb) in enumerate(qk):
                            C.op("tensor", "matmul", PS[pb][:, c0:c0 + n], lhsT=lhsT, rhs=rhs, start=False, stop=(qi == len(qk) - 1),
                                 reads=rb, writes=[PSB[pb]])
                        flush_pv()
                        C.op("scalar", "activation", out=pt[j][:, 0:ncols], in_=PS[pb][:, 0:ncols], func=AF.Exp, reads=[PSB[pb]], writes=[b_pt[j]])
                        pending.append((j, pv))

                    def flush_pv():
                        while pending:
                            j, pv = pending.pop(0)
                            issue_pv(j, pv)

                    def issue_pv(j, pv):
                        for (out_ap, c0, rhs, st_, sp_, ob, rb) in pv:
                            if c0 is None:
                                vl, cc0, ncl = rhs
                                C.op("tensor", "matmul", out_ap, lhsT=vl, rhs=pt[j][:, cc0:cc0 + ncl], start=st_, stop=sp_,
                                     reads=[b_pt[j]] + rb, writes=[ob])
                            else:
                                C.op("tensor", "matmul", out_ap, lhsT=pt[j][:, c0:c0 + 128], rhs=rhs, start=st_, stop=sp_,
                                     reads=[b_pt[j]] + rb, writes=[ob])

                    def load_q(ti):
                        s_ = ti % 2
                        tsl = slice(ti * 512, (ti + 1) * 512)
                        for kvh in range(2):
                            for e in range(2):
                                C.dma(qt[s_][kvh * 64:(kvh + 1) * 64, :, 2 * kvh + e, :], fm[e, kvh * 64:(kvh + 1) * 64, tsl].rearrange("p (b q) -> p b q", b=4),
                                      reads=[fm_b[ti]], writes=[b_qt[s_]], sbuf=b_qt[s_])
                        for h in range(4):
                            r0 = (h % 2) * 64
                            C.dma(qt[s_][r0:r0 + 64, :, 4 + h, :], fm[3 + h // 2, r0:r0 + 64, tsl].rearrange("p (b q) -> p b q", b=4),
                                  reads=[fm_b[ti]], writes=[b_qt[s_]], sbuf=b_qt[s_])
                    load_q(0)
                    for bi in range(NB if want("p3loop") else 0):
                        ti = bi // 4
                        s_ = ti % 2
                        qo = (bi % 4) * 128
                        if bi % 4 == 0 and ti + 1 < NT:
                            load_q(ti + 1)
                        qT = qt[s_]
                        bq = b_qt[s_]

                        qb_ = bi % 4
                        qc_all = qT[:, qb_, 4:8, :].rearrange("p h q -> p (h q)")
                        qa_all = qT[:, qb_, 0:4, :].rearrange("p h q -> p (h q)")
                        nck = min(NCH, (8 * bi + 6) // 128 + 1)
                        for cch in range(nck if want("cmp") else 0):
                            dli = min(bi - 16 * cch, 23)
                            qk = [(kcmpT[:, cch * 128:(cch + 1) * 128], qc_all, 0, 512, [b_kcmpT, bq])]
                            pv = [(PS[2 + h // 2][:, (h % 2) * 193:(h % 2) * 193 + 193], h * 128, vco[:, cch, :], cch == 0 and h % 2 == 0, cch == nck - 1 and h % 2 == 1, PSB[2 + h // 2], [b_vco])
                                  for h in range(4)]
                            score_tile(512, bCt[:, dli, :], [b_bCt], None, qk, pv)
                        k0 = max(0, bi - 4)
                        for kc in range(k0, bi + 1 if want("win") else 0):
                            dl = bi - kc
                            qk = [(kwT[:, kc * 128:(kc + 1) * 128], qc_all, 0, 512, [b_kwT, bq])]
                            pv = [(PS[5][0:65, 0:512], None, (vall[:, kc, 195:260], 0, 512), kc == k0, kc == bi, PSB[5], [b_vall])]
                            score_tile(512, bW4[:, :] if dl == 4 else bNp[:, dl, :], [b_bW4, b_bNp], None, qk, pv)
                        k0 = max(0, bi - 1)
                        for kc in range(k0, bi + 1 if want("swa") else 0):
                            dl = bi - kc
                            qk = [(kaT[:, kc * 128:(kc + 1) * 128], qa_all, 0, 512, [b_kaT, bq])]
                            pv = [(PS[6][0:65, kvh * 256:(kvh + 1) * 256], None, (vall[:, kc, kvh * 65:(kvh + 1) * 65], kvh * 256, 256),
                                   kc == k0 and kvh == 0, kc == bi and kvh == 1, PSB[6], [b_vall]) for kvh in range(2)]
                            score_tile(512, bA[:, dl, :], [b_bA], None, qk, pv)
                        flush_pv()
                        if want("topk"):
                            for bk in range(2):
                                C.op("vector", "tensor_scalar", out=dens[:, 2 * bk:2 * bk + 2], in0=PS[2 + bk][:, 64:64 + 194:193], scalar1=1e-30, scalar2=None,
                                     op0=ALU.max, reads=[PSB[2 + bk]], writes=[b_dens])
                            C.op("vector", "reciprocal", out=rden[:, 0:4], in_=dens[:, 0:4], reads=[b_dens], writes=[b_rden])
                            for h in range(4):
                                src = PS[2 + h // 2][:, (h % 2) * 193 + 65:(h % 2) * 193 + 193]
                                if h == 0:
                                    C.op("vector", "tensor_scalar", out=imp[:], in0=src, scalar1=rden[:, 0:1], scalar2=None, op0=ALU.mult,
                                         reads=[PSB[2], b_rden], writes=[b_imp])
                                else:
                                    C.op("vector", "scalar_tensor_tensor", out=imp[:], in0=src, scalar=rden[:, h:h + 1], in1=imp[:], op0=ALU.mult, op1=ALU.add,
                                         reads=[PSB[2 + h // 2], b_rden, b_imp], writes=[b_imp])
                            w0 = 126 - 2 * bi
                            C.op("vector", "tensor_tensor", out=score[:], in0=imp[:], in1=m1b[:, w0:w0 + 128], op=ALU.mult, reads=[b_imp, b_m1b], writes=[b_score])
                            C.op("vector", "tensor_tensor", out=score[:], in0=score[:], in1=m2b[:, w0:w0 + 128], op=ALU.add, reads=[b_score, b_m2b], writes=[b_score])
                            C.op("vector", "memset", score[:, 0:1], 1e4, writes=[b_score])
                            C.op("vector", "max", out=m8[:, 0:8], in_=score[:], reads=[b_score], writes=[b_m8])
                            C.op("vector", "match_replace", out=sc2[:], in_to_replace=m8[:, 0:8], in_values=score[:], imm_value=-3e4,
                                 reads=[b_score, b_m8], writes=[b_sc2])
                            C.op("vector", "max", out=m8[:, 8:16], in_=sc2[:], reads=[b_sc2], writes=[b_m8])
                            C.op("vector", "tensor_scalar", out=negm[:], in0=score[:], scalar1=m8[:, 15:16], scalar2=NEG, op0=ALU.is_lt, op1=ALU.mult,
                                 reads=[b_score, b_m8], writes=[b_negm])
                            C.op("tensor", "transpose", PSTb[:, 0:128], negm[:, :], identb[:, :], reads=[b_negm, b_identb], writes=[PSB[7]])
                            for h in range(4):
                                C.op("vector" if h % 2 == 0 else "gpsimd" if False else "vector", "tensor_scalar", out=negmT4[:, h, :], in0=PSTb[:, 0:128],
                                     scalar1=b31[:, h:h + 1], scalar2=None, op0=ALU.add, reads=[PSB[7], b_b31], writes=[b_negmT4])
                        for kc in range(bi + 1 if want("sel") else 0):
                            dl = bi - kc
                            near = dl < 8
                            qk = [(ksT[:, kc * 128:(kc + 1) * 128], qc_all, 0, 512, [b_ksT, bq])]
                            pv = [(PS[4][0:65, 0:512], None, (vall[:, kc, 130:195], 0, 512), kc == 0, kc == bi, PSB[4], [b_vall])]
                            score_tile(512, bNs[:, dl, :] if near else None, [b_bNs], (wexpb[:, kc * 128:(kc + 1) * 128], negmT4[:].rearrange("p h q -> p (h q)")), qk, pv)
                        flush_pv()
                        if want("epi"):
                            for xi, bnk in enumerate((4, 5, 6)):
                                C.op("scalar" if xi != 1 else "vector", "copy" if xi != 1 else "tensor_copy", out=otx[xi][0:65, :], in_=PS[bnk][0:65, 0:512],
                                     reads=[PSB[bnk]], writes=[b_otx[xi]])
                                for h in range(4):
                                    C.op("tensor", "transpose", PS[bnk][:, h * 65:(h + 1) * 65], otx[xi][0:65, h * 128:(h + 1) * 128], ident[0:65, 0:65],
                                         reads=[b_otx[xi], b_ident], writes=[PSB[bnk]])
                            C.op("vector", "tensor_copy", out=dens[:, 4:8], in_=PS[4][:, 64:64 + 4 * 65:65], reads=[PSB[4]], writes=[b_dens])
                            C.op("vector", "tensor_copy", out=dens[:, 8:12], in_=PS[5][:, 64:64 + 4 * 65:65], reads=[PSB[5]], writes=[b_dens])
                            C.op("vector", "tensor_tensor", out=dens[:, 12:16], in0=PS[6][:, 64:64 + 4 * 65:65], in1=es4[:], op=ALU.add, reads=[PSB[6], b_es4], writes=[b_dens])
                            C.op("vector", "reciprocal", out=rden[:, 4:16], in_=dens[:, 4:16], reads=[b_dens], writes=[b_rden])
                            C.op("vector", "tensor_tensor", out=coef[:].rearrange("p (b h) -> p b h", b=3), in0=rden[:, 0:12].rearrange("p (b h) -> p b h", b=3),
                                 in1=gall[:, bi, :].rearrange("p (h b) -> p b h", b=3), op=ALU.mult, reads=[b_rden, b_gall], writes=[b_coef])
                            for bk in range(2):
                                C.op("vector", "tensor_tensor", out=o_c[:, 2 * bk:2 * bk + 2, :],
                                     in0=PS[2 + bk][:, 0:386].rearrange("p (h c) -> p h c", c=193)[:, :, 0:64],
                                     in1=coef[:, 2 * bk:2 * bk + 2].unsqueeze(2).broadcast_to([128, 2, 64]), op=ALU.mult,
                                     reads=[PSB[2 + bk], b_coef], writes=[b_oc])
                            for (bnk, c0) in [(4, 4), (5, 8)]:
                                C.op("vector", "tensor_tensor", out=otmp[:], in0=PS[bnk][:, 0:260].rearrange("p (h c) -> p h c", c=65)[:, :, 0:64],
                                     in1=coef[:, c0:c0 + 4].unsqueeze(2).broadcast_to([128, 4, 64]), op=ALU.mult, reads=[PSB[bnk], b_coef], writes=[b_otmp])
                                C.op("gpsimd", "tensor_tensor", out=o_c[:], in0=o_c[:], in1=otmp[:], op=ALU.add, reads=[b_oc, b_otmp], writes=[b_oc])
                            C.op("vector", "tensor_tensor", out=o_a[:], in0=PS[6][:, 0:260].rearrange("p (h c) -> p h c", c=65)[:, :, 0:64],
                                 in1=rden[:, 12:16].unsqueeze(2).broadcast_to([128, 4, 64]), op=ALU.mult, reads=[PSB[6], b_rden], writes=[b_oa])
                            for gi, (ot, bo) in enumerate([(o_a, b_oa), (o_c, b_oc)]):
                                C.op("scalar", "activation", out=junk[:], in_=ot[:].rearrange("p h d -> p (h d)"), func=AF.Square, accum_out=ssn[:, gi:gi + 1],
                                     reads=[bo], writes=[b_junk, b_ssn])
                            C.op("scalar", "activation", out=ssn[:, 2:4], in_=ssn[:, 0:2], func=AF.Ln, bias=EPS, scale=1.0 / 256, reads=[b_ssn], writes=[b_ssn])
                            C.op("scalar", "activation", out=ssn[:, 2:4], in_=ssn[:, 2:4], func=AF.Exp, scale=-0.5, reads=[b_ssn], writes=[b_ssn])
                            for gi, (ot, bo) in enumerate([(o_a, b_oa), (o_c, b_oc)]):
                                C.op("vector", "tensor_scalar", out=onb[:, gi, :], in0=ot[:].rearrange("p h d -> p (h d)"), scalar1=ssn[:, 2 + gi:3 + gi], scalar2=None,
                                     op0=ALU.mult, reads=[bo, b_ssn], writes=[b_onb])
                            for gi in range(2):
                                for cc in range(2):
                                    C.op("tensor", "transpose", PSTb[:, 128 + (gi * 2 + cc) * 128:128 + (gi * 2 + cc + 1) * 128], onb[:, gi, cc * 128:(cc + 1) * 128],
                                         identb[:, :], reads=[b_onb, b_identb], writes=[PSB[7]])
                            ms = mst[ti % 2]
                            bms = b_mst[ti % 2]
                            C.op("scalar", "activation", out=ms[:, :, qo:qo + 128], in_=PSTb[:, 128:640].rearrange("p (c q) -> p c q", c=4), func=AF.Copy, reads=[PSB[7]], writes=[bms])
                            if bi % 4 == 3:
                                tsl = slice(ti * 512, (ti + 1) * 512)
                                mv = mixT.rearrange("(k p) t -> p k t", p=128)
                                C.dma(mv[:, 0:2, tsl], ms[:, 0:2, :], reads=[bms], writes=[mixa_b[ti]], sbuf=bms)
                                C.dma(mv[:, 6:8, tsl], ms[:, 2:4, :], reads=[bms], writes=[mixa_b[ti]], sbuf=bms)
                    C.barrier()
            if want("p4"):
                with contextlib.ExitStack() as ph:
                    wo = sb(ph, "p4_wo", [128, 8, D], BF16)
                    b_wo = C.buf("p4wo")
                    stg = [sb(ph, f"p4_stg{i}", [128, 1, D], F32) for i in range(2)]
                    b_stg = [C.buf(f"p4stg{i}") for i in range(2)]
                    gout, b_gout = load_cols(ph, "p4_gout", out_norm_g[l], 8)
                    wov = w_out_d[l].rearrange("(k p) c -> p k c", p=128)
                    for kk2 in range(8):
                        s_ = kk2 % 2
                        C.dma(stg[s_][:], wov[:, kk2:kk2 + 1, :], reads=[inb], writes=[b_stg[s_]], sbuf=b_stg[s_])
                        C.op("vector" if s_ == 0 else "gpsimd", "tensor_tensor", out=wo[:, kk2:kk2 + 1, :], in0=stg[s_][:],
                             in1=gout[:, kk2:kk2 + 1].unsqueeze(2).broadcast_to([128, 1, D]), op=ALU.mult, reads=[b_stg[s_], b_gout], writes=[b_wo])
                    xa = [sb(ph, f"p4_xa{i}", [128, 8, 512], F32) for i in range(2)]
                    b_xa = [C.buf(f"p4xa{i}") for i in range(2)]
                    mx = [sb(ph, f"p4_mx{i}", [128, 8, 512], BF16) for i in range(2)]
                    b_mx = [C.buf(f"p4mx{i}") for i in range(2)]

                    def p4a_load(i):
                        s_ = i % 2
                        tsl = slice(i * 512, (i + 1) * 512)
                        C.dma(xa[s_][:], xT.rearrange("(k p) t -> p k t", p=128)[:, :, tsl], reads=[xT_b[i]], writes=[b_xa[s_]], sbuf=b_xa[s_])
                        C.dma(mx[s_][:], mixT.rearrange("(k p) t -> p k t", p=128)[:, :, tsl], reads=[mixa_b[i], mixb_b[i]], writes=[b_mx[s_]], sbuf=b_mx[s_])
                    p4a_load(0)
                    pc = 0
                    for i in range(NT):
                        s_ = i % 2
                        if i + 1 < NT:
                            p4a_load(i + 1)
                        for oc in range(8):
                            pb = pc % 4
                            pc += 1
                            for k in range(8):
                                C.op("tensor", "matmul", PS[pb][:, :], lhsT=wo[:, k, oc * 128:(oc + 1) * 128], rhs=mx[s_][:, k, :],
                                     start=(k == 0), stop=(k == 7), reads=[b_wo, b_mx[s_]], writes=[PSB[pb]])
                            C.op("vector", "tensor_tensor", out=xa[s_][:, oc, :], in0=PS[pb][:, :], in1=xa[s_][:, oc, :], op=ALU.add,
                                 reads=[PSB[pb], b_xa[s_]], writes=[b_xa[s_]])
                        C.dma(xT.rearrange("(k p) t -> p k t", p=128)[:, :, i * 512:(i + 1) * 512], xa[s_][:], reads=[b_xa[s_]], writes=[xT_b[i]], sbuf=b_xa[s_])
                    C.barrier()
            if want("p4"):
                with contextlib.ExitStack() as ph:
                    TT = 512
                    wu = sb(ph, "p4_wu", [128, 8, 4 * D], BF16)
                    wd = sb(ph, "p4_wd", [128, 32, D], BF16)
                    b_wu, b_wd = C.buf("p4wu"), C.buf("p4wd")
                    stg = [sb(ph, f"p4b_stg{i}", [128, 1, D], F32) for i in range(2)]
                    b_stg = [C.buf(f"p4bstg{i}") for i in range(2)]
                    g2, b_g2 = load_cols(ph, "p4_g2", norm2_g[l], 8)
                    pieces = []
                    wuv = w_up_d[l].rearrange("(k p) c -> p k c", p=128)
                    for kk2 in range(8):
                        for cq in range(4):
                            pieces.append((wuv[:, kk2:kk2 + 1, cq * D:(cq + 1) * D], wu[:, kk2:kk2 + 1, cq * D:(cq + 1) * D],
                                           g2[:, kk2:kk2 + 1], b_g2, b_wu))
                    wdv = w_down_d[l].rearrange("(k p) c -> p k c", p=128)
                    for kk2 in range(32):
                        pieces.append((wdv[:, kk2:kk2 + 1, :], wd[:, kk2:kk2 + 1, :], None, None, b_wd))
                    for pi, (src, dst, gsc, bg, bdst) in enumerate(pieces):
                        s_ = pi % 2
                        C.dma(stg[s_][:], src, reads=[inb], writes=[b_stg[s_]], sbuf=b_stg[s_])
                        eng = "vector" if pi % 2 == 0 else "gpsimd"
                        if gsc is None:
                            C.op(eng, "tensor_copy", out=dst, in_=stg[s_][:], reads=[b_stg[s_]], writes=[bdst])
                        else:
                            C.op(eng, "tensor_tensor", out=dst, in0=stg[s_][:], in1=gsc.unsqueeze(2).broadcast_to([128, 1, D]), op=ALU.mult,
                                 reads=[b_stg[s_], bg], writes=[bdst])
                    xt4 = sb(ph, "p4_xt", [128, 8, TT], F32)
                    b_xt4 = C.buf("p4xt")
                    h2 = sb(ph, "p4_h2", [128, 8, TT], BF16)
                    sq4 = sb(ph, "p4_sq", [128, 8, TT], BF16)
                    b_h2, b_sq4 = C.buf("p4h2"), C.buf("p4sq")
                    hid = sb(ph, "p4_hid", [128, 16, TT], BF16)
                    b_hid = C.buf("p4hid")
                    lnv4 = sb(ph, "p4_lnv", [128, TT], F32)
                    rs4 = sb(ph, "p4_rs", [128, TT], F32)
                    b_lnv4, b_rs4 = C.buf("p4lnv"), C.buf("p4rs")
                    tmp4 = [sb(ph, f"p4_tmp{i}", [128, TT], F32) for i in range(2)]
                    b_tmp4 = [C.buf(f"p4tmp{i}") for i in range(2)]
                    yst = sb(ph, "p4_yst0", [128, D], F32)
                    b_yst = C.buf("p4yst0")
                    pc = 0
                    for i in range(NT):
                        tsl = slice(i * TT, (i + 1) * TT)
                        C.dma(xt4[:], xT.rearrange("(k p) t -> p k t", p=128)[:, :, tsl], reads=[xT_b[i]], writes=[b_xt4], sbuf=b_xt4)
                        C.op("scalar", "activation", out=sq4[:], in_=xt4[:], func=AF.Square, reads=[b_xt4], writes=[b_sq4])
                        C.op("gpsimd", "tensor_copy", out=h2[:], in_=xt4[:], reads=[b_xt4], writes=[b_h2])
                        for k in range(8):
                            C.op("tensor", "matmul", PS[4][:, :], lhsT=onesb[:, :], rhs=sq4[:, k, :], start=(k == 0), stop=(k == 7),
                                 reads=[b_onesb, b_sq4], writes=[PSB[4]])
                        rstd_from(None, rs4[:], PS[4][:, :], 1.0 / D, lnv4[:], [PSB[4]], b_rs4, b_lnv4)
                        for hf in range(2):
                            for fcl in range(16):
                                fc = hf * 16 + fcl
                                pb = pc % 4
                                pc += 1
                                q = fc % 2
                                for k in range(8):
                                    C.op("tensor", "matmul", PS[pb][:, :], lhsT=wu[:, k, fc * 128:(fc + 1) * 128], rhs=h2[:, k, :],
                                         start=(k == 0), stop=(k == 7), reads=[b_wu, b_h2], writes=[PSB[pb]])
                                C.op("vector", "scalar_tensor_tensor", out=tmp4[q][:], in0=PS[pb][:, :], scalar=0.0, in1=rs4[:], op0=ALU.max, op1=ALU.mult,
                                     reads=[PSB[pb], b_rs4], writes=[b_tmp4[q]])
                                C.op("scalar", "activation", out=hid[:, fcl, :], in_=tmp4[q][:], func=AF.Square, reads=[b_tmp4[q]], writes=[b_hid])
                            for oc in range(8):
                                pb = pc % 4
                                pc += 1
                                for k in range(16):
                                    C.op("tensor", "matmul", PS[pb][:, :], lhsT=wd[:, hf * 16 + k, oc * 128:(oc + 1) * 128], rhs=hid[:, k, :],
                                         start=(k == 0), stop=(k == 15), reads=[b_wd, b_hid], writes=[PSB[pb]])
                                C.op("vector", "tensor_tensor", out=xt4[:, oc, :], in0=PS[pb][:, :], in1=xt4[:, oc, :], op=ALU.add,
                                     reads=[PSB[pb], b_xt4], writes=[b_xt4])
                        if not last:
                            C.dma(xT.rearrange("(k p) t -> p k t", p=128)[:, :, tsl], xt4[:], reads=[b_xt4], writes=[xT_b[i]], sbuf=b_xt4)
                        else:
                            for sbk in range(TT // 128):
                                for half in range(2):
                                    pb = 5 + half
                                    for k in range(4):
                                        C.op("tensor", "transpose", PS[pb][:, k * 128:(k + 1) * 128],
                                             xt4[:, half * 4 + k, sbk * 128:(sbk + 1) * 128], ident[:, :],
                                             reads=[b_xt4, b_ident], writes=[PSB[pb]])
                                    C.op("vector" if half == 0 else "scalar", "tensor_copy" if half == 0 else "copy",
                                         out=yst[:, half * 512:(half + 1) * 512], in_=PS[pb][:, :], reads=[PSB[pb]], writes=[b_yst])
                                t0 = i * TT + sbk * 128
                                C.dma(y_out[t0:t0 + 128, :], yst[:], reads=[b_yst], writes=[y_b], sbuf=b_yst)
                    C.barrier()
        C.barrier()
    return nc


_CACHE = {}
LAYERS_PER_LAUNCH = 2


def kernel(**inputs):
    x = np.ascontiguousarray(np.asarray(inputs["x"], dtype=np.float32))
    B, L, _ = x.shape
    consts = host_consts(L)
    tabs = host_tables(inputs["rel_bias"], L)
    shared = {}
    for k, v in inputs.items():
        if k in ("x", "rel_bias"):
            continue
        shared[k] = np.ascontiguousarray(np.asarray(v, dtype=np.float32))
    for k, v in consts.items():
        shared["c_" + k] = v
    for k, v in tabs.items():
        shared["t_" + k] = v
    cur = [x[b] for b in range(B)]
    for l0 in range(0, DEPTH, LAYERS_PER_LAUNCH):
        key = (L, l0, LAYERS_PER_LAUNCH)
        if key not in _CACHE:
            _CACHE[key] = build(L, nlayers=LAYERS_PER_LAUNCH, layer0=l0)
        nc = _CACHE[key]
        in_maps = []
        for b in range(B):
            m = dict(shared)
            m["x"] = np.ascontiguousarray(cur[b])
            in_maps.append(m)
        res = run_bass_kernel_spmd(nc, in_maps, core_ids=list(range(B)))
        cur = [np.asarray(r["y"], dtype=np.float32) for r in res.results]
    return np.stack(cur, axis=0)
```
